# Optimizing a Trainium2 kernel written in Bass

```python
import jax
import jax.numpy as jnp
from jax import lax

D_MODEL = 1024
BATCH = 32
SEQ = 256
DEPTH = 4
DEC_BATCH = 8
DEC_SEQ = 4096
PAST_LEN = 512

GRID_W = 64
HEAD_DIM = 64
A_HEADS = 8
A_KV_HEADS = 2
A_GROUPS = A_HEADS // A_KV_HEADS
A_WIDTH = A_HEADS * HEAD_DIM
B_HEADS = 8
B_NOPE = 64
B_ROPE = 32
B_VDIM = 64
B_KV_RANK = 128
B_WIDTH = B_HEADS * B_VDIM
C_HEADS = 8
C_WIDTH = C_HEADS * HEAD_DIM
C_DECAY_LORA = 64
C_AAA_LORA = 64
C_SHIFT_DIM = 3 * C_WIDTH + 2 * C_DECAY_LORA + 2 * C_AAA_LORA
N_BRANCHES = 3
IN_SPLITS = (A_WIDTH, A_KV_HEADS * HEAD_DIM, A_KV_HEADS * HEAD_DIM, A_WIDTH,
             B_HEADS * (B_NOPE + B_ROPE), B_KV_RANK, B_ROPE, B_WIDTH,
             C_SHIFT_DIM, C_WIDTH, N_BRANCHES * D_MODEL)
IN_DIM = sum(IN_SPLITS)
Q_BLOCK = 128
ROPE_THETA = 10000.0
NORM_EPS = 1e-6
C_GN_EPS = 64e-5

kernel_name = 'bidir_hybrid_flow_trunk_step'


def _rmsnorm(x, w):
    xf = x.astype(jnp.float32)
    y = xf * lax.rsqrt(jnp.mean(xf * xf, axis=-1, keepdims=True) + NORM_EPS)
    return (y * w.astype(jnp.float32)).astype(x.dtype)


def _split_in(u):
    idx = []
    acc = 0
    for s in IN_SPLITS[:-1]:
        acc += s
        idx.append(acc)
    return jnp.split(u, idx, axis=-1)


def _modulation(cond, w, b):
    m = jax.nn.silu(cond) @ w + b
    return jnp.split(m, 3, axis=-1)


def _grid_positions(n_tokens):
    rows = n_tokens // GRID_W
    row = jnp.repeat(jnp.arange(rows, dtype=jnp.int32), GRID_W)
    col = jnp.tile(jnp.arange(GRID_W, dtype=jnp.int32), rows)
    return row, col


def _rope_1d(x, pos):
    half = x.shape[-1] // 2
    inv = ROPE_THETA ** (-jnp.arange(half, dtype=jnp.float32) / half)
    ang = pos.astype(jnp.float32)[:, None] * inv[None, :]
    cos = jnp.cos(ang)[:, None, :]
    sin = jnp.sin(ang)[:, None, :]
    xf = x.astype(jnp.float32)
    x1, x2 = xf[..., :half], xf[..., half:]
    out = jnp.concatenate([x1 * cos - x2 * sin, x2 * cos + x1 * sin], axis=-1)
    return out.astype(x.dtype)


def _rope_2d(x, row, col):
    half = x.shape[-1] // 2
    return jnp.concatenate([_rope_1d(x[..., :half], row), _rope_1d(x[..., half:], col)], axis=-1)


def _block_attention(q, k, v):
    bsz, sq = q.shape[0], q.shape[1]
    n_blocks = sq // Q_BLOCK
    scale = q.shape[-1] ** -0.5
    qb = jnp.swapaxes(q.reshape((bsz, n_blocks, Q_BLOCK) + q.shape[2:]), 0, 1)

    def one_block(q_blk):
        s = jnp.einsum('bqkgd,bskd->bkgqs', q_blk, k, preferred_element_type=jnp.float32) * scale
        p = jax.nn.softmax(s, axis=-1)
        return jnp.einsum('bkgqs,bskd->bqkgd', p.astype(v.dtype), v)

    o = lax.map(one_block, qb)
    return jnp.swapaxes(o, 0, 1).reshape(bsz, sq, -1)


def _shift_centered(s, mu_prev, mu_next):
    prev = jnp.pad(s[:, :-1], ((0, 0), (1, 0), (0, 0)))
    nxt = jnp.pad(s[:, 1:], ((0, 0), (0, 1), (0, 0)))
    return s + mu_prev * (prev - s) + mu_next * (nxt - s)


def _wkv_scan(s0, r, w, k, v, a_vec, b_vec, reverse):
    xs = tuple(jnp.moveaxis(t, 1, 0) for t in (r, w, k, v, a_vec, b_vec))

    def step(state, inp):
        r_t, w_t, k_t, v_t, a_t, b_t = inp
        sa = jnp.einsum('bhij,bhj->bhi', state, a_t)
        state = (state * w_t[:, :, None, :] + sa[..., None] * b_t[:, :, None, :]
                 + v_t[..., None] * k_t[:, :, None, :])
        y_t = jnp.einsum('bhij,bhj->bhi', state, r_t)
        return state, y_t

    s_fin, ys = lax.scan(step, s0, xs, reverse=reverse)
    return s_fin, jnp.moveaxis(ys, 0, 1)


def _branch_c(c_in, s0_f, s0_b, lp):
    f32 = jnp.float32
    bsz, t = c_in.shape[:2]
    x = _shift_centered(c_in, lp['c_mu_prev'], lp['c_mu_next']).astype(f32)
    wdt = C_WIDTH
    r = x[..., :wdt]
    k = x[..., wdt:2 * wdt]
    v = x[..., 2 * wdt:3 * wdt]
    wd = x[..., 3 * wdt:3 * wdt + 2 * C_DECAY_LORA].reshape(bsz, t, 2, C_DECAY_LORA)
    ad = x[..., 3 * wdt + 2 * C_DECAY_LORA:].reshape(bsz, t, 2, C_AAA_LORA)

    def hd(z):
        return z.reshape(bsz, t, C_HEADS, HEAD_DIM)

    w0 = lp['c_w0'].astype(f32)
    w_up = lp['c_w_up'].astype(f32)
    a0 = lp['c_a0'].astype(f32)
    a_up = lp['c_a_up'].astype(f32)
    k_a = lp['c_k_a'].astype(f32)
    r_k = lp['c_r_k'].astype(f32)
    rh, vh = hd(r), hd(v)
    kk = hd(k * lp['c_k_k'].astype(f32))
    kk = kk / jnp.maximum(jnp.sqrt(jnp.sum(kk * kk, axis=-1, keepdims=True)), 1e-12)
    ys, bonuses, finals = [], [], []
    for d, s0 in ((0, s0_f), (1, s0_b)):
        w_raw = w0[d] + jnp.tanh(wd[:, :, d]) @ w_up[d]
        decay = jnp.exp(-jnp.exp(-jax.nn.softplus(-w_raw) - 0.5))
        a = jax.nn.sigmoid(a0[d] + ad[:, :, d] @ a_up[d])
        k_d = hd(k * (1.0 + (a - 1.0) * k_a))
        s_fin, y_d = _wkv_scan(s0.astype(f32), rh, hd(decay), k_d, vh, -kk, kk * hd(a), reverse=(d == 1))
        ys.append(y_d)
        bonuses.append(jnp.sum(rh * k_d * r_k, axis=-1, keepdims=True) * vh)
        finals.append(s_fin)
    y = ys[0] + ys[1]
    mu = jnp.mean(y, axis=-1, keepdims=True)
    var = jnp.mean(jnp.square(y - mu), axis=-1, keepdims=True)
    yn = ((y - mu) * lax.rsqrt(var + C_GN_EPS)).reshape(bsz, t, wdt)
    yn = yn * lp['c_lnx_w'].astype(f32) + lp['c_lnx_b'].astype(f32)
    yn = yn + (bonuses[0] + bonuses[1]).reshape(bsz, t, wdt)
    return yn, finals[0], finals[1]


def _a_qkv(a_q, a_k, a_v, lp):
    bsz, t = a_q.shape[:2]
    q = _rmsnorm(a_q.reshape(bsz, t, A_HEADS, HEAD_DIM), lp['a_qnorm_w'])
    k = _rmsnorm(a_k.reshape(bsz, t, A_KV_HEADS, HEAD_DIM), lp['a_knorm_w'])
    v = a_v.reshape(bsz, t, A_KV_HEADS, HEAD_DIM)
    return q, k, v


def _mla_attend(q, ckv_all, kr_all, lp):
    bsz, s = ckv_all.shape[:2]
    k_nope = (ckv_all @ lp['b_w_uk']).reshape(bsz, s, B_HEADS, B_NOPE)
    k_rope = jnp.broadcast_to(kr_all[:, :, None, :], (bsz, s, B_HEADS, B_ROPE))
    k = jnp.concatenate([k_nope, k_rope], axis=-1)
    v = (ckv_all @ lp['b_w_uv']).reshape(bsz, s, B_HEADS, B_VDIM)
    return _block_attention(q[:, :, :, None, :], k, v)


def _merge(gates, ya, yb, yc, lp):
    ga, gb, gc = jnp.split(jax.nn.sigmoid(gates), 3, axis=-1)
    mixed = ga * (ya @ lp['w_oa']) + gb * (yb @ lp['w_ob']) + gc * (yc @ lp['w_oc'])
    return mixed @ lp['w_out']


def _context_mixer(h, lp):
    bsz, t = h.shape[:2]
    a_q, a_k, a_v, a_z, b_q, b_ckv, b_kr, b_z, c_in, c_z, gates = _split_in(h @ lp['w_in'])
    q, k, v = _a_qkv(a_q, a_k, a_v, lp)
    ya = _block_attention(q.reshape(bsz, t, A_KV_HEADS, A_GROUPS, HEAD_DIM), k, v) * jax.nn.silu(a_z)
    qb = b_q.reshape(bsz, t, B_HEADS, B_NOPE + B_ROPE)
    ckv = _rmsnorm(b_ckv, lp['b_kvnorm_w'])
    yb = _mla_attend(qb, ckv, b_kr, lp) * jax.nn.silu(b_z)
    s0 = jnp.zeros((bsz, C_HEADS, HEAD_DIM, HEAD_DIM), jnp.float32)
    yc, s_f, s_b = _branch_c(c_in, s0, s0, lp)
    yc = yc.astype(h.dtype) * jax.nn.silu(c_z)
    return _merge(gates, ya, yb, yc, lp), k, v, ckv, b_kr, s_f, s_b


def _latent_mixer(h, lp, row, col, ctx_k, ctx_v, ctx_ckv, ctx_kr, s0_f, s0_b):
    bsz, t = h.shape[:2]
    a_q, a_k, a_v, a_z, b_q, b_ckv, b_kr, b_z, c_in, c_z, gates = _split_in(h @ lp['w_in'])
    q, k, v = _a_qkv(a_q, a_k, a_v, lp)
    q = _rope_2d(q, row, col)
    k = _rope_2d(k, row, col)
    k_all = jnp.concatenate([k, ctx_k.astype(k.dtype)], axis=1)
    v_all = jnp.concatenate([v, ctx_v.astype(v.dtype)], axis=1)
    ya = _block_attention(q.reshape(bsz, t, A_KV_HEADS, A_GROUPS, HEAD_DIM), k_all, v_all) * jax.nn.silu(a_z)
    qb = b_q.reshape(bsz, t, B_HEADS, B_NOPE + B_ROPE)
    qb = jnp.concatenate([qb[..., :B_NOPE], _rope_2d(qb[..., B_NOPE:], row, col)], axis=-1)
    kr = _rope_2d(b_kr[:, :, None, :], row, col)[:, :, 0, :]
    ckv = _rmsnorm(b_ckv, lp['b_kvnorm_w'])
    ckv_all = jnp.concatenate([ckv, ctx_ckv.astype(ckv.dtype)], axis=1)
    kr_all = jnp.concatenate([kr, ctx_kr.astype(kr.dtype)], axis=1)
    yb = _mla_attend(qb, ckv_all, kr_all, lp) * jax.nn.silu(b_z)
    yc, _, _ = _branch_c(c_in, s0_f, s0_b, lp)
    yc = yc.astype(h.dtype) * jax.nn.silu(c_z)
    return _merge(gates, ya, yb, yc, lp)


def setup_inputs(seed: int = 0) -> dict:
    key = jax.random.key(seed)
    keys = list(jax.random.split(key, 48))

    def nrm(shape, scale):
        return jax.random.normal(keys.pop(), shape, jnp.float32) * scale

    def uni(shape, lo, hi):
        return jax.random.uniform(keys.pop(), shape, jnp.float32, lo, hi)

    D = D_MODEL
    return {
        'x_prompt': nrm((BATCH, SEQ, D), 1.0),
        'x_sample': nrm((DEC_BATCH, DEC_SEQ, D), 1.0),
        'cache_a_k': nrm((DEC_BATCH, DEPTH, PAST_LEN, A_KV_HEADS, HEAD_DIM), 1.0),
        'cache_a_v': nrm((DEC_BATCH, DEPTH, PAST_LEN, A_KV_HEADS, HEAD_DIM), 1.0),
        'cache_b_ckv': nrm((DEC_BATCH, DEPTH, PAST_LEN, B_KV_RANK), 1.0),
        'cache_b_krope': nrm((DEC_BATCH, DEPTH, PAST_LEN, B_ROPE), 1.0),
        'state_c_fwd': nrm((DEC_BATCH, DEPTH, C_HEADS, HEAD_DIM, HEAD_DIM), 1.0),
        'state_c_bwd': nrm((DEC_BATCH, DEPTH, C_HEADS, HEAD_DIM, HEAD_DIM), 1.0),
        'c': nrm((DEC_BATCH, D), 1.0),
        'c_ctx': nrm((D,), 1.0),
        'norm_w': 1.0 + nrm((DEPTH, D), 0.1),
        'w_mod': nrm((DEPTH, D, 3 * D), 0.5 * D ** -0.5),
        'b_mod': nrm((DEPTH, 3 * D), 0.02),
        'w_in': nrm((DEPTH, D, IN_DIM), D ** -0.5),
        'a_qnorm_w': 1.0 + nrm((DEPTH, HEAD_DIM), 0.1),
        'a_knorm_w': 1.0 + nrm((DEPTH, HEAD_DIM), 0.1),
        'b_kvnorm_w': 1.0 + nrm((DEPTH, B_KV_RANK), 0.1),
        'b_w_uk': nrm((DEPTH, B_KV_RANK, B_HEADS * B_NOPE), B_KV_RANK ** -0.5),
        'b_w_uv': nrm((DEPTH, B_KV_RANK, B_HEADS * B_VDIM), B_KV_RANK ** -0.5),
        'c_mu_prev': uni((DEPTH, C_SHIFT_DIM), 0.0, 0.5),
        'c_mu_next': uni((DEPTH, C_SHIFT_DIM), 0.0, 0.5),
        'c_w0': uni((DEPTH, 2, C_WIDTH), -4.0, 1.0),
        'c_w_up': nrm((DEPTH, 2, C_DECAY_LORA, C_WIDTH), 0.5 * C_DECAY_LORA ** -0.5),
        'c_a0': nrm((DEPTH, 2, C_WIDTH), 0.1),
        'c_a_up': nrm((DEPTH, 2, C_AAA_LORA, C_WIDTH), 0.5 * C_AAA_LORA ** -0.5),
        'c_k_k': 0.85 + nrm((DEPTH, C_WIDTH), 0.05),
        'c_k_a': 1.0 + nrm((DEPTH, C_WIDTH), 0.05),
        'c_r_k': nrm((DEPTH, C_HEADS, HEAD_DIM), 0.1),
        'c_lnx_w': 1.0 + nrm((DEPTH, C_WIDTH), 0.1),
        'c_lnx_b': nrm((DEPTH, C_WIDTH), 0.01),
        'w_oa': nrm((DEPTH, A_WIDTH, D), A_WIDTH ** -0.5),
        'w_ob': nrm((DEPTH, B_WIDTH, D), B_WIDTH ** -0.5),
        'w_oc': nrm((DEPTH, C_WIDTH, D), C_WIDTH ** -0.5),
        'w_out': nrm((DEPTH, D, D), D ** -0.5),
        'final_norm_w': 1.0 + nrm((D,), 0.1),
    }


def reference(x_prompt, x_sample, cache_a_k, cache_a_v, cache_b_ckv, cache_b_krope, state_c_fwd, state_c_bwd,
              c, c_ctx, norm_w, w_mod, b_mod, w_in, a_qnorm_w, a_knorm_w, b_kvnorm_w, b_w_uk, b_w_uv,
              c_mu_prev, c_mu_next, c_w0, c_w_up, c_a0, c_a_up, c_k_k, c_k_a, c_r_k, c_lnx_w, c_lnx_b,
              w_oa, w_ob, w_oc, w_out, final_norm_w):
    dt = x_prompt.dtype
    row, col = _grid_positions(x_sample.shape[1])
    xp = x_prompt
    xs = x_sample
    new_ak, new_av, new_ckv, new_kr, new_sf, new_sb = [], [], [], [], [], []
    for l in range(DEPTH):
        lp = {
            'w_in': w_in[l], 'a_qnorm_w': a_qnorm_w[l], 'a_knorm_w': a_knorm_w[l],
            'b_kvnorm_w': b_kvnorm_w[l], 'b_w_uk': b_w_uk[l], 'b_w_uv': b_w_uv[l],
            'c_mu_prev': c_mu_prev[l], 'c_mu_next': c_mu_next[l], 'c_w0': c_w0[l], 'c_w_up': c_w_up[l],
            'c_a0': c_a0[l], 'c_a_up': c_a_up[l], 'c_k_k': c_k_k[l], 'c_k_a': c_k_a[l], 'c_r_k': c_r_k[l],
            'c_lnx_w': c_lnx_w[l], 'c_lnx_b': c_lnx_b[l],
            'w_oa': w_oa[l], 'w_ob': w_ob[l], 'w_oc': w_oc[l], 'w_out': w_out[l],
        }
        shift, scale, gate = _modulation(c_ctx[None, None, :], w_mod[l], b_mod[l])
        h = _rmsnorm(xp, norm_w[l]) * (1.0 + scale) + shift
        out, ak, av, ckv, kr, s_f, s_b = _context_mixer(h, lp)
        xp = xp + gate * out
        new_ak.append(ak)
        new_av.append(av)
        new_ckv.append(ckv)
        new_kr.append(kr)
        new_sf.append(s_f.astype(dt))
        new_sb.append(s_b.astype(dt))
        shift, scale, gate = _modulation(c[:, None, :], w_mod[l], b_mod[l])
        h = _rmsnorm(xs, norm_w[l]) * (1.0 + scale) + shift
        out = _latent_mixer(h, lp, row, col, cache_a_k[:, l], cache_a_v[:, l], cache_b_ckv[:, l],
                            cache_b_krope[:, l], state_c_fwd[:, l], state_c_bwd[:, l])
        xs = xs + gate * out
    y_prompt = _rmsnorm(xp, final_norm_w)
    y_sample = _rmsnorm(xs, final_norm_w)
    new_a_k = jnp.stack(new_ak, axis=1)
    new_a_v = jnp.stack(new_av, axis=1)
    new_b_ckv = jnp.stack(new_ckv, axis=1)
    new_b_krope = jnp.stack(new_kr, axis=1)
    new_c_state_fwd = jnp.stack(new_sf, axis=1)
    new_c_state_bwd = jnp.stack(new_sb, axis=1)
    return (y_prompt, y_sample, new_a_k, new_a_v, new_b_ckv, new_b_krope, new_c_state_fwd, new_c_state_bwd)
```

```python
import os
import contextlib
import numpy as np
import concourse.bass as bass
import concourse.mybir as mybir
from concourse.bass_utils import run_bass_kernel_spmd

F32 = mybir.dt.float32
BF16 = mybir.dt.bfloat16
AF = mybir.ActivationFunctionType
ALU = mybir.AluOpType
AX = mybir.AxisListType

D = 1024
DEPTH = 4
NPS = 4
TP = 256
TS = 4096
PAST = 512
IN_DIM = 8096
NCORES = 8
O_AQ, O_AK, O_AV, O_AZ, O_BQ, O_CKV, O_KR, O_BZ, O_CIN, O_CZ, O_G = 0, 512, 640, 768, 1280, 2048, 2176, 2208, 2720, 4512, 5024
NORM_EPS = 1e-6
GN_EPS = 64e-5
DEC_C = -float(np.exp(-0.5))

EPOCH = 30000
N_EPOCH = {"pe": 16, "dve": 12, "act": 10, "pool": 8}
N_DMA_SEMS = 40
SB_BASE = 16512
SB_LIMIT = 229344


class Sched:
    def __init__(self, nc):
        self.nc = nc
        self.ops = {e: [] for e in ("pe", "dve", "act", "pool", "sp")}
        self.count = {e: 0 for e in ("pe", "dve", "act", "pool")}
        self.last_w = {}
        self.readers = {}
        self.dma_rr = 0
        self.dma_tot = [0] * N_DMA_SEMS
        self.sems = {}

    def _deps(self, eng, reads, writes):
        d = {}

        def add(tok, raw):
            sk, v, src = tok
            if src == eng and sk[0] != "dma":
                if not raw or eng == "pe":
                    return
            if d.get(sk, 0) < v:
                d[sk] = v

        for k in reads:
            w = self.last_w.get(k)
            if w is not None:
                add(w, True)
        for k in writes:
            w = self.last_w.get(k)
            if w is not None:
                add(w, False)
            for r in self.readers.get(k, {}).values():
                add(r, False)
        return d

    def _commit(self, tok, reads, writes):
        sk = tok[0]
        for k in reads:
            self.readers.setdefault(k, {})[sk] = tok
        for k in writes:
            self.last_w[k] = tok
            self.readers[k] = {}

    def op(self, eng, fn, reads=(), writes=()):
        deps = self._deps(eng, reads, writes)
        n = self.count[eng]
        self.count[eng] = n + 1
        ep, v = divmod(n, EPOCH)
        tok = ((eng, ep), v + 1, eng)
        self._commit(tok, reads, writes)
        self.ops[eng].append(("op", fn, deps, tok))
        return tok

    def dma(self, q, fn, reads=(), writes=()):
        deps = self._deps("dmaq", reads, writes)
        i = self.dma_rr
        self.dma_rr = (i + 1) % N_DMA_SEMS
        prev = self.dma_tot[i]
        if prev:
            sk = ("dma", i)
            if deps.get(sk, 0) < prev:
                deps[sk] = prev
        self.dma_tot[i] = prev + 16
        tok = (("dma", i), prev + 16, "dmaq")
        self._commit(tok, reads, writes)
        self.ops[q].append(("dma", fn, deps, tok))
        return tok

    def wait_all(self, eng, toks):
        deps = {}
        for sk, v, _ in toks:
            deps[sk] = max(deps.get(sk, 0), v)
        self.ops[eng].append(("wait", None, deps, None))

    def alloc_sems(self, es):
        nc = self.nc
        for e, ne in N_EPOCH.items():
            for ep in range(ne):
                self.sems[(e, ep)] = es.enter_context(nc.semaphore(f"s_{e}{ep}"))
        for i in range(N_DMA_SEMS):
            self.sems[("dma", i)] = es.enter_context(nc.semaphore(f"s_dma{i}"))

    def barrier(self):
        toks = []
        for e, n in self.count.items():
            if n:
                ep, v = divmod(n - 1, EPOCH)
                toks.append(((e, ep), v + 1, e))
        for i in range(N_DMA_SEMS):
            if self.dma_tot[i]:
                toks.append((("dma", i), self.dma_tot[i], "dmaq"))
        for e in ("pe", "dve", "act", "pool", "sp"):
            self.wait_all(e, toks)

    def emit_block(self):
        nc = self.nc
        sems = self.sems
        for e, ne in N_EPOCH.items():
            assert self.count[e] <= ne * EPOCH, (e, self.count[e])
        with nc.Block() as block:
            def run(engname):
                lst = self.ops[engname]

                def body(eng):
                    waited = {}
                    for kind, fn, deps, tok in lst:
                        for sk, v in deps.items():
                            if waited.get(sk, 0) < v:
                                eng.wait_ge(sems[sk], v)
                                waited[sk] = v
                        if kind == "wait":
                            continue
                        ins = fn(eng)
                        ins.then_inc(sems[tok[0]], 16 if kind == "dma" else 1)
                return body

            block.tensor(run("pe"))
            block.vector(run("dve"))
            block.scalar(run("act"))
            block.gpsimd(run("pool"))
            block.sync(run("sp"))
        self.ops = {e: [] for e in ("pe", "dve", "act", "pool", "sp")}


class Arena:
    cnt = 0

    def __init__(self, nc, base, limit):
        self.nc, self.off, self.limit = nc, base, limit

    def alloc(self, name, shape, dt):
        esz = 4 if dt == F32 else 2
        nb = int(np.prod(shape[1:])) * esz
        nb = (nb + 63) // 64 * 64
        off = self.off
        self.off += nb
        assert self.off <= self.limit, (name, self.off, self.limit)
        Arena.cnt += 1
        return self.nc.alloc_sbuf_tensor_at(f"{name}_{Arena.cnt}", list(shape), dt, offset=off)

    def can(self, shape, dt):
        esz = 4 if dt == F32 else 2
        nb = (int(np.prod(shape[1:])) * esz + 63) // 64 * 64
        return self.off + nb <= self.limit

    def mark(self):
        return self.off

    def release(self, m):
        self.off = m


def _k(x):
    return x if isinstance(x, (str, tuple)) else x.name


def host_constants():
    c = {}
    c["ident"] = np.eye(128, dtype=np.float32)
    s = np.arange(128)[:, None]
    t = np.arange(128)[None, :]
    bank = np.zeros((2, 128, 512), np.float32)
    lvl = np.zeros((2, 7, 128, 128), np.float32)
    tri = np.zeros((2, 128, 3 * 128 + 2), np.float32)
    for d in range(2):
        if d == 0:
            strict = (s < t).astype(np.float32)
            incl = (s <= t).astype(np.float32)
        else:
            strict = (s > t).astype(np.float32)
            incl = (s >= t).astype(np.float32)
        bank[d] = np.concatenate([strict, incl, strict, incl], axis=1)
        for l in range(7):
            b = 1 << l
            tt_ = np.arange(128)[:, None]
            ss_ = np.arange(128)[None, :]
            same = (tt_ // (2 * b)) == (ss_ // (2 * b))
            t_second = (tt_ // b) % 2 == 1
            s_second = (ss_ // b) % 2 == 1
            if d == 0:
                m = same & t_second & (~s_second)
            else:
                m = same & (~t_second) & s_second
            lvl[d, l] = m.astype(np.float32)
        if d == 0:
            G = (s <= t).astype(np.float32) - (s <= 63).astype(np.float32)
            G1 = (s < t).astype(np.float32) - (s <= 63).astype(np.float32)
            Dm = (s > t).astype(np.float32)
            cm = (np.arange(128) <= 63).astype(np.float32)
        else:
            G = (s >= t).astype(np.float32) - (s >= 64).astype(np.float32)
            G1 = (s > t).astype(np.float32) - (s >= 64).astype(np.float32)
            Dm = (s < t).astype(np.float32)
            cm = (np.arange(128) >= 64).astype(np.float32)
        tri[d, :, 0:128] = DEC_C * G
        tri[d, :, 128:256] = DEC_C * G1
        tri[d, :, 256:384] = DEC_C * Dm
        tri[d, :, 384] = DEC_C * cm
        tri[d, :, 385] = DEC_C
    c["bankmask"] = bank
    c["lvlmask"] = lvl
    c["tri"] = tri
    hs = np.zeros((128, 2), np.float32)
    hs[:64, 0] = 1.0
    hs[64:, 1] = 1.0
    c["headsel"] = hs
    tok = np.arange(TS)
    row = (tok // 64).astype(np.float32)
    col = (tok % 64).astype(np.float32)
    tab = np.zeros((TS, 192), np.float32)

    def fill(base_c, base_s, half):
        inv = 10000.0 ** (-np.arange(half, dtype=np.float32) / half)
        for pi, pos in enumerate((row, col)):
            ang = pos[:, None] * inv[None, :]
            co, si = np.cos(ang), np.sin(ang)
            o = pi * 2 * half
            tab[:, base_c + o: base_c + o + half] = co
            tab[:, base_c + o + half: base_c + o + 2 * half] = co
            tab[:, base_s + o: base_s + o + half] = -si
            tab[:, base_s + o + half: base_s + o + 2 * half] = si

    fill(0, 64, 16)
    fill(128, 160, 8)
    c["rope"] = tab
    return c


class Builder:
    def __init__(self, stage=99):
        self.stage = stage
        self.nc = bass.Bass("TRN2", target_bir_lowering=False)
        self.S = Sched(self.nc)
        self.din = {}
        self.dout = {}

    def inp(self, name, shape):
        self.din[name] = self.nc.dram_tensor(name, list(shape), F32, kind="ExternalInput").ap()
        return self.din[name]

    def outp(self, name, shape):
        self.dout[name] = self.nc.dram_tensor(name, list(shape), F32, kind="ExternalOutput").ap()
        return self.dout[name]

    def scratch(self, name, shape, dt):
        return self.nc.dram_tensor(name, list(shape), dt).ap()

    def dma(self, out, in_, rk=None, wk=None, q="sp"):
        r = [_k(in_)] if rk is None else rk
        w = [_k(out)] if wk is None else wk
        self.S.dma(q, lambda e: e.dma_start(out=out, in_=in_), r, w)

    def mm(self, out, lhsT, rhs, start=True, stop=True, xr=()):
        self.S.op("pe", lambda e: e.matmul(out, lhsT, rhs, start=start, stop=stop),
                  [_k(lhsT), _k(rhs)] + [_k(x) for x in xr], [_k(out)])

    def tr(self, out, in_):
        p = in_.shape[0]
        idn = self.ident[0:p, 0:p]
        self.S.op("pe", lambda e: e.transpose(out, in_, idn), [_k(in_), _k(self.ident)], [_k(out)])

    def act(self, out, in_, func, bias=None, scale=None, accum=None):
        r = [_k(in_)]
        kw = {}
        if bias is not None:
            kw["bias"] = bias
            if not isinstance(bias, float):
                r.append(_k(bias))
        if scale is not None:
            kw["scale"] = scale
            if not isinstance(scale, float):
                r.append(_k(scale))
        w = [_k(out)]
        if accum is not None:
            kw["accum_out"] = accum
            w.append(_k(accum))
        self.S.op("act", lambda e: e.activation(out, in_, func, **kw), r, w)

    def tt(self, eng, out, a, b, op):
        self.S.op(eng, lambda e: e.tensor_tensor(out, a, b, op), [_k(a), _k(b)], [_k(out)])

    def ts(self, eng, out, a, s1, s2, op0, op1=None):
        r = [_k(a)] + [_k(s) for s in (s1, s2) if s is not None and not isinstance(s, (float, int))]
        if op1 is None:
            self.S.op(eng, lambda e: e.tensor_scalar(out, a, s1, s2, op0), r, [_k(out)])
        else:
            self.S.op(eng, lambda e: e.tensor_scalar(out, a, s1, s2, op0, op1), r, [_k(out)])

    def stt(self, out, a, scalar, b, op0, op1):
        r = [_k(a), _k(b)] + ([] if isinstance(scalar, (float, int)) else [_k(scalar)])
        self.S.op("dve", lambda e: e.scalar_tensor_tensor(out, a, scalar, b, op0, op1), r, [_k(out)])

    def cp(self, eng, out, in_):
        if eng == "act":
            self.S.op("act", lambda e: e.copy(out, in_), [_k(in_)], [_k(out)])
        else:
            self.S.op(eng, lambda e: e.tensor_copy(out, in_), [_k(in_)], [_k(out)])

    def memset(self, eng, out, val):
        self.S.op(eng, lambda e: e.memset(out, val), [], [_k(out)])

    def red(self, out, in_, op=ALU.add):
        self.S.op("dve", lambda e: e.tensor_reduce(out, in_, AX.X, op), [_k(in_)], [_k(out)])

    def recip(self, out, in_):
        self.S.op("dve", lambda e: e.reciprocal(out, in_), [_k(in_)], [_k(out)])

    def rstd(self, out, ss, n, eps, tmp):
        self.act(tmp, ss, AF.Sqrt, bias=float(eps), scale=1.0 / n)
        self.recip(out, tmp)

    def declare(self):
        i = self.inp
        self.x_prompt = i("x_prompt", [NPS, TP, D])
        self.x_sample = i("x_sample", [TS, D])
        self.cache_a_k = i("cache_a_k", [DEPTH, PAST, 128])
        self.cache_a_v = i("cache_a_v", [DEPTH, PAST, 128])
        self.cache_ckv = i("cache_b_ckv", [DEPTH, PAST, 128])
        self.cache_kr = i("cache_b_krope", [DEPTH, PAST, 32])
        self.state_f = i("state_c_fwd", [DEPTH, 8, 64, 64])
        self.state_b = i("state_c_bwd", [DEPTH, 8, 64, 64])
        self.cond = i("cond", [16, 128])
        self.nb_rows = i("nb_rows", [128, 128])
        self.w_mod = i("w_mod", [DEPTH, D, 3 * D])
        self.w_in = i("w_in", [DEPTH, D, IN_DIM])
        self.qkn = i("qkn", [DEPTH, 256])
        self.b_w_uk = i("b_w_uk", [DEPTH, 128, 512])
        self.b_w_uv = i("b_w_uv", [DEPTH, 128, 512])
        self.lfm_rows = i("lfm_rows", [DEPTH, 48, 128])
        self.c_w0 = i("c_w0", [DEPTH, 2, 512])
        self.c_w_up = i("c_w_up", [DEPTH, 128, 512])
        self.c_a_up = i("c_a_up", [DEPTH, 128, 512])
        self.lnx = i("lnx", [DEPTH, 1024])
        self.w_oa = i("w_oa", [DEPTH, 512, D])
        self.w_ob = i("w_ob", [DEPTH, 512, D])
        self.w_oc = i("w_oc", [DEPTH, 512, D])
        self.w_out = i("w_out", [DEPTH, D, D])
        self.fnw = i("final_norm_w", [D])
        self.c_ident = i("c_ident", [128, 128])
        self.c_bank = i("c_bank", [2, 128, 512])
        self.c_lvl = i("c_lvl", [2, 7, 128, 128])
        self.c_tri = i("c_tri", [2, 128, 386])
        self.c_hsel = i("c_hsel", [128, 2])
        self.c_rope = i("c_rope", [TS, 192])
        o = self.outp
        self.y_prompt = o("y_prompt", [NPS, TP, D])
        self.y_sample = o("y_sample", [TS, D])
        self.new_a_k = o("new_a_k", [NPS, DEPTH, TP, 128])
        self.new_a_v = o("new_a_v", [NPS, DEPTH, TP, 128])
        self.new_ckv = o("new_b_ckv", [NPS, DEPTH, TP, 128])
        self.new_kr = o("new_b_krope", [NPS, DEPTH, TP, 32])
        self.new_sf = o("new_c_state_fwd", [NPS, DEPTH, 8, 64, 64])
        self.new_sb = o("new_c_state_bwd", [NPS, DEPTH, 8, 64, 64])
        sc = self.scratch
        self.wsc = [sc(f"wsc{l}", [128, 8, IN_DIM], BF16) for l in range(DEPTH)]
        self.wosc = [[sc(f"wo{l}_{b}", ([64, 8, D] if b < 2 else [128, 4, D]), BF16) for b in range(3)] for l in range(DEPTH)]
        self.woutsc = [sc(f"wout{l}", [128, 8, D], BF16) for l in range(DEPTH)]
        self.xs_p = sc("xs_p", [NPS, TP, D], F32)
        self.xs_s = sc("xs_s", [TS, D], F32)
        self.yb_sc = sc("yb_sc", [TS, 512], F32)
        self.yc_sc = sc("yc_sc", [TS, 512], F32)

    def build(self):
        nc, S = self.nc, self.S
        self.declare()
        with contextlib.ExitStack() as es:
            S.alloc_sems(es)
            self.ps = [es.enter_context(nc.psum_tensor(f"ps{i}", [128, 512], F32)) for i in range(8)]
            self.carena = Arena(nc, SB_BASE, SB_BASE + 24576)
            self.arena = Arena(nc, SB_BASE + 24576, SB_LIMIT)
            self.prologue()
            seqs = [("p", s) for s in range(NPS)] + [("s", 0)]
            if self.stage < 50:
                seqs = seqs[:NPS]
            nlayers = DEPTH if self.stage >= 40 else 1
            for l in range(nlayers):
                self.layer_prologue(l)
                for kind, si in seqs:
                    self.run_seq(l, kind, si)
            S.barrier()
            S.emit_block()
        return nc

    def prologue(self):
        A = self.carena
        S = self.S
        self.ident = A.alloc("ident", [128, 128], F32)
        self.identb = A.alloc("identb", [128, 128], BF16)
        self.ones = A.alloc("ones", [128, 128], F32)
        self.onesb = A.alloc("onesb", [128, 128], BF16)
        self.bank = A.alloc("bank", [128, 2, 512], BF16)
        self.lvl = A.alloc("lvl", [128, 14, 128], BF16)
        self.tri = A.alloc("tri", [128, 2, 386], F32)
        self.hsel = A.alloc("hsel", [128, 2], F32)
        self.modA = A.alloc("modA", [128, DEPTH, 2, 8], F32)
        self.modB = A.alloc("modB", [128, DEPTH, 2, 8], F32)
        self.modG = A.alloc("modG", [128, DEPTH, 2, 8], F32)
        self.qkn_t = A.alloc("qkn_t", [128, 256], F32)
        self.uk = A.alloc("uk", [128, 512], BF16)
        self.uv = A.alloc("uv", [128, 512], BF16)
        self.wup = A.alloc("wup", [128, 512], BF16)
        self.aup = A.alloc("aup", [128, 512], BF16)
        self.w0row = A.alloc("w0row", [1, 2, 512], BF16)
        self.lfm = A.alloc("lfm", [128, 48], F32)
        self.c0fm = A.alloc("c0fm", [128, 14], F32)
        self.bonb = A.alloc("bonb", [128, 32, 8], F32)
        self.fnw_t = A.alloc("fnw_t", [128, D], F32)
        self.bones = A.alloc("bones", [128, 128], F32)
        self.omka = A.alloc("omka", [128, 4], F32)
        ar = self.arena
        m0 = ar.mark()
        st = ar.alloc("st", [128, 128], F32)
        st2 = ar.alloc("st2", [128, 512], F32)
        lv = ar.alloc("lv", [128, 14, 128], F32)
        cst = ar.alloc("cst", [16, 128], F32)
        scond = ar.alloc("scond", [128, 16], F32)
        nbfm = ar.alloc("nbfm", [128, 128], F32)
        mods = ar.alloc("mods", [128, DEPTH, 24, 2], F32)
        wst = [ar.alloc(f"wst{i}", [128, 8, 512], F32) for i in range(2)]
        self.dma(self.ident[:], self.c_ident)
        self.cp("dve", self.identb[:], self.ident[:])
        self.memset("pool", self.ones[:], 1.0)
        self.memset("pool", self.onesb[:], 1.0)
        self.memset("pool", self.bones[:], 0.0)
        self.memset("pool", self.bones[0:64, 0:64], 1.0)
        self.memset("pool", self.bones[64:128, 64:128], 1.0)
        for d in range(2):
            self.dma(st2[:], self.c_bank[d])
            self.cp("dve", self.bank[:, d, :], st2[:])
        for d in range(2):
            self.dma(lv[:, 0:7, :], self.c_lvl[d].rearrange("l t s -> t l s"))
            self.cp("dve", self.lvl[:, d * 7:(d + 1) * 7, :], lv[:, 0:7, :])
        self.dma(self.tri[:], self.c_tri.rearrange("d s c -> s d c"))
        self.dma(self.hsel[:], self.c_hsel)
        self.dma(self.fnw_t[:], self.fnw.partition_broadcast(128))
        self.dma(cst[:], self.cond)
        self.tr(self.ps[0][:, 0:16], cst[:])
        self.act(scond[:], self.ps[0][:, 0:16], AF.Silu)
        self.dma(st[:], self.nb_rows)
        self.tr(self.ps[1][:, 0:128], st[:])
        self.cp("dve", nbfm[:], self.ps[1][:, 0:128])
        nwm = 0
        for l in range(DEPTH):
            pm = self.ps[2 + (l % 2)]
            for j in range(6):
                w = wst[nwm % 2]
                nwm += 1
                self.dma(w[:], self.w_mod[l].rearrange("(kc p) n -> p kc n", p=128)[:, :, j * 512:(j + 1) * 512])
                for f in range(4):
                    fc = j * 4 + f
                    for kc in range(8):
                        self.mm(pm[:, fc * 2:fc * 2 + 2], w[:, kc, f * 128:(f + 1) * 128],
                                scond[:, kc:16:8], start=(kc == 0), stop=(kc == 7))
            self.tt("dve", mods[:, l, :, :], pm[:, 0:48].rearrange("p (a b) -> p a b", b=2),
                    nbfm[:, l * 32 + 8:l * 32 + 32].unsqueeze(2).broadcast_to([128, 24, 2]), ALU.add)
            for c in range(2):
                self.ts("dve", self.modA[:, l, c, :], mods[:, l, 8:16, c], 1.0, None, ALU.add)
                self.tt("dve", self.modA[:, l, c, :], self.modA[:, l, c, :], nbfm[:, l * 32:l * 32 + 8], ALU.mult)
                self.cp("dve", self.modB[:, l, c, :], mods[:, l, 0:8, c])
                self.cp("dve", self.modG[:, l, c, :], mods[:, l, 16:24, c])
        S.barrier()
        S.emit_block()
        ar.release(m0)

    def layer_prologue(self, l):
        S, ar = self.S, self.arena
        m0 = ar.mark()
        wst = [ar.alloc(f"cst{i}", [128, 8, 512], F32) for i in range(2)]
        wbf = [ar.alloc(f"cbf{i}", [128, 8, 512], BF16) for i in range(2)]
        st = ar.alloc("lst", [128, 512], F32)
        st48 = ar.alloc("lst48", [48, 128], F32)
        n = 0
        engs = ["dve", "pool", "act"]
        src = self.w_in[l].rearrange("(kc p) n -> p kc n", p=128)
        for c0 in range(0, IN_DIM, 512):
            w = min(512, IN_DIM - c0)
            a, b = wst[n % 2], wbf[n % 2]
            self.dma(a[:, :, 0:w], src[:, :, c0:c0 + w])
            self.cp(engs[n % 3], b[:, :, 0:w], a[:, :, 0:w])
            self.dma(self.wsc[l][:, :, c0:c0 + w], b[:, :, 0:w])
            n += 1
        src = self.w_out[l].rearrange("(kc p) n -> p kc n", p=128)
        for c0 in range(0, D, 512):
            a, b = wst[n % 2], wbf[n % 2]
            self.dma(a[:], src[:, :, c0:c0 + 512])
            self.cp(engs[n % 3], b[:], a[:])
            self.dma(self.woutsc[l][:, :, c0:c0 + 512], b[:])
            n += 1
        for bi, wsrc in enumerate((self.w_oa, self.w_ob, self.w_oc)):
            if bi < 2:
                src = wsrc[l].rearrange("(h p) n -> p h n", p=64)
                np_, nh_ = 64, 8
            else:
                src = wsrc[l].rearrange("(kc p) n -> p kc n", p=128)
                np_, nh_ = 128, 4
            for c0 in range(0, D, 512):
                a, b = wst[n % 2], wbf[n % 2]
                self.dma(a[0:np_, 0:nh_, :], src[:, :, c0:c0 + 512])
                self.cp(engs[n % 3], b[0:np_, 0:nh_, :], a[0:np_, 0:nh_, :])
                self.dma(self.wosc[l][bi][:, :, c0:c0 + 512], b[0:np_, 0:nh_, :])
                n += 1
        for dst, srcw in ((self.uk, self.b_w_uk), (self.uv, self.b_w_uv), (self.wup, self.c_w_up), (self.aup, self.c_a_up)):
            self.dma(st[:], srcw[l])
            self.cp("dve", dst[:], st[:])
        self.dma(st[0:1, 0:512], self.c_w0[l, 0:1, :])
        self.cp("dve", self.w0row[:, 0, :], st[0:1, 0:512])
        self.dma(st[0:1, 0:512], self.c_w0[l, 1:2, :])
        self.cp("dve", self.w0row[:, 1, :], st[0:1, 0:512])
        self.dma(self.qkn_t[:], self.qkn[l].partition_broadcast(128))
        self.dma(st48[:], self.lfm_rows[l])
        self.tr(self.ps[0][:, 0:48], st48[:])
        self.cp("dve", self.lfm[:], self.ps[0][:, 0:48])
        self.tt("dve", self.c0fm[:], self.lfm[:, 12:26], self.lfm[:, 26:40], ALU.add)
        self.ts("dve", self.c0fm[:], self.c0fm[:], -1.0, 1.0, ALU.mult, ALU.add)
        self.ts("dve", self.omka[:], self.lfm[:, 4:8], -1.0, 1.0, ALU.mult, ALU.add)
        S.barrier()
        S.emit_block()
        ar.release(m0)

    def run_seq(self, l, kind, si):
        S, ar = self.S, self.arena
        T = TP if kind == "p" else TS
        cond = 0 if kind == "p" else 1
        last = (l == DEPTH - 1)
        if l == 0:
            xin = self.x_prompt[si] if kind == "p" else self.x_sample
        else:
            xin = self.xs_p[si] if kind == "p" else self.xs_s
        xout = self.xs_p[si] if kind == "p" else self.xs_s
        ctx = dict(l=l, kind=kind, si=si, T=T, cond=cond, xin=xin, xout=xout, last=last)
        m0 = ar.mark()
        hT = ar.alloc("hT", [128, 8, T + 2], BF16)
        mH = ar.mark()
        ctx["hT"] = hT
        self.phase_H(ctx)
        S.barrier()
        S.emit_block()
        if self.stage >= 20:
            self.phase_rwkv(ctx)
        self.phase_KV(ctx)
        S.barrier()
        S.emit_block()
        if self.stage >= 30:
            lo = Arena(self.nc, m0, mH)
            self.phase_Q(ctx, lo, ar)
            S.barrier()
            S.emit_block()
        ar.release(m0)

    def compute_hT(self, ctx, hT, col0, x_ap, nst, work):
        l, cond = ctx["l"], ctx["cond"]
        xt, xn, ss, tmp, rs = work
        for st in range(nst):
            b = st % 2
            self.dma(xt[b][:], x_ap[st * 128:(st + 1) * 128, :])
            self.act(xn[b][:], xt[b][:], AF.Square, accum=ss[:, b:b + 1])
            self.rstd(rs[:, b:b + 1], ss[:, b:b + 1], D, NORM_EPS, tmp[:, b:b + 1])
            self.ts("dve", xn[b][:], xt[b][:], rs[:, b:b + 1], None, ALU.mult)
            for half in range(2):
                pb = self.ps[(2 * st + half) % 4]
                for q in range(4):
                    kc = half * 4 + q
                    self.tr(pb[:, q * 128:(q + 1) * 128], xn[b][:, kc * 128:(kc + 1) * 128])
                for q in range(4):
                    kc = half * 4 + q
                    dst = hT[:, kc, col0 + st * 128: col0 + (st + 1) * 128]
                    if q % 2 == 0:
                        self.act(dst, pb[:, q * 128:(q + 1) * 128], AF.Identity,
                                 bias=self.modB[:, l, cond, kc:kc + 1], scale=self.modA[:, l, cond, kc:kc + 1])
                    else:
                        self.ts("dve", dst, pb[:, q * 128:(q + 1) * 128], self.modA[:, l, cond, kc:kc + 1],
                                self.modB[:, l, cond, kc:kc + 1], ALU.mult, ALU.add)

    def hT_work(self, ar):
        xt = [ar.alloc(f"xt{i}", [128, D], F32) for i in range(2)]
        xn = [ar.alloc(f"xn{i}", [128, D], F32) for i in range(2)]
        ss = ar.alloc("ss", [128, 2], F32)
        tmp = ar.alloc("tmpr", [128, 2], F32)
        rs = ar.alloc("rs", [128, 2], F32)
        return (xt, xn, ss, tmp, rs)

    def phase_H(self, ctx):
        ar = self.arena
        T, hT = ctx["T"], ctx["hT"]
        m = ar.mark()
        work = self.hT_work(ar)
        self.memset("pool", hT[:, :, 0:1], 0.0)
        self.memset("pool", hT[:, :, T + 1:T + 2], 0.0)
        self.compute_hT(ctx, hT, 1, ctx["xin"], T // 128, work)
        ar.release(m)

    def phase_KV(self, ctx):
        ar = self.arena
        l, kind, si, T, hT = ctx["l"], ctx["kind"], ctx["si"], ctx["T"], ctx["hT"]
        nk = T + (PAST if kind == "s" else 0)
        nkt = nk // 128
        KaT = ar.alloc("KaT", [128, nk], BF16)
        Va = ar.alloc("Va", [128, nkt, 2, 65], BF16)
        ckvT = ar.alloc("ckvT", [128, nk], BF16)
        KbT = ar.alloc("KbT", [96, nk], BF16)
        Vb = ar.alloc("Vb", [128, nkt, 8, 65], BF16)
        ctx.update(KaT=KaT, Va=Va, ckvT=ckvT, KbT=KbT, Vb=Vb, nk=nk, nkt=nkt)
        m = ar.mark()
        wkv = ar.alloc("wkv", [128, 8, 416], BF16)
        kn = [ar.alloc(f"kn{i}", [128, 128], F32) for i in range(2)]
        vf = [ar.alloc(f"vf{i}", [128, 128], F32) for i in range(2)]
        cn = [ar.alloc(f"cn{i}", [128, 128], F32) for i in range(2)]
        krf = [ar.alloc(f"krf{i}", [128, 32], F32) for i in range(2)]
        junk = ar.alloc("kvjunk", [128, 128], F32)
        ss = ar.alloc("kvss", [128, 4], F32)
        tm = ar.alloc("kvtm", [128, 4], F32)
        rs = ar.alloc("kvrs", [128, 4], F32)
        rp = [ar.alloc(f"kvrp{i}", [128, 192], F32) for i in range(2)]
        rtmp = ar.alloc("kvrtmp", [128, 128], F32)
        self.memset("pool", ss[:], 0.0)
        self.memset("pool", Va[:, :, :, 64:65], 1.0)
        self.memset("pool", Vb[:, :, :, 64:65], 1.0)
        self.dma(wkv[:, :, 0:256], self.wsc[l][:, :, O_AK:O_AK + 256])
        self.dma(wkv[:, :, 256:416], self.wsc[l][:, :, O_CKV:O_CKV + 160])
        qk = self.qkn_t
        for st in range(nkt):
            b = st % 2
            new = st < T // 128
            pA, pB, pT, pV = self.ps[0 + 4 * b], self.ps[1 + 4 * b], self.ps[2 + 4 * b], self.ps[3 + 4 * b]
            if new:
                hs = hT[:, :, 1 + st * 128: 1 + (st + 1) * 128]
                for kc in range(8):
                    self.mm(pA[:, 0:256], hs[:, kc, :], wkv[:, kc, 0:256], start=(kc == 0), stop=(kc == 7))
                for kc in range(8):
                    self.mm(pB[:, 0:160], hs[:, kc, :], wkv[:, kc, 256:416], start=(kc == 0), stop=(kc == 7))
                for h in range(2):
                    self.act(junk[:, 0:64], pA[:, h * 64:(h + 1) * 64], AF.Square, accum=ss[:, h:h + 1])
                self.rstd(rs[:, 0:2], ss[:, 0:2], 64, NORM_EPS, tm[:, 0:2])
                self.tt("dve", kn[b][:].rearrange("p (a c) -> p a c", a=2), pA[:, 0:128].rearrange("p (a c) -> p a c", a=2),
                        rs[:, 0:2].unsqueeze(2).broadcast_to([128, 2, 64]), ALU.mult)
                self.tt("pool", kn[b][:].rearrange("p (a c) -> p a c", a=2), kn[b][:].rearrange("p (a c) -> p a c", a=2),
                        qk[:, 64:128].unsqueeze(1).broadcast_to([128, 2, 64]), ALU.mult)
                self.cp("act", vf[b][:], pA[:, 128:256])
                self.act(junk[:], pB[:, 0:128], AF.Square, accum=ss[:, 2:3])
                self.rstd(rs[:, 2:3], ss[:, 2:3], 128, NORM_EPS, tm[:, 2:3])
                self.ts("dve", cn[b][:], pB[:, 0:128], rs[:, 2:3], None, ALU.mult)
                self.tt("pool", cn[b][:], cn[b][:], qk[:, 128:256], ALU.mult)
                self.cp("act", krf[b][:], pB[:, 128:160])
                if kind == "p":
                    self.dma(self.new_a_k[si, l, st * 128:(st + 1) * 128, :], kn[b][:], wk=[("nak", si, l, st)])
                    self.dma(self.new_a_v[si, l, st * 128:(st + 1) * 128, :], vf[b][:], wk=[("nav", si, l, st)])
                    self.dma(self.new_ckv[si, l, st * 128:(st + 1) * 128, :], cn[b][:], wk=[("nck", si, l, st)])
                    self.dma(self.new_kr[si, l, st * 128:(st + 1) * 128, :], krf[b][:], wk=[("nkr", si, l, st)])
                else:
                    self.dma(rp[b][:], self.c_rope[st * 128:(st + 1) * 128, :])
                    self.rope(kn[b][:].rearrange("p (h c) -> p h c", h=2), rp[b][:, 0:64], rp[b][:, 64:128], 2, 16, rtmp)
                    self.rope(krf[b][:].rearrange("p (h c) -> p h c", h=1), rp[b][:, 128:160], rp[b][:, 160:192], 1, 8, rtmp)
            else:
                c0 = (st - T // 128) * 128
                self.dma(kn[b][:], self.cache_a_k[l, c0:c0 + 128, :])
                self.dma(vf[b][:], self.cache_a_v[l, c0:c0 + 128, :])
                self.dma(cn[b][:], self.cache_ckv[l, c0:c0 + 128, :])
                self.dma(krf[b][:], self.cache_kr[l, c0:c0 + 128, :])
            ks = slice(st * 128, (st + 1) * 128)
            self.tr(pT[:, 0:128], kn[b][:])
            self.cp("act", KaT[:, ks], pT[:, 0:128])
            self.tr(pT[:, 128:256], cn[b][:])
            self.cp("dve", ckvT[:, ks], pT[:, 128:256])
            self.mm(pT[64:96, 256:384], krf[b][:], self.ident[:], start=True, stop=True)
            self.cp("act", KbT[64:96, ks], pT[64:96, 256:384])
            self.cp("pool", Va[:, st, :, 0:64], vf[b][:].rearrange("p (a c) -> p a c", a=2))
            self.mm(pV[:, :], ckvT[:, ks], self.uv[:], start=True, stop=True)
            self.cp("act" if st % 2 else "dve", Vb[:, st, :, 0:64], pV[:, :].rearrange("p (a c) -> p a c", a=8))
        ar.release(m)

    def rope(self, x3, cc, ss_, nh, half, tmp):
        w = 4 * half
        t3 = tmp[:, 0:nh * w].rearrange("p (h c) -> p h c", h=nh)
        for a in range(2):
            for hb in range(2):
                o = a * 2 * half + hb * half
                o2 = a * 2 * half + (1 - hb) * half
                self.tt("pool", t3[:, :, o:o + half], x3[:, :, o2:o2 + half],
                        ss_[:, o:o + half].unsqueeze(1).broadcast_to([128, nh, half]), ALU.mult)
        self.tt("dve", x3, x3, cc.unsqueeze(1).broadcast_to([128, nh, w]), ALU.mult)
        self.tt("dve", x3, x3, t3, ALU.add)

    def phase_rwkv(self, ctx):
        S, ar = self.S, self.arena
        l, kind, si, T, hT = ctx["l"], ctx["kind"], ctx["si"], ctx["T"], ctx["hT"]
        nb = T // 128
        m = ar.mark()
        W = {}
        a = lambda n, shp, dt=F32: W.__setitem__(n, ar.alloc(n, shp, dt))
        a("wc", [128, 8, 1792], BF16)
        a("wcz", [128, 8, 512], BF16)
        a("lnx", [128, 1024])
        for n in ("rT", "kT", "vT", "kq", "sq", "tmpf", "kk", "bT", "kdT", "asg", "u"):
            a(n, [128, 4, 128])
        for n in ("sg", "Ep", "Em", "E1", "ED", "ysb", "ybt", "vtok", "sz"):
            a(n, [128, 512])
        a("t1a", [128, 128]); a("t1b", [128, 128]); a("wdT", [128, 128]); a("adT", [128, 128])
        a("tw", [128, 128], BF16); a("adb", [128, 128], BF16)
        a("OPS", [128, 4, 4, 128], BF16)
        a("Bt", [128, 512], BF16); a("Kt", [128, 512], BF16); a("Vt", [128, 512], BF16)
        a("AT", [128, 8, 512], BF16); a("ATr", [128, 512], BF16)
        a("Tm", [128, 8, 128], BF16); a("Zm", [128, 8, 128], BF16); a("Qm", [128, 8, 128], BF16)
        a("S", [128, 4, 64]); a("S0s", [128, 4, 64], BF16); a("emwl", [128, 4, 2])
        a("Xb", [128, 512], BF16); a("Ub", [128, 512], BF16)
        a("st8", [128, 6, 8]); a("sti", [64, 4, 128])
        for c0 in range(0, 1792, 448):
            self.dma(W["wc"][:, :, c0:c0 + 448], self.wsc[l][:, :, O_CIN + c0:O_CIN + c0 + 448])
        self.dma(W["wcz"][:], self.wsc[l][:, :, O_CZ:O_CZ + 512])
        self.dma(W["lnx"][:], self.lnx[l].partition_broadcast(128))
        for d in (1, 0):
            Sst = W["S"]
            if kind == "p":
                self.memset("pool", Sst[:], 0.0)
            else:
                src = (self.state_f if d == 0 else self.state_b)[l]
                self.dma(W["sti"][:].rearrange("i p (a j) -> i p a j", a=2), src.rearrange("(p a) i j -> i p a j", a=2))
                for p in range(4):
                    self.tr(self.ps[0][:, p * 64:(p + 1) * 64], W["sti"][:, p, :])
                self.cp("dve", Sst[:].rearrange("q p i -> q (p i)"), self.ps[0][:, 0:256])
            order = range(nb - 1, -1, -1) if d == 1 else range(nb)
            for bi in order:
                self.rwkv_block(ctx, d, bi, W)
            if kind == "p":
                dst = (self.new_sf if d == 0 else self.new_sb)[si, l]
                for p in range(4):
                    self.tr(self.ps[0][0:64, p * 128:(p + 1) * 128], Sst[:, p, :])
                self.cp("dve", W["sti"][:].rearrange("i p c -> i (p c)"), self.ps[0][0:64, :])
                self.dma(dst.rearrange("(p a) i j -> i p a j", a=2), W["sti"][:].rearrange("i p (a j) -> i p a j", a=2),
                         wk=[("nst", d, si, l)])
            S.barrier()
            S.emit_block()
        ar.release(m)

    def rwkv_block(self, ctx, d, bi, W):
        l, kind, T, hT = ctx["l"], ctx["kind"], ctx["T"], ctx["hT"]
        ps = self.ps
        t0 = bi * 128
        lfm = self.lfm
        bc3 = lambda ap, n: ap.unsqueeze(2).broadcast_to([128, ap.shape[1], n])
        v3 = lambda t: t[:].rearrange("q (p c) -> q p c", p=4)
        rT, kT, vT = W["rT"], W["kT"], W["vT"]
        if d == 0:
            self.dma(W["ybt"][:], self.yb_sc[t0:t0 + 128, :])
        for ci in range(14):
            pb = ps[ci % 2]
            for kc in range(8):
                self.mm(pb[:, 0:130], W["wc"][:, kc, ci * 128:(ci + 1) * 128], hT[:, kc, t0:t0 + 130],
                        start=(kc == 0), stop=(kc == 7))
            if ci < 12:
                dst = (rT, kT, vT)[ci // 4][:, ci % 4, :]
            else:
                dst = (W["wdT"], W["adT"])[ci - 12][:]
            t1 = W["t1a" if ci % 2 == 0 else "t1b"]
            self.act(t1[:], pb[:, 1:129], AF.Identity, scale=self.c0fm[:, ci:ci + 1])
            self.stt(t1[:], pb[:, 0:128], lfm[:, 12 + ci:13 + ci], t1[:], ALU.mult, ALU.add)
            self.stt(dst, pb[:, 2:130], lfm[:, 26 + ci:27 + ci], t1[:], ALU.mult, ALU.add)
        self.act(W["tw"][:], W["wdT"][:], AF.Tanh)
        self.cp("pool", W["adb"][:], W["adT"][:])
        hd = slice(d * 64, (d + 1) * 64)
        self.mm(ps[2][:, :], W["tw"][hd, :], self.wup[hd, :], start=True, stop=False)
        self.mm(ps[2][:, :], self.onesb[0:1, 0:128], self.w0row[0:1, d, :], start=False, stop=True)
        self.act(W["sg"][:], ps[2][:, :], AF.Sigmoid)
        for p in range(4):
            self.mm(ps[3][:, p * 128:(p + 1) * 128], self.aup[hd, p * 128:(p + 1) * 128], W["adb"][hd, :])
        for p in range(4):
            self.act(W["asg"][:, p, :], ps[3][:, p * 128:(p + 1) * 128], AF.Sigmoid, bias=lfm[:, 40 + d * 4 + p:41 + d * 4 + p])
        self.tt("pool", W["kq"][:], kT[:], bc3(lfm[:, 0:4], 128), ALU.mult)
        self.tt("pool", W["sq"][:], W["kq"][:], W["kq"][:], ALU.mult)
        self.mm(ps[4][:, :], self.bones[:], W["sq"][:].rearrange("q p c -> q (p c)"))
        self.act(W["tmpf"][:].rearrange("q p c -> q (p c)"), ps[4][:, :], AF.Sqrt)
        self.ts("dve", W["tmpf"][:], W["tmpf"][:], 1e-12, None, ALU.max)
        self.recip(W["tmpf"][:], W["tmpf"][:])
        self.tt("pool", W["kk"][:], W["kq"][:], W["tmpf"][:], ALU.mult)
        self.tt("pool", W["bT"][:], W["kk"][:], W["asg"][:], ALU.mult)
        self.tt("pool", W["u"][:], W["asg"][:], bc3(lfm[:, 4:8], 128), ALU.mult)
        self.tt("pool", W["u"][:], W["u"][:], bc3(self.omka[:, 0:4], 128), ALU.add)
        self.tt("pool", W["kdT"][:], kT[:], W["u"][:], ALU.mult)
        tri = self.tri
        for p in range(4):
            self.mm(ps[5][:, p * 128:(p + 1) * 128], W["sg"][:, p * 128:(p + 1) * 128], tri[:, d, 0:128])
        for p in range(4):
            self.mm(ps[6][:, p * 128:(p + 1) * 128], W["sg"][:, p * 128:(p + 1) * 128], tri[:, d, 128:256])
        self.mm(ps[7][:, :], tri[:, d, 256:384], W["sg"][:])
        for p in range(4):
            self.mm(ps[2][:, p * 2:p * 2 + 2], W["sg"][:, p * 128:(p + 1) * 128], tri[:, d, 384:386])
        self.act(W["emwl"][:].rearrange("q p c -> q (p c)"), ps[2][:, 0:8], AF.Exp)
        self.act(W["Ep"][:], ps[5][:, :], AF.Exp)
        self.act(W["Em"][:], ps[5][:, :], AF.Exp, scale=-1.0)
        self.act(W["E1"][:], ps[6][:, :], AF.Exp)
        self.act(W["ED"][:], ps[7][:, :], AF.Exp)
        OPS = W["OPS"]
        self.stt(OPS[:, :, 0, :], W["kk"][:], -1.0, v3(W["E1"]), ALU.mult, ALU.mult)
        self.tt("pool", OPS[:, :, 1, :], rT[:], v3(W["Ep"]), ALU.mult)
        self.tt("pool", OPS[:, :, 2, :], W["bT"][:], v3(W["Em"]), ALU.mult)
        self.tt("dve", OPS[:, :, 3, :], W["kdT"][:], v3(W["Em"]), ALU.mult)
        for p in range(4):
            self.tr(ps[5][:, p * 128:(p + 1) * 128], W["bT"][:, p, :])
        self.tt("dve", W["Bt"][:], ps[5][:, :], W["ED"][:], ALU.mult)
        for p in range(4):
            self.tr(ps[6][:, p * 128:(p + 1) * 128], W["kdT"][:, p, :])
        self.tt("dve", W["Kt"][:], ps[6][:, :], W["ED"][:], ALU.mult)
        for p in range(4):
            self.tr(ps[7][:, p * 128:(p + 1) * 128], vT[:, p, :])
        self.cp("act", W["Vt"][:], ps[7][:, :])
        if d == 0:
            self.cp("act", W["vtok"][:], ps[7][:, :])
        AT = W["AT"]
        for h in range(8):
            p, h2 = h // 2, h % 2
            hp = slice(h2 * 64, h2 * 64 + 64)
            pa = ps[h % 2]
            self.mm(pa[:, 0:256], OPS[hp, p, 2, :], OPS[hp, p, 0:2, :])
            self.mm(pa[:, 256:512], OPS[hp, p, 3, :], OPS[hp, p, 0:2, :])
            if h % 2 == 0:
                self.tt("dve", AT[:, h, :], pa[:, :], self.bank[:, d, :], ALU.mult)
            else:
                self.cp("act", W["ATr"][:], pa[:, :])
                self.tt("pool", AT[:, h, :], W["ATr"][:], self.bank[:, d, :], ALU.mult)
        Tm, Zm, Qm = W["Tm"], W["Zm"], W["Qm"]
        idb = self.identb[:].unsqueeze(1).broadcast_to([128, 8, 128])
        self.cp("pool", Tm[:], idb)
        self.cp("pool", Zm[:], idb)
        for lev in range(7):
            for g in range(2):
                pq, pt, pz = ps[2 + 3 * g], ps[3 + 3 * g], ps[4 + 3 * g]
                hs = range(4 * g, 4 * g + 4)
                for c, h in enumerate(hs):
                    self.mm(pq[:, c * 128:(c + 1) * 128], AT[:, h, 0:128], Tm[:, h, :])
                self.tt("dve", Qm[:, 4 * g:4 * g + 4, :], pq[:, :].rearrange("q (c s) -> q c s", c=4),
                        self.lvl[:, d * 7 + lev, :].unsqueeze(1).broadcast_to([128, 4, 128]), ALU.mult)
                if lev < 6:
                    for c, h in enumerate(hs):
                        cs = slice(c * 128, (c + 1) * 128)
                        self.mm(pt[:, cs], self.identb[:], Tm[:, h, :], start=True, stop=False)
                        self.mm(pt[:, cs], Zm[:, h, :], Qm[:, h, :], start=False, stop=True)
                for c, h in enumerate(hs):
                    cs = slice(c * 128, (c + 1) * 128)
                    self.mm(pz[:, cs], self.identb[:], Zm[:, h, :], start=True, stop=False)
                    self.mm(pz[:, cs], Qm[:, h, :], Zm[:, h, :], start=False, stop=True)
                if lev < 6:
                    self.cp("act", Tm[:, 4 * g:4 * g + 4, :], pt[:, :].rearrange("q (c s) -> q c s", c=4))
                self.cp("act", Zm[:, 4 * g:4 * g + 4, :], pz[:, :].rearrange("q (c s) -> q c s", c=4))
        Sst, S0s, emwl = W["S"], W["S0s"], W["emwl"]
        Vt, Xb, Ub = W["Vt"], W["Xb"], W["Ub"]
        self.tt("dve", S0s[:], Sst[:], emwl[:, :, 0:1].broadcast_to([128, 4, 64]), ALU.mult)
        for h in range(8):
            p, h2 = h // 2, h % 2
            hp = slice(h2 * 64, h2 * 64 + 64)
            hc = slice(h * 64, (h + 1) * 64)
            self.mm(ps[0][:, hc], OPS[hp, p, 0, :], S0s[hp, p, :], start=True, stop=False)
            self.mm(ps[0][:, hc], AT[:, h, 256:384], Vt[:, hc], start=False, stop=True)
        self.cp("act", Xb[:], ps[0][:, :])
        for h in range(8):
            hc = slice(h * 64, (h + 1) * 64)
            self.mm(ps[1][:, hc], Zm[:, h, :], Xb[:, hc])
        self.cp("act", Ub[:], ps[1][:, :])
        for h in range(8):
            p, h2 = h // 2, h % 2
            hp = slice(h2 * 64, h2 * 64 + 64)
            hc = slice(h * 64, (h + 1) * 64)
            self.mm(ps[2][:, hc], OPS[hp, p, 1, :], S0s[hp, p, :], start=True, stop=False)
            self.mm(ps[2][:, hc], AT[:, h, 128:256], Ub[:, hc], start=False, stop=False)
            self.mm(ps[2][:, hc], AT[:, h, 384:512], Vt[:, hc], start=False, stop=True)
        for h in range(8):
            p, h2 = h // 2, h % 2
            hp = slice(h2 * 64, h2 * 64 + 64)
            hc = slice(h * 64, (h + 1) * 64)
            self.mm(ps[3][hp, p * 64:(p + 1) * 64], W["Bt"][:, hc], Ub[:, hc], start=True, stop=False)
            self.mm(ps[3][hp, p * 64:(p + 1) * 64], W["Kt"][:, hc], Vt[:, hc], start=False, stop=True)
        self.tt("dve", Sst[:], Sst[:], emwl[:, :, 1:2].broadcast_to([128, 4, 64]), ALU.mult)
        self.tt("dve", Sst[:], Sst[:], ps[3][:, 0:256].rearrange("q (p i) -> q p i", p=4), ALU.add)
        prod = W["kq"]
        self.tt("pool", prod[:], rT[:], W["kdT"][:], ALU.mult)
        self.tt("pool", prod[:], prod[:], bc3(lfm[:, 8:12], 128), ALU.mult)
        for p in range(4):
            self.mm(ps[4][:, p * 2:p * 2 + 2], prod[:, p, :], self.hsel[:, :])
        st8 = W["st8"]
        if d == 1:
            self.cp("dve", self.bonb[:, bi, :], ps[4][:, 0:8])
            self.cp("act", W["ysb"][:], ps[2][:, :])
            self.dma(self.yb_sc[t0:t0 + 128, :], W["ysb"][:])
            return
        ysb = W["ysb"]
        b8 = lambda ap: ap.unsqueeze(2).broadcast_to([128, 8, 64])
        y3 = lambda t: t[:].rearrange("q (h i) -> q h i", h=8)
        self.tt("dve", st8[:, 5, :], ps[4][:, 0:8], self.bonb[:, bi, :], ALU.add)
        self.tt("dve", ysb[:], ps[2][:, :], W["ybt"][:], ALU.add)
        for kc in range(8):
            self.mm(ps[5][:, :], hT[:, kc, 1 + t0:1 + t0 + 128], W["wcz"][:, kc, :], start=(kc == 0), stop=(kc == 7))
        self.act(W["sz"][:], ps[5][:, :], AF.Silu)
        sqy = W["sq"][:].rearrange("q p c -> q (p c)")
        yn = W["u"][:].rearrange("q p c -> q (p c)")
        yn3 = W["u"][:].rearrange("q p (a i) -> q (p a) i", a=2)
        self.red(st8[:, 0, :], y3(ysb))
        self.tt("pool", sqy, ysb[:], ysb[:], ALU.mult)
        self.red(st8[:, 1, :], W["sq"][:].rearrange("q p (a i) -> q (p a) i", a=2))
        self.ts("dve", st8[:, 0, :], st8[:, 0, :], 1.0 / 64, None, ALU.mult)
        self.tt("dve", st8[:, 2, :], st8[:, 0, :], st8[:, 0, :], ALU.mult)
        self.stt(st8[:, 3, :], st8[:, 1, :], 1.0 / 64, st8[:, 2, :], ALU.mult, ALU.subtract)
        self.act(st8[:, 4, :], st8[:, 3, :], AF.Sqrt, bias=float(GN_EPS))
        self.recip(st8[:, 4, :], st8[:, 4, :])
        self.tt("pool", yn3, y3(ysb), b8(st8[:, 0, :]), ALU.subtract)
        self.tt("pool", yn3, yn3, b8(st8[:, 4, :]), ALU.mult)
        self.tt("pool", yn, yn, W["lnx"][:, 0:512], ALU.mult)
        self.tt("pool", yn, yn, W["lnx"][:, 512:1024], ALU.add)
        self.tt("dve", y3(ysb), y3(W["vtok"]), b8(st8[:, 5, :]), ALU.mult)
        self.tt("pool", yn, yn, ysb[:], ALU.add)
        self.tt("dve", ysb[:], yn, W["sz"][:], ALU.mult)
        self.dma(self.yc_sc[t0:t0 + 128, :], ysb[:])

    def phase_Q(self, ctx, lo, hi):
        S = self.S
        l, kind, si, T, cond, last = ctx["l"], ctx["kind"], ctx["si"], ctx["T"], ctx["cond"], ctx["last"]
        KaT, Va, ckvT, KbT, Vb, nk, nkt = (ctx[k] for k in ("KaT", "Va", "ckvT", "KbT", "Vb", "nk", "nkt"))
        ps = self.ps
        TQ = min(T, 512)
        nst = TQ // 128
        ntile = T // TQ
        xin, xout = ctx["xin"], ctx["xout"]
        yout = (self.y_prompt[si] if kind == "p" else self.y_sample)

        def qa(name, shape, dt=F32):
            if lo.can(shape, dt):
                return lo.alloc(name, shape, dt)
            return hi.alloc(name, shape, dt)

        hTq = qa("hTq", [128, 8, TQ], BF16)
        qaT = qa("qaT", [128, 4, TQ], BF16)
        qbT = qa("qbT", [96, 8, TQ], BF16)
        yaT = qa("yaT", [64, 8, TQ], BF16)
        ybT = qa("ybT", [64, 8, TQ], BF16)
        ycT = qa("ycT", [128, 4, TQ], BF16)
        mlo, mhi = lo.mark(), hi.mark()
        for ti in range(ntile):
            tok0 = ti * TQ
            x_ap = xin[tok0:tok0 + TQ, :]
            xt = [qa(f"qxt{i}", [128, D]) for i in range(2)]
            xn = [qa(f"qxn{i}", [128, D]) for i in range(2)]
            ss2 = qa("qss", [128, 2]); tm2 = qa("qtm", [128, 2]); rs2 = qa("qrs", [128, 2])
            waq = qa("waq", [128, 8, 512], BF16)
            wbq = qa("wbq", [128, 8, 768], BF16)
            qn = qa("qn", [128, 512]); qsq = qa("qsq", [128, 512]); qb = qa("qbf", [128, 768])
            yct = [qa(f"yct{i}", [128, 512]) for i in range(2)]
            rp = [qa(f"qrp{i}", [128, 192]) for i in range(2)]
            rtmp = qa("qrtmp", [128, 512])
            s8 = qa("qs8", [128, 3, 8])
            self.dma(waq[:], self.wsc[l][:, :, O_AQ:O_AQ + 512])
            self.dma(wbq[:], self.wsc[l][:, :, O_BQ:O_BQ + 768])
            self.compute_hT(ctx, hTq, 0, x_ap, nst, (xt, xn, ss2, tm2, rs2))
            for st in range(nst):
                sc = slice(st * 128, (st + 1) * 128)
                b = st % 2
                if kind == "s":
                    self.dma(rp[b][:], self.c_rope[tok0 + st * 128: tok0 + (st + 1) * 128, :])
                for kc in range(8):
                    self.mm(ps[0][:, :], hTq[:, kc, sc], waq[:, kc, :], start=(kc == 0), stop=(kc == 7))
                self.act(qsq[:], ps[0][:, :], AF.Square)
                self.red(s8[:, 0, :], qsq[:].rearrange("p (h d) -> p h d", h=8))
                self.rstd(s8[:, 1, :], s8[:, 0, :], 64, NORM_EPS, s8[:, 2, :])
                self.tt("dve", qn[:].rearrange("p (g kv d) -> p kv g d", g=4, kv=2),
                        ps[0][:, :].rearrange("p (kv g d) -> p kv g d", kv=2, g=4),
                        s8[:, 1, :].rearrange("p (kv g) -> p kv g", kv=2).unsqueeze(3).broadcast_to([128, 2, 4, 64]), ALU.mult)
                qn3 = qn[:].rearrange("p (h d) -> p h d", h=8)
                self.tt("pool", qn3, qn3, self.qkn_t[:, 0:64].unsqueeze(1).broadcast_to([128, 8, 64]), ALU.mult)
                if kind == "s":
                    self.rope(qn3, rp[b][:, 0:64], rp[b][:, 64:128], 8, 16, rtmp)
                for g in range(4):
                    self.tr(ps[1][:, g * 128:(g + 1) * 128], qn[:, g * 128:(g + 1) * 128])
                self.cp("act", qaT[:, :, sc], ps[1][:, :].rearrange("q (g t) -> q g t", g=4))
                for half in range(2):
                    for kc in range(8):
                        self.mm(ps[2 + half][:, 0:384], hTq[:, kc, sc], wbq[:, kc, half * 384:(half + 1) * 384],
                                start=(kc == 0), stop=(kc == 7))
                self.cp("act", qb[:, 0:384], ps[2][:, 0:384])
                self.cp("dve", qb[:, 384:768], ps[3][:, 0:384])
                qb3 = qb[:].rearrange("p (h c) -> p h c", h=8)
                if kind == "s":
                    self.rope(qb3[:, :, 64:96], rp[b][:, 128:160], rp[b][:, 160:192], 8, 8, rtmp)
                for h in range(8):
                    self.tr(ps[4 + h // 4][0:96, (h % 4) * 128:(h % 4 + 1) * 128], qb[:, h * 96:(h + 1) * 96])
                self.cp("act", qbT[:, 0:4, sc], ps[4][0:96, :].rearrange("q (g t) -> q g t", g=4))
                self.cp("dve", qbT[:, 4:8, sc], ps[5][0:96, :].rearrange("q (g t) -> q g t", g=4))
                self.dma(yct[b][:], self.yc_sc[tok0 + st * 128: tok0 + (st + 1) * 128, :])
                for p in range(4):
                    self.tr(ps[6][:, p * 128:(p + 1) * 128], yct[b][:, p * 128:(p + 1) * 128])
                self.cp("dve", ycT[:, :, sc], ps[6][:, :].rearrange("q (g t) -> q g t", g=4))
            S.barrier()
            lo.release(mlo); hi.release(mhi)
            PT = [qa(f"PT{i}", [128, TQ], BF16) for i in range(3)]
            den = qa("den", [65, TQ]); rden = qa("rden", [64, TQ]); szq = qa("szq", [64, TQ]); gq = qa("gq", [64, TQ])
            wz = [qa(f"wz{i}", [128, 8, 64], BF16) for i in range(2)]

            def finalize(h, po, zoff, yT):
                self.cp("act", den[64:65, :], po[64:65, 0:TQ])
                self.mm(ps[6][0:64, 0:TQ], self.ones[64:65, 0:64], den[64:65, :])
                self.recip(rden[:, :], ps[6][0:64, 0:TQ])
                for kc in range(8):
                    self.mm(ps[7][0:64, 0:TQ], wz[h % 2][:, kc, :], hTq[:, kc, :], start=(kc == 0), stop=(kc == 7))
                self.act(szq[:, :], ps[7][0:64, 0:TQ], AF.Silu)
                self.tt("pool", gq[:, :], rden[:, :], szq[:, :], ALU.mult)
                self.tt("dve", yT[:, h, :], po[0:64, 0:TQ], gq[:, :], ALU.mult)

            nS = 0
            for h in range(8):
                kv, g = h // 4, h % 4
                hp = slice(kv * 64, kv * 64 + 64)
                self.dma(wz[h % 2][:], self.wsc[l][:, :, O_AZ + h * 64:O_AZ + (h + 1) * 64])
                po = ps[3 + h % 2]
                for kt in range(nkt):
                    pS, Pt = ps[nS % 3], PT[nS % 3]
                    nS += 1
                    self.mm(pS[:, 0:TQ], KaT[hp, kt * 128:(kt + 1) * 128], qaT[hp, g, :])
                    self.act(Pt[:, :], pS[:, 0:TQ], AF.Exp, scale=0.125)
                    self.mm(po[0:65, 0:TQ], Va[:, kt, kv, :], Pt[:, :], start=(kt == 0), stop=(kt == nkt - 1))
                finalize(h, po, O_AZ, yaT)
            bscale = float(96 ** -0.5)
            for h in range(8):
                self.dma(wz[h % 2][:], self.wsc[l][:, :, O_BZ + h * 64:O_BZ + (h + 1) * 64])
                nkb = 0
                for kb in range(0, nk, 512):
                    w = min(512, nk - kb)
                    self.mm(ps[5][0:64, 0:w], self.uk[:, h * 64:(h + 1) * 64], ckvT[:, kb:kb + w])
                    self.cp("dve" if nkb % 2 else "act", KbT[0:64, kb:kb + w], ps[5][0:64, 0:w])
                    nkb += 1
                po = ps[3 + h % 2]
                for kt in range(nkt):
                    pS, Pt = ps[nS % 3], PT[nS % 3]
                    nS += 1
                    self.mm(pS[:, 0:TQ], KbT[0:96, kt * 128:(kt + 1) * 128], qbT[0:96, h, :])
                    self.act(Pt[:, :], pS[:, 0:TQ], AF.Exp, scale=bscale)
                    self.mm(po[0:65, 0:TQ], Vb[:, kt, h, :], Pt[:, :], start=(kt == 0), stop=(kt == nkt - 1))
                finalize(h, po, O_BZ, ybT)
            S.barrier()
            lo.release(mlo); hi.release(mhi)
            woa = [qa(f"woa{i}", [64, 8, 128], BF16) for i in range(2)]
            wob = [qa(f"wob{i}", [64, 8, 128], BF16) for i in range(2)]
            woc = [qa(f"woc{i}", [128, 4, 128], BF16) for i in range(2)]
            wg = [qa(f"wg{i}", [128, 8, 3, 128], BF16) for i in range(2)]
            wo_ = [qa(f"wout{i}", [128, 8, 128], BF16) for i in range(2)]
            sgm = [qa(f"sgm{i}", [128, TQ]) for i in range(2)]
            mix = qa("mix", [128, TQ]); mtmp = qa("mtmp", [128, TQ])
            mixT = qa("mixT", [128, 8, TQ], BF16)
            oT = [qa(f"oT{i}", [128, TQ]) for i in range(2)]
            xres = qa("xres", [128, nst, D])
            for st in range(nst):
                self.dma(xres[:, st, :], x_ap[st * 128:(st + 1) * 128, :])
            for dc in range(8):
                b = dc % 2
                dcs = slice(dc * 128, (dc + 1) * 128)
                self.dma(woa[b][:], self.wosc[l][0][:, :, dcs])
                self.dma(wob[b][:], self.wosc[l][1][:, :, dcs])
                self.dma(woc[b][:], self.wosc[l][2][:, :, dcs])
                for br in range(3):
                    self.dma(wg[b][:, :, br, :], self.wsc[l][:, :, O_G + br * 1024 + dc * 128:O_G + br * 1024 + (dc + 1) * 128])
                for br in range(3):
                    pp, pg = ps[br], ps[3 + br]
                    if br < 2:
                        wo, yT = (woa, wob)[br][b], (yaT, ybT)[br]
                        for h in range(8):
                            self.mm(pp[:, 0:TQ], wo[:, h, :], yT[:, h, :], start=(h == 0), stop=(h == 7))
                    else:
                        for p in range(4):
                            self.mm(pp[:, 0:TQ], woc[b][:, p, :], ycT[:, p, :], start=(p == 0), stop=(p == 3))
                    for kc in range(8):
                        self.mm(pg[:, 0:TQ], wg[b][:, kc, br, :], hTq[:, kc, :], start=(kc == 0), stop=(kc == 7))
                    sg_ = sgm[br % 2]
                    self.act(sg_[:, :], pg[:, 0:TQ], AF.Sigmoid)
                    if br == 0:
                        self.tt("dve", mix[:, :], pp[:, 0:TQ], sg_[:, :], ALU.mult)
                    else:
                        self.tt("dve", mtmp[:, :], pp[:, 0:TQ], sg_[:, :], ALU.mult)
                        dst = mix[:, :] if br == 1 else mixT[:, dc, :]
                        self.tt("pool", dst, mix[:, :], mtmp[:, :], ALU.add)
            for dc in range(8):
                b = dc % 2
                dcs = slice(dc * 128, (dc + 1) * 128)
                self.dma(wo_[b][:], self.woutsc[l][:, :, dcs])
                po = ps[6 + b]
                for kc in range(8):
                    self.mm(po[:, 0:TQ], wo_[b][:, kc, :], mixT[:, kc, :], start=(kc == 0), stop=(kc == 7))
                gsc = self.modG[:, l, cond, dc:dc + 1]
                if b == 0:
                    self.act(oT[b][:, :], po[:, 0:TQ], AF.Identity, scale=gsc)
                else:
                    self.ts("dve", oT[b][:, :], po[:, 0:TQ], gsc, None, ALU.mult)
                pt = ps[dc % 4]
                for st in range(nst):
                    self.tr(pt[:, st * 128:(st + 1) * 128], oT[b][:, st * 128:(st + 1) * 128])
                self.tt("dve", xres[:, :, dcs], xres[:, :, dcs],
                        pt[:, 0:nst * 128].rearrange("q (s f) -> q s f", s=nst), ALU.add)
            for st in range(nst):
                rows = slice(tok0 + st * 128, tok0 + (st + 1) * 128)
                if not last:
                    self.dma(xout[rows, :], xres[:, st, :])
                else:
                    fs = qa(f"fss{st}", [128, 4])
                    jv = mixT[:, 0:1024 // TQ, :].rearrange("q a t -> q (a t)")
                    self.act(jv, xres[:, st, :], AF.Square, accum=fs[:, 0:1])
                    self.rstd(fs[:, 2:3], fs[:, 0:1], D, NORM_EPS, fs[:, 1:2])
                    self.ts("dve", xres[:, st, :], xres[:, st, :], fs[:, 2:3], None, ALU.mult)
                    self.tt("pool", xres[:, st, :], xres[:, st, :], self.fnw_t[:], ALU.mult)
                    self.dma(yout[rows, :], xres[:, st, :], wk=[("yout", kind, si, tok0, st)])
            S.barrier()
            lo.release(mlo); hi.release(mhi)


def _in_maps(inputs):
    c = host_constants()
    f = lambda a: np.ascontiguousarray(np.asarray(a, dtype=np.float32))
    g = {k: f(v) for k, v in inputs.items()}
    nb = np.zeros((128, 128), np.float32)
    for l in range(DEPTH):
        nb[l * 32:l * 32 + 8] = g["norm_w"][l].reshape(8, 128)
        nb[l * 32 + 8:l * 32 + 32] = g["b_mod"][l].reshape(24, 128)
    lfm = np.zeros((DEPTH, 48, 128), np.float32)
    for l in range(DEPTH):
        lfm[l, 0:4] = g["c_k_k"][l].reshape(4, 128)
        lfm[l, 4:8] = g["c_k_a"][l].reshape(4, 128)
        lfm[l, 8:12] = g["c_r_k"][l].reshape(4, 128)
        lfm[l, 12:26] = g["c_mu_prev"][l].reshape(14, 128)
        lfm[l, 26:40] = g["c_mu_next"][l].reshape(14, 128)
        lfm[l, 40:48] = g["c_a0"][l].reshape(8, 128)
    qkn = np.concatenate([g["a_qnorm_w"], g["a_knorm_w"], g["b_kvnorm_w"]], axis=1)
    lnx = np.concatenate([g["c_lnx_w"], g["c_lnx_b"]], axis=1)
    shared = {
        "nb_rows": nb, "w_mod": g["w_mod"], "w_in": g["w_in"], "qkn": f(qkn),
        "b_w_uk": g["b_w_uk"], "b_w_uv": g["b_w_uv"], "lfm_rows": lfm, "c_w0": g["c_w0"],
        "c_w_up": f(g["c_w_up"].reshape(DEPTH, 128, 512)), "c_a_up": f(g["c_a_up"].reshape(DEPTH, 128, 512)),
        "lnx": f(lnx), "w_oa": g["w_oa"], "w_ob": g["w_ob"], "w_oc": g["w_oc"], "w_out": g["w_out"],
        "final_norm_w": g["final_norm_w"],
        "c_ident": c["ident"], "c_bank": c["bankmask"], "c_lvl": c["lvlmask"], "c_tri": c["tri"],
        "c_hsel": c["headsel"], "c_rope": c["rope"],
    }
    maps = []
    for i in range(NCORES):
        m = dict(shared)
        m["x_prompt"] = f(g["x_prompt"][i * NPS:(i + 1) * NPS])
        m["x_sample"] = f(g["x_sample"][i])
        m["cache_a_k"] = f(g["cache_a_k"][i].reshape(DEPTH, PAST, 128))
        m["cache_a_v"] = f(g["cache_a_v"][i].reshape(DEPTH, PAST, 128))
        m["cache_b_ckv"] = f(g["cache_b_ckv"][i])
        m["cache_b_krope"] = f(g["cache_b_krope"][i])
        m["state_c_fwd"] = f(g["state_c_fwd"][i])
        m["state_c_bwd"] = f(g["state_c_bwd"][i])
        m["cond"] = f(np.concatenate([g["c_ctx"].reshape(8, 128), g["c"][i].reshape(8, 128)], axis=0))
        maps.append(m)
    return maps


_NC_CACHE = {}


def kernel(**inputs):
    stage = int(os.environ.get("MK_STAGE", "99"))
    if stage not in _NC_CACHE:
        _NC_CACHE[stage] = Builder(stage).build()
    nc = _NC_CACHE[stage]
    maps = _in_maps(inputs)
    res = run_bass_kernel_spmd(nc, maps, core_ids=list(range(NCORES)))
    r = res.results
    cat = lambda name: np.concatenate([np.asarray(x[name]) for x in r], axis=0)
    y_prompt = cat("y_prompt")
    y_sample = np.stack([np.asarray(x["y_sample"]) for x in r], axis=0)
    new_a_k = cat("new_a_k").reshape(32, DEPTH, TP, 2, 64)
    new_a_v = cat("new_a_v").reshape(32, DEPTH, TP, 2, 64)
    new_ckv = cat("new_b_ckv")
    new_kr = cat("new_b_krope")
    new_sf = cat("new_c_state_fwd")
    new_sb = cat("new_c_state_bwd")
    return (y_prompt, y_sample, new_a_k, new_a_v, new_ckv, new_kr, new_sf, new_sb)
```

```python
import os
import contextlib
import numpy as np
import concourse.bass as bass
import concourse.mybir as mybir
from concourse.bass_utils import run_bass_kernel_spmd

F32 = mybir.dt.float32
BF16 = mybir.dt.bfloat16
AF = mybir.ActivationFunctionType
ALU = mybir.AluOpType
AX = mybir.AxisListType

D = 1024
DEPTH = 4
NPS = 4
TP = 256
TS = 4096
PAST = 512
IN_DIM = 8096
NCORES = 8
O_AQ, O_AK, O_AV, O_AZ, O_BQ, O_CKV, O_KR, O_BZ, O_CIN, O_CZ, O_G = 0, 512, 640, 768, 1280, 2048, 2176, 2208, 2720, 4512, 5024
NORM_EPS = 1e-6
GN_EPS = 64e-5
DEC_C = -float(np.exp(-0.5))

EPOCH = 30000
N_EPOCH = {"pe": 16, "dve": 12, "act": 10, "pool": 8}
N_DMA_SEMS = 40
SB_BASE = 16512
SB_LIMIT = 229344
CUT = int(os.environ.get('MK_CUT', '0'))


class Sched:
    def __init__(self, nc):
        self.nc = nc
        self.ops = {e: [] for e in ("pe", "dve", "act", "pool", "sp")}
        self.count = {e: 0 for e in ("pe", "dve", "act", "pool")}
        self.last_w = {}
        self.readers = {}
        self.dma_rr = 0
        self.dma_tot = [0] * N_DMA_SEMS
        self.sems = {}

    def _deps(self, eng, reads, writes):
        d = {}

        def add(tok, raw):
            sk, v, src = tok
            if src == eng and sk[0] != "dma":
                if eng == "pe":
                    return
            if d.get(sk, 0) < v:
                d[sk] = v

        for k in reads:
            w = self.last_w.get(k)
            if w is not None:
                add(w, True)
        for k in writes:
            w = self.last_w.get(k)
            if w is not None:
                add(w, False)
            for r in self.readers.get(k, {}).values():
                add(r, False)
        return d

    def _commit(self, tok, reads, writes):
        sk = tok[0]
        for k in reads:
            self.readers.setdefault(k, {})[sk] = tok
        for k in writes:
            self.last_w[k] = tok
            self.readers[k] = {}

    def op(self, eng, fn, reads=(), writes=()):
        deps = self._deps(eng, reads, writes)
        n = self.count[eng]
        self.count[eng] = n + 1
        ep, v = divmod(n, EPOCH)
        tok = ((eng, ep), v + 1, eng)
        self._commit(tok, reads, writes)
        self.ops[eng].append(("op", fn, deps, tok))
        return tok

    def dma(self, q, fn, reads=(), writes=()):
        deps = self._deps("dmaq", reads, writes)
        i = self.dma_rr
        self.dma_rr = (i + 1) % N_DMA_SEMS
        prev = self.dma_tot[i]
        if prev:
            sk = ("dma", i)
            if deps.get(sk, 0) < prev:
                deps[sk] = prev
        self.dma_tot[i] = prev + 16
        tok = (("dma", i), prev + 16, "dmaq")
        self._commit(tok, reads, writes)
        self.ops[q].append(("dma", fn, deps, tok))
        return tok

    def wait_all(self, eng, toks):
        deps = {}
        for sk, v, _ in toks:
            deps[sk] = max(deps.get(sk, 0), v)
        self.ops[eng].append(("wait", None, deps, None))

    def alloc_sems(self, es):
        nc = self.nc
        for e, ne in N_EPOCH.items():
            for ep in range(ne):
                self.sems[(e, ep)] = es.enter_context(nc.semaphore(f"s_{e}{ep}"))
        for i in range(N_DMA_SEMS):
            self.sems[("dma", i)] = es.enter_context(nc.semaphore(f"s_dma{i}"))

    def barrier(self):
        toks = []
        for e, n in self.count.items():
            if n:
                ep, v = divmod(n - 1, EPOCH)
                toks.append(((e, ep), v + 1, e))
        for i in range(N_DMA_SEMS):
            if self.dma_tot[i]:
                toks.append((("dma", i), self.dma_tot[i], "dmaq"))
        for e in ("pe", "dve", "act", "pool", "sp"):
            self.wait_all(e, toks)

    def emit_block(self):
        nc = self.nc
        sems = self.sems
        for e, ne in N_EPOCH.items():
            assert self.count[e] <= ne * EPOCH, (e, self.count[e])
        with nc.Block() as block:
            def run(engname):
                lst = self.ops[engname]

                def body(eng):
                    waited = {}
                    for kind, fn, deps, tok in lst:
                        for sk, v in deps.items():
                            if waited.get(sk, 0) < v:
                                eng.wait_ge(sems[sk], v)
                                waited[sk] = v
                        if kind == "wait":
                            continue
                        ins = fn(eng)
                        ins.then_inc(sems[tok[0]], 16 if kind == "dma" else 1)
                return body

            block.tensor(run("pe"))
            block.vector(run("dve"))
            block.scalar(run("act"))
            block.gpsimd(run("pool"))
            block.sync(run("sp"))
        self.ops = {e: [] for e in ("pe", "dve", "act", "pool", "sp")}


class Arena:
    cnt = 0

    def __init__(self, nc, base, limit):
        self.nc, self.off, self.limit = nc, base, limit

    def alloc(self, name, shape, dt):
        esz = 4 if dt == F32 else 2
        nb = int(np.prod(shape[1:])) * esz
        nb = (nb + 63) // 64 * 64
        off = self.off
        self.off += nb
        assert self.off <= self.limit, (name, self.off, self.limit)
        Arena.cnt += 1
        return self.nc.alloc_sbuf_tensor_at(f"{name}_{Arena.cnt}", list(shape), dt, offset=off)

    def can(self, shape, dt):
        esz = 4 if dt == F32 else 2
        nb = (int(np.prod(shape[1:])) * esz + 63) // 64 * 64
        return self.off + nb <= self.limit

    def mark(self):
        return self.off

    def release(self, m):
        self.off = m


def _k(x):
    return x if isinstance(x, (str, tuple)) else x.name


def host_constants():
    c = {}
    c["ident"] = np.eye(128, dtype=np.float32)
    s = np.arange(128)[:, None]
    t = np.arange(128)[None, :]
    bank = np.zeros((2, 128, 512), np.float32)
    lvl = np.zeros((2, 7, 128, 128), np.float32)
    tri = np.zeros((2, 128, 3 * 128 + 2), np.float32)
    for d in range(2):
        if d == 0:
            strict = (s < t).astype(np.float32)
            incl = (s <= t).astype(np.float32)
        else:
            strict = (s > t).astype(np.float32)
            incl = (s >= t).astype(np.float32)
        bank[d] = np.concatenate([strict, incl, strict, incl], axis=1)
        for l in range(7):
            b = 1 << l
            tt_ = np.arange(128)[:, None]
            ss_ = np.arange(128)[None, :]
            same = (tt_ // (2 * b)) == (ss_ // (2 * b))
            t_second = (tt_ // b) % 2 == 1
            s_second = (ss_ // b) % 2 == 1
            if d == 0:
                m = same & t_second & (~s_second)
            else:
                m = same & (~t_second) & s_second
            lvl[d, l] = m.astype(np.float32)
        if d == 0:
            G = (s <= t).astype(np.float32) - (s <= 63).astype(np.float32)
            G1 = (s < t).astype(np.float32) - (s <= 63).astype(np.float32)
            Dm = (s > t).astype(np.float32)
            cm = (np.arange(128) <= 63).astype(np.float32)
        else:
            G = (s >= t).astype(np.float32) - (s >= 64).astype(np.float32)
            G1 = (s > t).astype(np.float32) - (s >= 64).astype(np.float32)
            Dm = (s < t).astype(np.float32)
            cm = (np.arange(128) >= 64).astype(np.float32)
        tri[d, :, 0:128] = DEC_C * G
        tri[d, :, 128:256] = DEC_C * G1
        tri[d, :, 256:384] = DEC_C * Dm
        tri[d, :, 384] = DEC_C * cm
        tri[d, :, 385] = DEC_C
    c["bankmask"] = bank
    c["lvlmask"] = lvl
    c["tri"] = tri
    hs = np.zeros((128, 2), np.float32)
    hs[:64, 0] = 1.0
    hs[64:, 1] = 1.0
    c["headsel"] = hs
    tok = np.arange(TS)
    row = (tok // 64).astype(np.float32)
    col = (tok % 64).astype(np.float32)
    tab = np.zeros((TS, 192), np.float32)

    def fill(base_c, base_s, half):
        inv = 10000.0 ** (-np.arange(half, dtype=np.float32) / half)
        for pi, pos in enumerate((row, col)):
            ang = pos[:, None] * inv[None, :]
            co, si = np.cos(ang), np.sin(ang)
            o = pi * 2 * half
            tab[:, base_c + o: base_c + o + half] = co
            tab[:, base_c + o + half: base_c + o + 2 * half] = co
            tab[:, base_s + o: base_s + o + half] = -si
            tab[:, base_s + o + half: base_s + o + 2 * half] = si

    fill(0, 64, 16)
    fill(128, 160, 8)
    c["rope"] = tab
    return c


class Builder:
    def __init__(self, stage=99):
        self.stage = stage
        self.nc = bass.Bass("TRN2", target_bir_lowering=False)
        self.S = Sched(self.nc)
        self.din = {}
        self.dout = {}

    def inp(self, name, shape):
        self.din[name] = self.nc.dram_tensor(name, list(shape), F32, kind="ExternalInput").ap()
        return self.din[name]

    def outp(self, name, shape):
        self.dout[name] = self.nc.dram_tensor(name, list(shape), F32, kind="ExternalOutput").ap()
        return self.dout[name]

    def scratch(self, name, shape, dt):
        return self.nc.dram_tensor(name, list(shape), dt).ap()

    def dma(self, out, in_, rk=None, wk=None, q="sp"):
        r = [_k(in_)] if rk is None else rk
        w = [_k(out)] if wk is None else wk
        self.S.dma(q, lambda e: e.dma_start(out=out, in_=in_), r, w)

    def mm(self, out, lhsT, rhs, start=True, stop=True, xr=()):
        self.S.op("pe", lambda e: e.matmul(out, lhsT, rhs, start=start, stop=stop),
                  [_k(lhsT), _k(rhs)] + [_k(x) for x in xr], [_k(out)])

    def tr(self, out, in_):
        p = in_.shape[0]
        idn = self.ident[0:p, 0:p]
        self.S.op("pe", lambda e: e.transpose(out, in_, idn), [_k(in_), _k(self.ident)], [_k(out)])

    def act(self, out, in_, func, bias=None, scale=None, accum=None):
        r = [_k(in_)]
        kw = {}
        if bias is not None:
            kw["bias"] = bias
            if not isinstance(bias, float):
                r.append(_k(bias))
        if scale is not None:
            kw["scale"] = scale
            if not isinstance(scale, float):
                r.append(_k(scale))
        w = [_k(out)]
        if accum is not None:
            kw["accum_out"] = accum
            w.append(_k(accum))
        self.S.op("act", lambda e: e.activation(out, in_, func, **kw), r, w)

    def tt(self, eng, out, a, b, op):
        self.S.op(eng, lambda e: e.tensor_tensor(out, a, b, op), [_k(a), _k(b)], [_k(out)])

    def ts(self, eng, out, a, s1, s2, op0, op1=None):
        r = [_k(a)] + [_k(s) for s in (s1, s2) if s is not None and not isinstance(s, (float, int))]
        if op1 is None:
            self.S.op(eng, lambda e: e.tensor_scalar(out, a, s1, s2, op0), r, [_k(out)])
        else:
            self.S.op(eng, lambda e: e.tensor_scalar(out, a, s1, s2, op0, op1), r, [_k(out)])

    def stt(self, out, a, scalar, b, op0, op1):
        r = [_k(a), _k(b)] + ([] if isinstance(scalar, (float, int)) else [_k(scalar)])
        self.S.op("dve", lambda e: e.scalar_tensor_tensor(out, a, scalar, b, op0, op1), r, [_k(out)])

    def cp(self, eng, out, in_):
        if eng == "act":
            self.S.op("act", lambda e: e.copy(out, in_), [_k(in_)], [_k(out)])
        else:
            self.S.op(eng, lambda e: e.tensor_copy(out, in_), [_k(in_)], [_k(out)])

    def memset(self, eng, out, val):
        self.S.op(eng, lambda e: e.memset(out, val), [], [_k(out)])

    def red(self, out, in_, op=ALU.add):
        self.S.op("dve", lambda e: e.tensor_reduce(out, in_, AX.X, op), [_k(in_)], [_k(out)])

    def recip(self, out, in_):
        self.S.op("dve", lambda e: e.reciprocal(out, in_), [_k(in_)], [_k(out)])

    def rstd(self, out, ss, n, eps, tmp):
        self.act(tmp, ss, AF.Sqrt, bias=float(eps), scale=1.0 / n)
        self.recip(out, tmp)

    def declare(self):
        i = self.inp
        self.x_prompt = i("x_prompt", [NPS, TP, D])
        self.x_sample = i("x_sample", [TS, D])
        self.cache_a_k = i("cache_a_k", [DEPTH, PAST, 128])
        self.cache_a_v = i("cache_a_v", [DEPTH, PAST, 128])
        self.cache_ckv = i("cache_b_ckv", [DEPTH, PAST, 128])
        self.cache_kr = i("cache_b_krope", [DEPTH, PAST, 32])
        self.state_f = i("state_c_fwd", [DEPTH, 8, 64, 64])
        self.state_b = i("state_c_bwd", [DEPTH, 8, 64, 64])
        self.cond = i("cond", [16, 128])
        self.nb_rows = i("nb_rows", [128, 128])
        self.w_mod = i("w_mod", [DEPTH, D, 3 * D])
        self.w_in = i("w_in", [DEPTH, D, IN_DIM])
        self.qkn = i("qkn", [DEPTH, 256])
        self.b_w_uk = i("b_w_uk", [DEPTH, 128, 512])
        self.b_w_uv = i("b_w_uv", [DEPTH, 128, 512])
        self.lfm_rows = i("lfm_rows", [DEPTH, 48, 128])
        self.c_w0 = i("c_w0", [DEPTH, 2, 512])
        self.c_w_up = i("c_w_up", [DEPTH, 128, 512])
        self.c_a_up = i("c_a_up", [DEPTH, 128, 512])
        self.lnx = i("lnx", [DEPTH, 1024])
        self.w_oa = i("w_oa", [DEPTH, 512, D])
        self.w_ob = i("w_ob", [DEPTH, 512, D])
        self.w_oc = i("w_oc", [DEPTH, 512, D])
        self.w_out = i("w_out", [DEPTH, D, D])
        self.fnw = i("final_norm_w", [D])
        self.c_ident = i("c_ident", [128, 128])
        self.c_bank = i("c_bank", [2, 128, 512])
        self.c_lvl = i("c_lvl", [2, 7, 128, 128])
        self.c_tri = i("c_tri", [2, 128, 386])
        self.c_hsel = i("c_hsel", [128, 2])
        self.c_rope = i("c_rope", [TS, 192])
        o = self.outp
        self.y_prompt = o("y_prompt", [NPS, TP, D])
        self.y_sample = o("y_sample", [TS, D])
        self.new_a_k = o("new_a_k", [NPS, DEPTH, TP, 128])
        self.new_a_v = o("new_a_v", [NPS, DEPTH, TP, 128])
        self.new_ckv = o("new_b_ckv", [NPS, DEPTH, TP, 128])
        self.new_kr = o("new_b_krope", [NPS, DEPTH, TP, 32])
        self.new_sf = o("new_c_state_fwd", [NPS, DEPTH, 8, 64, 64])
        self.new_sb = o("new_c_state_bwd", [NPS, DEPTH, 8, 64, 64])
        sc = self.scratch
        self.wsc = [sc(f"wsc{l}", [128, 8, IN_DIM], BF16) for l in range(DEPTH)]
        self.wosc = [[sc(f"wo{l}_{b}", ([64, 8, D] if b < 2 else [128, 4, D]), BF16) for b in range(3)] for l in range(DEPTH)]
        self.woutsc = [sc(f"wout{l}", [128, 8, D], BF16) for l in range(DEPTH)]
        self.xs_p = sc("xs_p", [NPS, TP, D], F32)
        self.xs_s = sc("xs_s", [TS, D], F32)
        self.yb_sc = sc("yb_sc", [TS, 512], F32)
        self.yc_sc = sc("yc_sc", [TS, 512], F32)

    def build(self):
        nc, S = self.nc, self.S
        self.declare()
        with contextlib.ExitStack() as es:
            S.alloc_sems(es)
            self.ps = [es.enter_context(nc.psum_tensor(f"ps{i}", [128, 512], F32)) for i in range(8)]
            self.carena = Arena(nc, SB_BASE, SB_BASE + 24576)
            self.arena = Arena(nc, SB_BASE + 24576, SB_LIMIT)
            self.prologue()
            seqs = [("p", s) for s in range(NPS)] + [("s", 0)]
            if self.stage < 50:
                seqs = seqs[:NPS]
            nlayers = DEPTH if self.stage >= 40 else 1
            for l in range(nlayers):
                self.layer_prologue(l)
                for kind, si in seqs:
                    self.run_seq(l, kind, si)
            S.barrier()
            S.emit_block()
        return nc

    def prologue(self):
        A = self.carena
        S = self.S
        self.ident = A.alloc("ident", [128, 128], F32)
        self.identb = A.alloc("identb", [128, 128], BF16)
        self.ones = A.alloc("ones", [128, 128], F32)
        self.onesb = A.alloc("onesb", [128, 128], BF16)
        self.bank = A.alloc("bank", [128, 2, 512], BF16)
        self.lvl = A.alloc("lvl", [128, 14, 128], BF16)
        self.tri = A.alloc("tri", [128, 2, 386], F32)
        self.hsel = A.alloc("hsel", [128, 2], F32)
        self.modA = A.alloc("modA", [128, DEPTH, 2, 8], F32)
        self.modB = A.alloc("modB", [128, DEPTH, 2, 8], F32)
        self.modG = A.alloc("modG", [128, DEPTH, 2, 8], F32)
        self.qkn_t = A.alloc("qkn_t", [128, 256], F32)
        self.uk = A.alloc("uk", [128, 512], BF16)
        self.uv = A.alloc("uv", [128, 512], BF16)
        self.wup = A.alloc("wup", [128, 512], BF16)
        self.aup = A.alloc("aup", [128, 512], BF16)
        self.w0row = A.alloc("w0row", [1, 2, 512], BF16)
        self.lfm = A.alloc("lfm", [128, 48], F32)
        self.c0fm = A.alloc("c0fm", [128, 14], F32)
        self.bonb = A.alloc("bonb", [128, 32, 8], F32)
        self.fnw_t = A.alloc("fnw_t", [128, D], F32)
        self.bones = A.alloc("bones", [128, 128], F32)
        self.omka = A.alloc("omka", [128, 4], F32)
        ar = self.arena
        m0 = ar.mark()
        st = ar.alloc("st", [128, 128], F32)
        st2 = ar.alloc("st2", [128, 512], F32)
        lv = ar.alloc("lv", [128, 14, 128], F32)
        cst = ar.alloc("cst", [16, 128], F32)
        scond = ar.alloc("scond", [128, 16], F32)
        nbfm = ar.alloc("nbfm", [128, 128], F32)
        mods = ar.alloc("mods", [128, DEPTH, 24, 2], F32)
        wst = [ar.alloc(f"wst{i}", [128, 8, 512], F32) for i in range(2)]
        self.dma(self.ident[:], self.c_ident)
        self.cp("dve", self.identb[:], self.ident[:])
        self.memset("pool", self.ones[:], 1.0)
        self.memset("pool", self.onesb[:], 1.0)
        self.memset("pool", self.bones[:], 0.0)
        self.memset("pool", self.bones[0:64, 0:64], 1.0)
        self.memset("pool", self.bones[64:128, 64:128], 1.0)
        for d in range(2):
            self.dma(st2[:], self.c_bank[d])
            self.cp("dve", self.bank[:, d, :], st2[:])
        for d in range(2):
            self.dma(lv[:, 0:7, :], self.c_lvl[d].rearrange("l t s -> t l s"))
            self.cp("dve", self.lvl[:, d * 7:(d + 1) * 7, :], lv[:, 0:7, :])
        self.dma(self.tri[:], self.c_tri.rearrange("d s c -> s d c"))
        self.dma(self.hsel[:], self.c_hsel)
        self.dma(self.fnw_t[:], self.fnw.partition_broadcast(128))
        self.dma(cst[:], self.cond)
        self.tr(self.ps[0][:, 0:16], cst[:])
        self.act(scond[:], self.ps[0][:, 0:16], AF.Silu)
        self.dma(st[:], self.nb_rows)
        self.tr(self.ps[1][:, 0:128], st[:])
        self.cp("dve", nbfm[:], self.ps[1][:, 0:128])
        nwm = 0
        for l in range(DEPTH):
            pm = self.ps[2 + (l % 2)]
            for j in range(6):
                w = wst[nwm % 2]
                nwm += 1
                self.dma(w[:], self.w_mod[l].rearrange("(kc p) n -> p kc n", p=128)[:, :, j * 512:(j + 1) * 512])
                for f in range(4):
                    fc = j * 4 + f
                    for kc in range(8):
                        self.mm(pm[:, fc * 2:fc * 2 + 2], w[:, kc, f * 128:(f + 1) * 128],
                                scond[:, kc:16:8], start=(kc == 0), stop=(kc == 7))
            self.tt("dve", mods[:, l, :, :], pm[:, 0:48].rearrange("p (a b) -> p a b", b=2),
                    nbfm[:, l * 32 + 8:l * 32 + 32].unsqueeze(2).broadcast_to([128, 24, 2]), ALU.add)
            for c in range(2):
                self.ts("dve", self.modA[:, l, c, :], mods[:, l, 8:16, c], 1.0, None, ALU.add)
                self.tt("dve", self.modA[:, l, c, :], self.modA[:, l, c, :], nbfm[:, l * 32:l * 32 + 8], ALU.mult)
                self.cp("dve", self.modB[:, l, c, :], mods[:, l, 0:8, c])
                self.cp("dve", self.modG[:, l, c, :], mods[:, l, 16:24, c])
        S.barrier()
        S.emit_block()
        ar.release(m0)

    def layer_prologue(self, l):
        S, ar = self.S, self.arena
        m0 = ar.mark()
        wst = [ar.alloc(f"cst{i}", [128, 8, 512], F32) for i in range(2)]
        wbf = [ar.alloc(f"cbf{i}", [128, 8, 512], BF16) for i in range(2)]
        st = ar.alloc("lst", [128, 512], F32)
        st48 = ar.alloc("lst48", [48, 128], F32)
        n = 0
        engs = ["dve", "pool", "act"]
        src = self.w_in[l].rearrange("(kc p) n -> p kc n", p=128)
        for c0 in range(0, IN_DIM, 512):
            w = min(512, IN_DIM - c0)
            a, b = wst[n % 2], wbf[n % 2]
            self.dma(a[:, :, 0:w], src[:, :, c0:c0 + w])
            self.cp(engs[n % 3], b[:, :, 0:w], a[:, :, 0:w])
            self.dma(self.wsc[l][:, :, c0:c0 + w], b[:, :, 0:w])
            n += 1
        src = self.w_out[l].rearrange("(kc p) n -> p kc n", p=128)
        for c0 in range(0, D, 512):
            a, b = wst[n % 2], wbf[n % 2]
            self.dma(a[:], src[:, :, c0:c0 + 512])
            self.cp(engs[n % 3], b[:], a[:])
            self.dma(self.woutsc[l][:, :, c0:c0 + 512], b[:])
            n += 1
        for bi, wsrc in enumerate((self.w_oa, self.w_ob, self.w_oc)):
            if bi < 2:
                src = wsrc[l].rearrange("(h p) n -> p h n", p=64)
                np_, nh_ = 64, 8
            else:
                src = wsrc[l].rearrange("(kc p) n -> p kc n", p=128)
                np_, nh_ = 128, 4
            for c0 in range(0, D, 512):
                a, b = wst[n % 2], wbf[n % 2]
                self.dma(a[0:np_, 0:nh_, :], src[:, :, c0:c0 + 512])
                self.cp(engs[n % 3], b[0:np_, 0:nh_, :], a[0:np_, 0:nh_, :])
                self.dma(self.wosc[l][bi][:, :, c0:c0 + 512], b[0:np_, 0:nh_, :])
                n += 1
        for dst, srcw in ((self.uk, self.b_w_uk), (self.uv, self.b_w_uv), (self.wup, self.c_w_up), (self.aup, self.c_a_up)):
            self.dma(st[:], srcw[l])
            self.cp("dve", dst[:], st[:])
        self.dma(st[0:1, 0:512], self.c_w0[l, 0:1, :])
        self.cp("dve", self.w0row[:, 0, :], st[0:1, 0:512])
        self.dma(st[0:1, 0:512], self.c_w0[l, 1:2, :])
        self.cp("dve", self.w0row[:, 1, :], st[0:1, 0:512])
        self.dma(self.qkn_t[:], self.qkn[l].partition_broadcast(128))
        self.dma(st48[:], self.lfm_rows[l])
        self.tr(self.ps[0][:, 0:48], st48[:])
        self.cp("dve", self.lfm[:], self.ps[0][:, 0:48])
        self.tt("dve", self.c0fm[:], self.lfm[:, 12:26], self.lfm[:, 26:40], ALU.add)
        self.ts("dve", self.c0fm[:], self.c0fm[:], -1.0, 1.0, ALU.mult, ALU.add)
        self.ts("dve", self.omka[:], self.lfm[:, 4:8], -1.0, 1.0, ALU.mult, ALU.add)
        S.barrier()
        S.emit_block()
        ar.release(m0)

    def run_seq(self, l, kind, si):
        S, ar = self.S, self.arena
        T = TP if kind == "p" else TS
        cond = 0 if kind == "p" else 1
        last = (l == DEPTH - 1)
        if l == 0:
            xin = self.x_prompt[si] if kind == "p" else self.x_sample
        else:
            xin = self.xs_p[si] if kind == "p" else self.xs_s
        xout = self.xs_p[si] if kind == "p" else self.xs_s
        ctx = dict(l=l, kind=kind, si=si, T=T, cond=cond, xin=xin, xout=xout, last=last)
        m0 = ar.mark()
        hT = ar.alloc("hT", [128, 8, T + 2], BF16)
        mH = ar.mark()
        ctx["hT"] = hT
        self.phase_H(ctx)
        S.barrier()
        S.emit_block()
        if self.stage >= 20:
            self.phase_rwkv(ctx)
        self.phase_KV(ctx)
        S.barrier()
        S.emit_block()
        if self.stage >= 30:
            lo = Arena(self.nc, m0, mH)
            self.phase_Q(ctx, lo, ar)
            S.barrier()
            S.emit_block()
        ar.release(m0)

    def compute_hT(self, ctx, hT, col0, x_ap, nst, work):
        l, cond = ctx["l"], ctx["cond"]
        xt, xn, ss, tmp, rs = work
        for st in range(nst):
            b = st % 2
            self.dma(xt[b][:], x_ap[st * 128:(st + 1) * 128, :])
            self.act(xn[b][:], xt[b][:], AF.Square, accum=ss[:, b:b + 1])
            self.rstd(rs[:, b:b + 1], ss[:, b:b + 1], D, NORM_EPS, tmp[:, b:b + 1])
            self.ts("dve", xn[b][:], xt[b][:], rs[:, b:b + 1], None, ALU.mult)
            for half in range(2):
                pb = self.ps[(2 * st + half) % 4]
                for q in range(4):
                    kc = half * 4 + q
                    self.tr(pb[:, q * 128:(q + 1) * 128], xn[b][:, kc * 128:(kc + 1) * 128])
                for q in range(4):
                    kc = half * 4 + q
                    dst = hT[:, kc, col0 + st * 128: col0 + (st + 1) * 128]
                    if q % 2 == 0:
                        self.act(dst, pb[:, q * 128:(q + 1) * 128], AF.Identity,
                                 bias=self.modB[:, l, cond, kc:kc + 1], scale=self.modA[:, l, cond, kc:kc + 1])
                    else:
                        self.ts("dve", dst, pb[:, q * 128:(q + 1) * 128], self.modA[:, l, cond, kc:kc + 1],
                                self.modB[:, l, cond, kc:kc + 1], ALU.mult, ALU.add)

    def hT_work(self, ar):
        xt = [ar.alloc(f"xt{i}", [128, D], F32) for i in range(2)]
        xn = [ar.alloc(f"xn{i}", [128, D], F32) for i in range(2)]
        ss = ar.alloc("ss", [128, 2], F32)
        tmp = ar.alloc("tmpr", [128, 2], F32)
        rs = ar.alloc("rs", [128, 2], F32)
        return (xt, xn, ss, tmp, rs)

    def phase_H(self, ctx):
        ar = self.arena
        T, hT = ctx["T"], ctx["hT"]
        m = ar.mark()
        work = self.hT_work(ar)
        self.memset("pool", hT[:, :, 0:1], 0.0)
        self.memset("pool", hT[:, :, T + 1:T + 2], 0.0)
        self.compute_hT(ctx, hT, 1, ctx["xin"], T // 128, work)
        ar.release(m)

    def phase_KV(self, ctx):
        ar = self.arena
        l, kind, si, T, hT = ctx["l"], ctx["kind"], ctx["si"], ctx["T"], ctx["hT"]
        nk = T + (PAST if kind == "s" else 0)
        nkt = nk // 128
        KaT = ar.alloc("KaT", [128, nk], BF16)
        Va = ar.alloc("Va", [128, nkt, 2, 65], BF16)
        ckvT = ar.alloc("ckvT", [128, nk], BF16)
        KbT = ar.alloc("KbT", [96, 2, nk], BF16)
        Vb = ar.alloc("Vb", [128, nkt, 8, 65], BF16)
        ctx.update(KaT=KaT, Va=Va, ckvT=ckvT, KbT=KbT, Vb=Vb, nk=nk, nkt=nkt)
        m = ar.mark()
        wkv = ar.alloc("wkv", [128, 8, 416], BF16)
        kn = [ar.alloc(f"kn{i}", [128, 128], F32) for i in range(2)]
        vf = [ar.alloc(f"vf{i}", [128, 128], F32) for i in range(2)]
        cn = [ar.alloc(f"cn{i}", [128, 128], F32) for i in range(2)]
        krf = [ar.alloc(f"krf{i}", [128, 32], F32) for i in range(2)]
        junk = ar.alloc("kvjunk", [128, 128], F32)
        ss = ar.alloc("kvss", [128, 4], F32)
        tm = ar.alloc("kvtm", [128, 4], F32)
        rs = ar.alloc("kvrs", [128, 4], F32)
        rp = [ar.alloc(f"kvrp{i}", [128, 192], F32) for i in range(2)]
        rtmp = ar.alloc("kvrtmp", [128, 128], F32)
        self.memset("pool", ss[:], 0.0)
        self.memset("pool", Va[:, :, :, 64:65], 1.0)
        self.memset("pool", Vb[:, :, :, 64:65], 1.0)
        self.dma(wkv[:, :, 0:256], self.wsc[l][:, :, O_AK:O_AK + 256])
        self.dma(wkv[:, :, 256:416], self.wsc[l][:, :, O_CKV:O_CKV + 160])
        qk = self.qkn_t
        for st in range(nkt):
            b = st % 2
            new = st < T // 128
            pA, pB, pT, pV = self.ps[0 + 4 * b], self.ps[1 + 4 * b], self.ps[2 + 4 * b], self.ps[3 + 4 * b]
            if new:
                hs = hT[:, :, 1 + st * 128: 1 + (st + 1) * 128]
                for kc in range(8):
                    self.mm(pA[:, 0:256], hs[:, kc, :], wkv[:, kc, 0:256], start=(kc == 0), stop=(kc == 7))
                for kc in range(8):
                    self.mm(pB[:, 0:160], hs[:, kc, :], wkv[:, kc, 256:416], start=(kc == 0), stop=(kc == 7))
                for h in range(2):
                    self.act(junk[:, 0:64], pA[:, h * 64:(h + 1) * 64], AF.Square, accum=ss[:, h:h + 1])
                self.rstd(rs[:, 0:2], ss[:, 0:2], 64, NORM_EPS, tm[:, 0:2])
                self.tt("dve", kn[b][:].rearrange("p (a c) -> p a c", a=2), pA[:, 0:128].rearrange("p (a c) -> p a c", a=2),
                        rs[:, 0:2].unsqueeze(2).broadcast_to([128, 2, 64]), ALU.mult)
                self.tt("pool", kn[b][:].rearrange("p (a c) -> p a c", a=2), kn[b][:].rearrange("p (a c) -> p a c", a=2),
                        qk[:, 64:128].unsqueeze(1).broadcast_to([128, 2, 64]), ALU.mult)
                self.cp("act", vf[b][:], pA[:, 128:256])
                self.act(junk[:], pB[:, 0:128], AF.Square, accum=ss[:, 2:3])
                self.rstd(rs[:, 2:3], ss[:, 2:3], 128, NORM_EPS, tm[:, 2:3])
                self.ts("dve", cn[b][:], pB[:, 0:128], rs[:, 2:3], None, ALU.mult)
                self.tt("pool", cn[b][:], cn[b][:], qk[:, 128:256], ALU.mult)
                self.cp("act", krf[b][:], pB[:, 128:160])
                if kind == "p":
                    self.dma(self.new_a_k[si, l, st * 128:(st + 1) * 128, :], kn[b][:], wk=[("nak", si, l, st)])
                    self.dma(self.new_a_v[si, l, st * 128:(st + 1) * 128, :], vf[b][:], wk=[("nav", si, l, st)])
                    self.dma(self.new_ckv[si, l, st * 128:(st + 1) * 128, :], cn[b][:], wk=[("nck", si, l, st)])
                    self.dma(self.new_kr[si, l, st * 128:(st + 1) * 128, :], krf[b][:], wk=[("nkr", si, l, st)])
                else:
                    self.dma(rp[b][:], self.c_rope[st * 128:(st + 1) * 128, :])
                    self.rope(kn[b][:].rearrange("p (h c) -> p h c", h=2), rp[b][:, 0:64], rp[b][:, 64:128], 2, 16, rtmp)
                    self.rope(krf[b][:].rearrange("p (h c) -> p h c", h=1), rp[b][:, 128:160], rp[b][:, 160:192], 1, 8, rtmp)
            else:
                c0 = (st - T // 128) * 128
                self.dma(kn[b][:], self.cache_a_k[l, c0:c0 + 128, :])
                self.dma(vf[b][:], self.cache_a_v[l, c0:c0 + 128, :])
                self.dma(cn[b][:], self.cache_ckv[l, c0:c0 + 128, :])
                self.dma(krf[b][:], self.cache_kr[l, c0:c0 + 128, :])
            ks = slice(st * 128, (st + 1) * 128)
            self.tr(pT[:, 0:128], kn[b][:])
            self.cp("act", KaT[:, ks], pT[:, 0:128])
            self.tr(pT[:, 128:256], cn[b][:])
            self.cp("dve", ckvT[:, ks], pT[:, 128:256])
            self.mm(pT[64:96, 256:384], krf[b][:], self.ident[:], start=True, stop=True)
            self.cp("act", KbT[64:96, 0, ks], pT[64:96, 256:384])
            self.cp("pool", KbT[64:96, 1, ks], KbT[64:96, 0, ks])
            self.cp("pool", Va[:, st, :, 0:64], vf[b][:].rearrange("p (a c) -> p a c", a=2))
            self.mm(pV[:, :], ckvT[:, ks], self.uv[:], start=True, stop=True)
            self.cp("act" if st % 2 else "dve", Vb[:, st, :, 0:64], pV[:, :].rearrange("p (a c) -> p a c", a=8))
        ar.release(m)

    def rope(self, x3, cc, ss_, nh, half, tmp):
        w = 4 * half
        t3 = tmp[:, 0:nh * w].rearrange("p (h c) -> p h c", h=nh)
        for a in range(2):
            for hb in range(2):
                o = a * 2 * half + hb * half
                o2 = a * 2 * half + (1 - hb) * half
                self.tt("pool", t3[:, :, o:o + half], x3[:, :, o2:o2 + half],
                        ss_[:, o:o + half].unsqueeze(1).broadcast_to([128, nh, half]), ALU.mult)
        self.tt("dve", x3, x3, cc.unsqueeze(1).broadcast_to([128, nh, w]), ALU.mult)
        self.tt("dve", x3, x3, t3, ALU.add)

    def phase_rwkv(self, ctx):
        S, ar = self.S, self.arena
        l, kind, si, T, hT = ctx["l"], ctx["kind"], ctx["si"], ctx["T"], ctx["hT"]
        nsb = T // 256
        m = ar.mark()
        W = {}
        a = lambda n, shp, dt=F32: W.__setitem__(n, ar.alloc(n, shp, dt))
        a("wc", [128, 8, 1792], BF16)
        a("wcz", [128, 8, 512], BF16)
        a("lnx", [128, 1024])
        for n in ("rT", "kT", "vT"):
            a(n, [128, 4, 256])
        for n in ("kq", "sq", "tmpf", "kk", "bT", "kdT", "asg", "u"):
            a(n, [128, 4, 128])
        for n in ("sg", "Ep", "Em", "E1", "ED", "ybt"):
            a(n, [128, 512])
        W["ysb"] = W["Ep"]; W["sz"] = W["Em"]; W["vtok"] = W["E1"]
        a("t1a", [128, 256]); a("t1b", [128, 256]); a("wdT", [128, 256]); a("adT", [128, 256])
        a("tw", [128, 256], BF16); a("adb", [128, 256], BF16)
        a("OPS", [128, 4, 4, 128], BF16)
        a("Bt", [128, 512], BF16); a("Kt", [128, 512], BF16); a("Vt", [128, 512], BF16)
        a("AT", [128, 8, 512], BF16); a("ATr", [128, 512], BF16)
        a("Tm", [128, 8, 128], BF16); a("Zm", [128, 8, 128], BF16); a("Qm", [128, 8, 128], BF16)
        a("S", [128, 4, 64]); a("S0s", [128, 4, 64], BF16); a("emwl", [128, 4, 2])
        a("Xb", [128, 512], BF16); a("Ub", [128, 512], BF16)
        a("st8", [128, 6, 8]); a("sti", [64, 4, 128])
        for c0 in range(0, 1792, 448):
            self.dma(W["wc"][:, :, c0:c0 + 448], self.wsc[l][:, :, O_CIN + c0:O_CIN + c0 + 448])
        self.dma(W["wcz"][:], self.wsc[l][:, :, O_CZ:O_CZ + 512])
        self.dma(W["lnx"][:], self.lnx[l].partition_broadcast(128))
        for d in (1, 0):
            Sst = W["S"]
            if kind == "p":
                self.memset("pool", Sst[:], 0.0)
            else:
                src = (self.state_f if d == 0 else self.state_b)[l]
                self.dma(W["sti"][:].rearrange("i p (a j) -> i p a j", a=2), src.rearrange("(p a) i j -> i p a j", a=2))
                for p in range(4):
                    self.tr(self.ps[0][:, p * 64:(p + 1) * 64], W["sti"][:, p, :])
                self.cp("dve", Sst[:].rearrange("q p i -> q (p i)"), self.ps[0][:, 0:256])
            order = range(nsb - 1, -1, -1) if d == 1 else range(nsb)
            for sb in order:
                self.rwkv_super(ctx, d, sb, W)
            if kind == "p":
                dst = (self.new_sf if d == 0 else self.new_sb)[si, l]
                for p in range(4):
                    self.tr(self.ps[0][0:64, p * 128:(p + 1) * 128], Sst[:, p, :])
                self.cp("dve", W["sti"][:].rearrange("i p c -> i (p c)"), self.ps[0][0:64, :])
                self.dma(dst.rearrange("(p a) i j -> i p a j", a=2), W["sti"][:].rearrange("i p (a j) -> i p a j", a=2),
                         wk=[("nst", d, si, l)])
            S.barrier()
            S.emit_block()
        ar.release(m)

    def rwkv_super(self, ctx, d, sb, W):
        hT = ctx["hT"]
        ps, lfm = self.ps, self.lfm
        ts0 = sb * 256
        rT, kT, vT = W["rT"], W["kT"], W["vT"]
        for ci in range(14):
            pb = ps[ci % 2]
            for kc in range(8):
                self.mm(pb[:, 0:258], W["wc"][:, kc, ci * 128:(ci + 1) * 128], hT[:, kc, ts0:ts0 + 258],
                        start=(kc == 0), stop=(kc == 7))
            if ci < 12:
                dst = (rT, kT, vT)[ci // 4][:, ci % 4, :]
            else:
                dst = (W["wdT"], W["adT"])[ci - 12][:]
            t1 = W["t1a" if ci % 2 == 0 else "t1b"]
            self.act(t1[:], pb[:, 1:257], AF.Identity, scale=self.c0fm[:, ci:ci + 1])
            self.stt(t1[:], pb[:, 0:256], lfm[:, 12 + ci:13 + ci], t1[:], ALU.mult, ALU.add)
            self.stt(dst, pb[:, 2:258], lfm[:, 26 + ci:27 + ci], t1[:], ALU.mult, ALU.add)
        self.act(W["tw"][:], W["wdT"][:], AF.Tanh)
        self.cp("pool", W["adb"][:], W["adT"][:])
        for blk in ((1, 0) if d == 1 else (0, 1)):
            self.rwkv_block(ctx, d, sb * 2 + blk, blk, W)

    def rwkv_block(self, ctx, d, bi, blk, W):
        l, kind, T, hT = ctx["l"], ctx["kind"], ctx["T"], ctx["hT"]
        ps = self.ps
        t0 = bi * 128
        bs = slice(blk * 128, (blk + 1) * 128)
        lfm = self.lfm
        bc3 = lambda ap, n: ap.unsqueeze(2).broadcast_to([128, ap.shape[1], n])
        v3 = lambda t: t[:].rearrange("q (p c) -> q p c", p=4)
        rT, kT, vT = W["rT"][:, :, bs], W["kT"][:, :, bs], W["vT"][:, :, bs]
        tw, adb = W["tw"][:, bs], W["adb"][:, bs]
        if d == 0:
            self.dma(W["ybt"][:], self.yb_sc[t0:t0 + 128, :])
        hd = slice(d * 64, (d + 1) * 64)
        self.mm(ps[2][:, :], tw[hd, :], self.wup[hd, :], start=True, stop=False)
        self.mm(ps[2][:, :], self.onesb[0:1, 0:128], self.w0row[0:1, d, :], start=False, stop=True)
        self.act(W["sg"][:], ps[2][:, :], AF.Sigmoid)
        for p in range(4):
            self.mm(ps[3][:, p * 128:(p + 1) * 128], self.aup[hd, p * 128:(p + 1) * 128], adb[hd, :])
        for p in range(4):
            self.act(W["asg"][:, p, :], ps[3][:, p * 128:(p + 1) * 128], AF.Sigmoid, bias=lfm[:, 40 + d * 4 + p:41 + d * 4 + p])
        self.tt("pool", W["kq"][:], kT, bc3(lfm[:, 0:4], 128), ALU.mult)
        self.tt("pool", W["sq"][:], W["kq"][:], W["kq"][:], ALU.mult)
        self.mm(ps[4][:, :], self.bones[:], W["sq"][:].rearrange("q p c -> q (p c)"))
        self.act(W["tmpf"][:].rearrange("q p c -> q (p c)"), ps[4][:, :], AF.Sqrt)
        self.ts("dve", W["tmpf"][:], W["tmpf"][:], 1e-12, None, ALU.max)
        self.recip(W["tmpf"][:], W["tmpf"][:])
        self.tt("pool", W["kk"][:], W["kq"][:], W["tmpf"][:], ALU.mult)
        self.tt("pool", W["bT"][:], W["kk"][:], W["asg"][:], ALU.mult)
        self.tt("pool", W["u"][:], W["asg"][:], bc3(lfm[:, 4:8], 128), ALU.mult)
        self.tt("pool", W["u"][:], W["u"][:], bc3(self.omka[:, 0:4], 128), ALU.add)
        self.tt("pool", W["kdT"][:], kT, W["u"][:], ALU.mult)
        if CUT == 1:
            return
        tri = self.tri
        first, lastt = (0, 127) if d == 0 else (127, 0)
        for p in range(4):
            pc = ps[5 + p // 2]
            self.mm(pc[:, (p % 2) * 256:(p % 2) * 256 + 256], W["sg"][:, p * 128:(p + 1) * 128], tri[:, d, 0:256])
        self.mm(ps[7][:, :], tri[:, d, 256:384], W["sg"][:])
        emwl = W["emwl"]
        for hf in range(2):
            pc = ps[5 + hf]
            pv_ = pc[:, :].rearrange("q (p k t) -> q p k t", p=2, k=2)
            hs_ = slice(hf * 256, (hf + 1) * 256)
            self.act(v3(W["Ep"])[:, hf * 2:hf * 2 + 2, :], pv_[:, :, 0, :], AF.Exp)
            self.act(v3(W["Em"])[:, hf * 2:hf * 2 + 2, :], pv_[:, :, 0, :], AF.Exp, scale=-1.0)
            self.act(v3(W["E1"])[:, hf * 2:hf * 2 + 2, :], pv_[:, :, 1, :], AF.Exp)
            self.act(emwl[:, hf * 2:hf * 2 + 2, 0], pv_[:, :, 1, first], AF.Exp, scale=-1.0)
        self.act(W["ED"][:], ps[7][:, :], AF.Exp)
        self.tt("dve", emwl[:, :, 1], emwl[:, :, 0], v3(W["Ep"])[:, :, lastt], ALU.mult)
        OPS = W["OPS"]
        self.stt(OPS[:, :, 0, :], W["kk"][:], -1.0, v3(W["E1"]), ALU.mult, ALU.mult)
        self.tt("pool", OPS[:, :, 1, :], rT, v3(W["Ep"]), ALU.mult)
        self.tt("pool", OPS[:, :, 2, :], W["bT"][:], v3(W["Em"]), ALU.mult)
        self.tt("dve", OPS[:, :, 3, :], W["kdT"][:], v3(W["Em"]), ALU.mult)
        prod = W["kq"]
        self.tt("pool", prod[:], rT, W["kdT"][:], ALU.mult)
        self.tt("pool", prod[:], prod[:], bc3(lfm[:, 8:12], 128), ALU.mult)
        if CUT == 2:
            return
        for p in range(4):
            self.tr(ps[5][:, p * 128:(p + 1) * 128], W["bT"][:, p, :])
        self.tt("dve", W["Bt"][:], ps[5][:, :], W["ED"][:], ALU.mult)
        for p in range(4):
            self.tr(ps[6][:, p * 128:(p + 1) * 128], W["kdT"][:, p, :])
        self.tt("dve", W["Kt"][:], ps[6][:, :], W["ED"][:], ALU.mult)
        for p in range(4):
            self.tr(ps[7][:, p * 128:(p + 1) * 128], vT[:, p, :])
        self.cp("act", W["Vt"][:], ps[7][:, :])
        if d == 0:
            self.cp("act", W["vtok"][:], ps[7][:, :])
        AT = W["AT"]
        for h in range(8):
            p, h2 = h // 2, h % 2
            hp = slice(h2 * 64, h2 * 64 + 64)
            pa = ps[h % 2]
            self.mm(pa[:, 0:256], OPS[hp, p, 2, :], OPS[hp, p, 0:2, :])
            self.mm(pa[:, 256:512], OPS[hp, p, 3, :], OPS[hp, p, 0:2, :])
            if h % 2 == 0:
                self.tt("dve", AT[:, h, :], pa[:, :], self.bank[:, d, :], ALU.mult)
            else:
                self.cp("act", W["ATr"][:], pa[:, :])
                self.tt("pool", AT[:, h, :], W["ATr"][:], self.bank[:, d, :], ALU.mult)
        if CUT == 3:
            return
        Tm, Zm, Qm = W["Tm"], W["Zm"], W["Qm"]
        idb = self.identb[:].unsqueeze(1).broadcast_to([128, 8, 128])
        self.cp("pool", Tm[:], idb)
        self.cp("pool", Zm[:], idb)
        for lev in range(7):
            for g in range(2):
                pq, pt, pz = ps[2 + 3 * g], ps[3 + 3 * g], ps[4 + 3 * g]
                gs = slice(4 * g, 4 * g + 4)
                hs = range(4 * g, 4 * g + 4)
                for c, h in enumerate(hs):
                    self.mm(pq[:, c * 128:(c + 1) * 128], AT[:, h, 0:128], Tm[:, h, :])
                self.tt("dve", Qm[:, gs, :], pq[:, :].rearrange("q (c s) -> q c s", c=4),
                        self.lvl[:, d * 7 + lev, :].unsqueeze(1).broadcast_to([128, 4, 128]), ALU.mult)
                if lev < 6:
                    for c, h in enumerate(hs):
                        self.mm(pt[:, c * 128:(c + 1) * 128], Zm[:, h, :], Qm[:, h, :])
                for c, h in enumerate(hs):
                    self.mm(pz[:, c * 128:(c + 1) * 128], Qm[:, h, :], Zm[:, h, :])
                if lev < 6:
                    self.tt("dve", Tm[:, gs, :], pt[:, :].rearrange("q (c s) -> q c s", c=4), Tm[:, gs, :], ALU.add)
                self.tt("dve", Zm[:, gs, :], pz[:, :].rearrange("q (c s) -> q c s", c=4), Zm[:, gs, :], ALU.add)
        if CUT == 4:
            return
        Sst, S0s = W["S"], W["S0s"]
        Vt, Xb, Ub = W["Vt"], W["Xb"], W["Ub"]
        self.tt("dve", S0s[:], Sst[:], emwl[:, :, 0:1].broadcast_to([128, 4, 64]), ALU.mult)
        for h in range(8):
            p, h2 = h // 2, h % 2
            hp = slice(h2 * 64, h2 * 64 + 64)
            hc = slice(h * 64, (h + 1) * 64)
            self.mm(ps[0][:, hc], OPS[hp, p, 0, :], S0s[hp, p, :], start=True, stop=False)
            self.mm(ps[0][:, hc], AT[:, h, 256:384], Vt[:, hc], start=False, stop=True)
        self.cp("act", Xb[:], ps[0][:, :])
        for h in range(8):
            hc = slice(h * 64, (h + 1) * 64)
            self.mm(ps[1][:, hc], Zm[:, h, :], Xb[:, hc])
        self.cp("act", Ub[:], ps[1][:, :])
        for h in range(8):
            p, h2 = h // 2, h % 2
            hp = slice(h2 * 64, h2 * 64 + 64)
            hc = slice(h * 64, (h + 1) * 64)
            self.mm(ps[2][:, hc], OPS[hp, p, 1, :], S0s[hp, p, :], start=True, stop=False)
            self.mm(ps[2][:, hc], AT[:, h, 128:256], Ub[:, hc], start=False, stop=False)
            self.mm(ps[2][:, hc], AT[:, h, 384:512], Vt[:, hc], start=False, stop=True)
        for h in range(8):
            p, h2 = h // 2, h % 2
            hp = slice(h2 * 64, h2 * 64 + 64)
            hc = slice(h * 64, (h + 1) * 64)
            self.mm(ps[3][hp, p * 64:(p + 1) * 64], W["Bt"][:, hc], Ub[:, hc], start=True, stop=False)
            self.mm(ps[3][hp, p * 64:(p + 1) * 64], W["Kt"][:, hc], Vt[:, hc], start=False, stop=True)
        self.tt("dve", Sst[:], Sst[:], emwl[:, :, 1:2].broadcast_to([128, 4, 64]), ALU.mult)
        self.tt("dve", Sst[:], Sst[:], ps[3][:, 0:256].rearrange("q (p i) -> q p i", p=4), ALU.add)
        if CUT == 5:
            return
        for p in range(4):
            self.mm(ps[4][:, p * 2:p * 2 + 2], prod[:, p, :], self.hsel[:, :])
        st8 = W["st8"]
        if d == 1:
            self.cp("dve", self.bonb[:, bi, :], ps[4][:, 0:8])
            self.cp("act", W["ysb"][:], ps[2][:, :])
            self.dma(self.yb_sc[t0:t0 + 128, :], W["ysb"][:])
            return
        ysb = W["ysb"]
        b8 = lambda ap: ap.unsqueeze(2).broadcast_to([128, 8, 64])
        y3 = lambda t: t[:].rearrange("q (h i) -> q h i", h=8)
        self.tt("dve", st8[:, 5, :], ps[4][:, 0:8], self.bonb[:, bi, :], ALU.add)
        self.tt("dve", ysb[:], ps[2][:, :], W["ybt"][:], ALU.add)
        for kc in range(8):
            self.mm(ps[5][:, :], hT[:, kc, 1 + t0:1 + t0 + 128], W["wcz"][:, kc, :], start=(kc == 0), stop=(kc == 7))
        self.act(W["sz"][:], ps[5][:, :], AF.Silu)
        sqy = W["sq"][:].rearrange("q p c -> q (p c)")
        yn = W["u"][:].rearrange("q p c -> q (p c)")
        yn3 = W["u"][:].rearrange("q p (a i) -> q (p a) i", a=2)
        self.red(st8[:, 0, :], y3(ysb))
        self.tt("pool", sqy, ysb[:], ysb[:], ALU.mult)
        self.red(st8[:, 1, :], W["sq"][:].rearrange("q p (a i) -> q (p a) i", a=2))
        self.ts("dve", st8[:, 0, :], st8[:, 0, :], 1.0 / 64, None, ALU.mult)
        self.tt("dve", st8[:, 2, :], st8[:, 0, :], st8[:, 0, :], ALU.mult)
        self.stt(st8[:, 3, :], st8[:, 1, :], 1.0 / 64, st8[:, 2, :], ALU.mult, ALU.subtract)
        self.act(st8[:, 4, :], st8[:, 3, :], AF.Sqrt, bias=float(GN_EPS))
        self.recip(st8[:, 4, :], st8[:, 4, :])
        self.tt("pool", yn3, y3(ysb), b8(st8[:, 0, :]), ALU.subtract)
        self.tt("pool", yn3, yn3, b8(st8[:, 4, :]), ALU.mult)
        self.tt("pool", yn, yn, W["lnx"][:, 0:512], ALU.mult)
        self.tt("pool", yn, yn, W["lnx"][:, 512:1024], ALU.add)
        self.tt("dve", y3(ysb), y3(W["vtok"]), b8(st8[:, 5, :]), ALU.mult)
        self.tt("pool", yn, yn, ysb[:], ALU.add)
        self.tt("dve", ysb[:], yn, W["sz"][:], ALU.mult)
        self.dma(self.yc_sc[t0:t0 + 128, :], ysb[:])

    def phase_Q(self, ctx, lo, hi):
        S = self.S
        l, kind, si, T, cond, last = ctx["l"], ctx["kind"], ctx["si"], ctx["T"], ctx["cond"], ctx["last"]
        KaT, Va, ckvT, KbT, Vb, nk, nkt = (ctx[k] for k in ("KaT", "Va", "ckvT", "KbT", "Vb", "nk", "nkt"))
        ps = self.ps
        TQ = min(T, 512)
        nst = TQ // 128
        ntile = T // TQ
        xin, xout = ctx["xin"], ctx["xout"]
        yout = (self.y_prompt[si] if kind == "p" else self.y_sample)

        def qa(name, shape, dt=F32):
            if lo.can(shape, dt):
                return lo.alloc(name, shape, dt)
            return hi.alloc(name, shape, dt)

        hTq = qa("hTq", [128, 8, TQ], BF16)
        yaT = qa("yaT", [64, 8, TQ], BF16)
        ybT = qa("ybT", [64, 8, TQ], BF16)
        ycT = qa("ycT", [128, 4, TQ], BF16)
        mlo_c, mhi_c = lo.mark(), hi.mark()
        for ti in range(ntile):
            tok0 = ti * TQ
            x_ap = xin[tok0:tok0 + TQ, :]
            qaT = qa("qaT", [128, 4, TQ], BF16)
            qbT = qa("qbT", [96, 8, TQ], BF16)
            mlo, mhi = lo.mark(), hi.mark()
            xt = [qa(f"qxt{i}", [128, D]) for i in range(2)]
            xn = [qa(f"qxn{i}", [128, D]) for i in range(2)]
            ss2 = qa("qss", [128, 2]); tm2 = qa("qtm", [128, 2]); rs2 = qa("qrs", [128, 2])
            waq = qa("waq", [128, 8, 512], BF16)
            wbq = qa("wbq", [128, 8, 768], BF16)
            qn = qa("qn", [128, 512]); qsq = qa("qsq", [128, 512]); qb = qa("qbf", [128, 768])
            yct = [qa(f"yct{i}", [128, 512]) for i in range(2)]
            rp = [qa(f"qrp{i}", [128, 192]) for i in range(2)]
            rtmp = qa("qrtmp", [128, 512])
            s8 = qa("qs8", [128, 3, 8])
            self.dma(waq[:], self.wsc[l][:, :, O_AQ:O_AQ + 512])
            self.dma(wbq[:], self.wsc[l][:, :, O_BQ:O_BQ + 768])
            self.compute_hT(ctx, hTq, 0, x_ap, nst, (xt, xn, ss2, tm2, rs2))
            for st in range(nst):
                sc = slice(st * 128, (st + 1) * 128)
                b = st % 2
                if kind == "s":
                    self.dma(rp[b][:], self.c_rope[tok0 + st * 128: tok0 + (st + 1) * 128, :])
                for kc in range(8):
                    self.mm(ps[0][:, :], hTq[:, kc, sc], waq[:, kc, :], start=(kc == 0), stop=(kc == 7))
                self.act(qsq[:], ps[0][:, :], AF.Square)
                self.red(s8[:, 0, :], qsq[:].rearrange("p (h d) -> p h d", h=8))
                self.rstd(s8[:, 1, :], s8[:, 0, :], 64, NORM_EPS, s8[:, 2, :])
                self.tt("dve", qn[:].rearrange("p (g kv d) -> p kv g d", g=4, kv=2),
                        ps[0][:, :].rearrange("p (kv g d) -> p kv g d", kv=2, g=4),
                        s8[:, 1, :].rearrange("p (kv g) -> p kv g", kv=2).unsqueeze(3).broadcast_to([128, 2, 4, 64]), ALU.mult)
                qn3 = qn[:].rearrange("p (h d) -> p h d", h=8)
                self.tt("pool", qn3, qn3, self.qkn_t[:, 0:64].unsqueeze(1).broadcast_to([128, 8, 64]), ALU.mult)
                if kind == "s":
                    self.rope(qn3, rp[b][:, 0:64], rp[b][:, 64:128], 8, 16, rtmp)
                for g in range(4):
                    self.tr(ps[1][:, g * 128:(g + 1) * 128], qn[:, g * 128:(g + 1) * 128])
                self.cp("act", qaT[:, :, sc], ps[1][:, :].rearrange("q (g t) -> q g t", g=4))
                for half in range(2):
                    for kc in range(8):
                        self.mm(ps[2 + half][:, 0:384], hTq[:, kc, sc], wbq[:, kc, half * 384:(half + 1) * 384],
                                start=(kc == 0), stop=(kc == 7))
                self.cp("act", qb[:, 0:384], ps[2][:, 0:384])
                self.cp("dve", qb[:, 384:768], ps[3][:, 0:384])
                qb3 = qb[:].rearrange("p (h c) -> p h c", h=8)
                if kind == "s":
                    self.rope(qb3[:, :, 64:96], rp[b][:, 128:160], rp[b][:, 160:192], 8, 8, rtmp)
                for h in range(8):
                    self.tr(ps[4 + h // 4][0:96, (h % 4) * 128:(h % 4 + 1) * 128], qb[:, h * 96:(h + 1) * 96])
                self.cp("act", qbT[:, 0:4, sc], ps[4][0:96, :].rearrange("q (g t) -> q g t", g=4))
                self.cp("dve", qbT[:, 4:8, sc], ps[5][0:96, :].rearrange("q (g t) -> q g t", g=4))
                self.dma(yct[b][:], self.yc_sc[tok0 + st * 128: tok0 + (st + 1) * 128, :])
                for p in range(4):
                    self.tr(ps[6][:, p * 128:(p + 1) * 128], yct[b][:, p * 128:(p + 1) * 128])
                self.cp("dve", ycT[:, :, sc], ps[6][:, :].rearrange("q (g t) -> q g t", g=4))
            S.barrier()
            lo.release(mlo); hi.release(mhi)
            PT = [qa(f"PT{i}", [128, TQ], BF16) for i in range(3)]
            den = qa("den", [65, TQ]); rden = qa("rden", [64, TQ]); szq = qa("szq", [64, TQ]); gq = qa("gq", [64, TQ])
            wz = [qa(f"wz{i}", [128, 8, 64], BF16) for i in range(2)]

            def finalize(gh, po, zoff, yT, h):
                self.cp("act", den[64:65, :], po[64:65, 0:TQ])
                self.mm(ps[6][0:64, 0:TQ], self.ones[64:65, 0:64], den[64:65, :])
                self.recip(rden[:, :], ps[6][0:64, 0:TQ])
                for kc in range(8):
                    self.mm(ps[7][0:64, 0:TQ], wz[gh % 2][:, kc, :], hTq[:, kc, :], start=(kc == 0), stop=(kc == 7))
                self.act(szq[:, :], ps[7][0:64, 0:TQ], AF.Silu)
                self.tt("pool", gq[:, :], rden[:, :], szq[:, :], ALU.mult)
                self.tt("dve", yT[:, h, :], po[0:64, 0:TQ], gq[:, :], ALU.mult)

            bscale = float(96 ** -0.5)
            steps = [("A", h, kt) for h in range(8) for kt in range(nkt)] + [("B", h, kt) for h in range(8) for kt in range(nkt)]
            N = len(steps)

            def build_kb(h):
                nkb = 0
                for kb in range(0, nk, 512):
                    w = min(512, nk - kb)
                    self.mm(ps[5][0:64, 0:w], self.uk[:, h * 64:(h + 1) * 64], ckvT[:, kb:kb + w])
                    self.cp("dve" if nkb % 2 else "act", KbT[0:64, h % 2, kb:kb + w], ps[5][0:64, 0:w])
                    nkb += 1

            def qk(i):
                mx, h, kt = steps[i]
                pS = ps[i % 3]
                gh = h if mx == "A" else 8 + h
                if kt == 0:
                    zo = O_AZ if mx == "A" else O_BZ
                    self.dma(wz[gh % 2][:], self.wsc[l][:, :, zo + h * 64:zo + (h + 1) * 64])
                    if mx == "A" and h == 7:
                        build_kb(0)
                    elif mx == "B" and h < 7:
                        build_kb(h + 1)
                if mx == "A":
                    kv, g = h // 4, h % 4
                    hp = slice(kv * 64, kv * 64 + 64)
                    self.mm(pS[:, 0:TQ], KaT[hp, kt * 128:(kt + 1) * 128], qaT[hp, g, :])
                else:
                    self.mm(pS[:, 0:TQ], KbT[0:96, h % 2, kt * 128:(kt + 1) * 128], qbT[0:96, h, :])

            def ex(i):
                mx = steps[i][0]
                self.act(PT[i % 3][:, :], ps[i % 3][:, 0:TQ], AF.Exp, scale=(0.125 if mx == "A" else bscale))

            def pv(i):
                mx, h, kt = steps[i]
                gh = h if mx == "A" else 8 + h
                po = ps[3 + gh % 2]
                if mx == "A":
                    vv = Va[:, kt, h // 4, :]
                else:
                    vv = Vb[:, kt, h, :]
                self.mm(po[0:65, 0:TQ], vv, PT[i % 3][:, :], start=(kt == 0), stop=(kt == nkt - 1))
                if kt == nkt - 1:
                    finalize(gh, po, None, yaT if mx == "A" else ybT, h)

            qk(0)
            if N > 1:
                qk(1)
            for i in range(N):
                ex(i)
                if i + 2 < N:
                    qk(i + 2)
                pv(i)
            S.barrier()
            lo.release(mlo_c); hi.release(mhi_c)
            woa = [qa(f"woa{i}", [64, 8, 128], BF16) for i in range(2)]
            wob = [qa(f"wob{i}", [64, 8, 128], BF16) for i in range(2)]
            woc = [qa(f"woc{i}", [128, 4, 128], BF16) for i in range(2)]
            wg = [qa(f"wg{i}", [128, 8, 3, 128], BF16) for i in range(2)]
            wo_ = [qa(f"wout{i}", [128, 8, 128], BF16) for i in range(2)]
            sgm = [qa(f"sgm{i}", [128, TQ]) for i in range(2)]
            mix = qa("mix", [128, TQ]); mtmp = qa("mtmp", [128, TQ])
            mixT = qa("mixT", [128, 8, TQ], BF16)
            oT = [qa(f"oT{i}", [128, TQ]) for i in range(2)]
            xres = qa("xres", [128, nst, D])
            for st in range(nst):
                self.dma(xres[:, st, :], x_ap[st * 128:(st + 1) * 128, :])
            for dc in range(8):
                b = dc % 2
                dcs = slice(dc * 128, (dc + 1) * 128)
                self.dma(woa[b][:], self.wosc[l][0][:, :, dcs])
                self.dma(wob[b][:], self.wosc[l][1][:, :, dcs])
                self.dma(woc[b][:], self.wosc[l][2][:, :, dcs])
                for br in range(3):
                    self.dma(wg[b][:, :, br, :], self.wsc[l][:, :, O_G + br * 1024 + dc * 128:O_G + br * 1024 + (dc + 1) * 128])
                for br in range(3):
                    pp, pg = ps[br], ps[3 + br]
                    if br < 2:
                        wo, yT = (woa, wob)[br][b], (yaT, ybT)[br]
                        for h in range(8):
                            self.mm(pp[:, 0:TQ], wo[:, h, :], yT[:, h, :], start=(h == 0), stop=(h == 7))
                    else:
                        for p in range(4):
                            self.mm(pp[:, 0:TQ], woc[b][:, p, :], ycT[:, p, :], start=(p == 0), stop=(p == 3))
                    for kc in range(8):
                        self.mm(pg[:, 0:TQ], wg[b][:, kc, br, :], hTq[:, kc, :], start=(kc == 0), stop=(kc == 7))
                    sg_ = sgm[br % 2]
                    self.act(sg_[:, :], pg[:, 0:TQ], AF.Sigmoid)
                    if br == 0:
                        self.tt("dve", mix[:, :], pp[:, 0:TQ], sg_[:, :], ALU.mult)
                    else:
                        self.tt("dve", mtmp[:, :], pp[:, 0:TQ], sg_[:, :], ALU.mult)
                        dst = mix[:, :] if br == 1 else mixT[:, dc, :]
                        self.tt("pool", dst, mix[:, :], mtmp[:, :], ALU.add)
            for dc in range(8):
                b = dc % 2
                dcs = slice(dc * 128, (dc + 1) * 128)
                self.dma(wo_[b][:], self.woutsc[l][:, :, dcs])
                po = ps[6 + b]
                for kc in range(8):
                    self.mm(po[:, 0:TQ], wo_[b][:, kc, :], mixT[:, kc, :], start=(kc == 0), stop=(kc == 7))
                gsc = self.modG[:, l, cond, dc:dc + 1]
                if b == 0:
                    self.act(oT[b][:, :], po[:, 0:TQ], AF.Identity, scale=gsc)
                else:
                    self.ts("dve", oT[b][:, :], po[:, 0:TQ], gsc, None, ALU.mult)
                pt = ps[dc % 4]
                for st in range(nst):
                    self.tr(pt[:, st * 128:(st + 1) * 128], oT[b][:, st * 128:(st + 1) * 128])
                self.tt("dve", xres[:, :, dcs], xres[:, :, dcs],
                        pt[:, 0:nst * 128].rearrange("q (s f) -> q s f", s=nst), ALU.add)
            for st in range(nst):
                rows = slice(tok0 + st * 128, tok0 + (st + 1) * 128)
                if not last:
                    self.dma(xout[rows, :], xres[:, st, :])
                else:
                    fs = qa(f"fss{st}", [128, 4])
                    jv = mixT[:, 0:1024 // TQ, :].rearrange("q a t -> q (a t)")
                    self.act(jv, xres[:, st, :], AF.Square, accum=fs[:, 0:1])
                    self.rstd(fs[:, 2:3], fs[:, 0:1], D, NORM_EPS, fs[:, 1:2])
                    self.ts("dve", xres[:, st, :], xres[:, st, :], fs[:, 2:3], None, ALU.mult)
                    self.tt("pool", xres[:, st, :], xres[:, st, :], self.fnw_t[:], ALU.mult)
                    self.dma(yout[rows, :], xres[:, st, :], wk=[("yout", kind, si, tok0, st)])
            S.barrier()
            lo.release(mlo_c); hi.release(mhi_c)


def _in_maps(inputs):
    c = host_constants()
    f = lambda a: np.ascontiguousarray(np.asarray(a, dtype=np.float32))
    g = {k: f(v) for k, v in inputs.items()}
    nb = np.zeros((128, 128), np.float32)
    for l in range(DEPTH):
        nb[l * 32:l * 32 + 8] = g["norm_w"][l].reshape(8, 128)
        nb[l * 32 + 8:l * 32 + 32] = g["b_mod"][l].reshape(24, 128)
    lfm = np.zeros((DEPTH, 48, 128), np.float32)
    for l in range(DEPTH):
        lfm[l, 0:4] = g["c_k_k"][l].reshape(4, 128)
        lfm[l, 4:8] = g["c_k_a"][l].reshape(4, 128)
        lfm[l, 8:12] = g["c_r_k"][l].reshape(4, 128)
        lfm[l, 12:26] = g["c_mu_prev"][l].reshape(14, 128)
        lfm[l, 26:40] = g["c_mu_next"][l].reshape(14, 128)
        lfm[l, 40:48] = g["c_a0"][l].reshape(8, 128)
    qkn = np.concatenate([g["a_qnorm_w"], g["a_knorm_w"], g["b_kvnorm_w"]], axis=1)
    lnx = np.concatenate([g["c_lnx_w"], g["c_lnx_b"]], axis=1)
    shared = {
        "nb_rows": nb, "w_mod": g["w_mod"], "w_in": g["w_in"], "qkn": f(qkn),
        "b_w_uk": g["b_w_uk"], "b_w_uv": g["b_w_uv"], "lfm_rows": lfm, "c_w0": g["c_w0"],
        "c_w_up": f(g["c_w_up"].reshape(DEPTH, 128, 512)), "c_a_up": f(g["c_a_up"].reshape(DEPTH, 128, 512)),
        "lnx": f(lnx), "w_oa": g["w_oa"], "w_ob": g["w_ob"], "w_oc": g["w_oc"], "w_out": g["w_out"],
        "final_norm_w": g["final_norm_w"],
        "c_ident": c["ident"], "c_bank": c["bankmask"], "c_lvl": c["lvlmask"], "c_tri": c["tri"],
        "c_hsel": c["headsel"], "c_rope": c["rope"],
    }
    maps = []
    for i in range(NCORES):
        m = dict(shared)
        m["x_prompt"] = f(g["x_prompt"][i * NPS:(i + 1) * NPS])
        m["x_sample"] = f(g["x_sample"][i])
        m["cache_a_k"] = f(g["cache_a_k"][i].reshape(DEPTH, PAST, 128))
        m["cache_a_v"] = f(g["cache_a_v"][i].reshape(DEPTH, PAST, 128))
        m["cache_b_ckv"] = f(g["cache_b_ckv"][i])
        m["cache_b_krope"] = f(g["cache_b_krope"][i])
        m["state_c_fwd"] = f(g["state_c_fwd"][i])
        m["state_c_bwd"] = f(g["state_c_bwd"][i])
        m["cond"] = f(np.concatenate([g["c_ctx"].reshape(8, 128), g["c"][i].reshape(8, 128)], axis=0))
        maps.append(m)
    return maps


_NC_CACHE = {}


def kernel(**inputs):
    stage = int(os.environ.get("MK_STAGE", "99"))
    if stage not in _NC_CACHE:
        _NC_CACHE[stage] = Builder(stage).build()
    nc = _NC_CACHE[stage]
    maps = _in_maps(inputs)
    res = run_bass_kernel_spmd(nc, maps, core_ids=list(range(NCORES)))
    r = res.results
    cat = lambda name: np.concatenate([np.asarray(x[name]) for x in r], axis=0)
    y_prompt = cat("y_prompt")
    y_sample = np.stack([np.asarray(x["y_sample"]) for x in r], axis=0)
    new_a_k = cat("new_a_k").reshape(32, DEPTH, TP, 2, 64)
    new_a_v = cat("new_a_v").reshape(32, DEPTH, TP, 2, 64)
    new_ckv = cat("new_b_ckv")
    new_kr = cat("new_b_krope")
    new_sf = cat("new_c_state_fwd")
    new_sb = cat("new_c_state_bwd")
    return (y_prompt, y_sample, new_a_k, new_a_v, new_ckv, new_kr, new_sf, new_sb)
```

```python
import os
import contextlib
import numpy as np
import concourse.bass as bass
import concourse.mybir as mybir
from concourse.bass_utils import run_bass_kernel_spmd

F32 = mybir.dt.float32
BF16 = mybir.dt.bfloat16
AF = mybir.ActivationFunctionType
ALU = mybir.AluOpType
AX = mybir.AxisListType

D = 1024
DEPTH = 4
NPS = 4
TP = 256
TS = 4096
PAST = 512
IN_DIM = 8096
NCORES = 8
O_AQ, O_AK, O_AV, O_AZ, O_BQ, O_CKV, O_KR, O_BZ, O_CIN, O_CZ, O_G = 0, 512, 640, 768, 1280, 2048, 2176, 2208, 2720, 4512, 5024
NORM_EPS = 1e-6
GN_EPS = 64e-5
DEC_C = -float(np.exp(-0.5))

EPOCH = 30000
N_EPOCH = {"pe": 16, "dve": 12, "act": 10, "pool": 8}
N_DMA_SEMS = 40
SB_BASE = 16512
SB_LIMIT = 229344
CUT = int(os.environ.get('MK_CUT', '0'))


class Sched:
    def __init__(self, nc):
        self.nc = nc
        self.ops = {e: [] for e in ("pe", "dve", "act", "pool", "sp")}
        self.count = {e: 0 for e in ("pe", "dve", "act", "pool")}
        self.last_w = {}
        self.readers = {}
        self.dma_rr = 0
        self.dma_tot = [0] * N_DMA_SEMS
        self.sems = {}

    def _deps(self, eng, reads, writes):
        d = {}

        def add(tok, raw):
            sk, v, src = tok
            if src == eng and sk[0] != "dma":
                if eng == "pe":
                    return
            if d.get(sk, 0) < v:
                d[sk] = v

        for k in reads:
            w = self.last_w.get(k)
            if w is not None:
                add(w, True)
        for k in writes:
            w = self.last_w.get(k)
            if w is not None:
                add(w, False)
            for r in self.readers.get(k, {}).values():
                add(r, False)
        return d

    def _commit(self, tok, reads, writes):
        sk = tok[0]
        for k in reads:
            self.readers.setdefault(k, {})[sk] = tok
        for k in writes:
            self.last_w[k] = tok
            self.readers[k] = {}

    def op(self, eng, fn, reads=(), writes=()):
        deps = self._deps(eng, reads, writes)
        n = self.count[eng]
        self.count[eng] = n + 1
        ep, v = divmod(n, EPOCH)
        tok = ((eng, ep), v + 1, eng)
        self._commit(tok, reads, writes)
        self.ops[eng].append(("op", fn, deps, tok))
        return tok

    def dma(self, q, fn, reads=(), writes=()):
        deps = self._deps("dmaq", reads, writes)
        i = self.dma_rr
        self.dma_rr = (i + 1) % N_DMA_SEMS
        prev = self.dma_tot[i]
        if prev:
            sk = ("dma", i)
            if deps.get(sk, 0) < prev:
                deps[sk] = prev
        self.dma_tot[i] = prev + 16
        tok = (("dma", i), prev + 16, "dmaq")
        self._commit(tok, reads, writes)
        self.ops[q].append(("dma", fn, deps, tok))
        return tok

    def wait_all(self, eng, toks):
        deps = {}
        for sk, v, _ in toks:
            deps[sk] = max(deps.get(sk, 0), v)
        self.ops[eng].append(("wait", None, deps, None))

    def alloc_sems(self, es):
        nc = self.nc
        for e, ne in N_EPOCH.items():
            for ep in range(ne):
                self.sems[(e, ep)] = es.enter_context(nc.semaphore(f"s_{e}{ep}"))
        for i in range(N_DMA_SEMS):
            self.sems[("dma", i)] = es.enter_context(nc.semaphore(f"s_dma{i}"))

    def barrier(self):
        toks = []
        for e, n in self.count.items():
            if n:
                ep, v = divmod(n - 1, EPOCH)
                toks.append(((e, ep), v + 1, e))
        for i in range(N_DMA_SEMS):
            if self.dma_tot[i]:
                toks.append((("dma", i), self.dma_tot[i], "dmaq"))
        for e in ("pe", "dve", "act", "pool", "sp"):
            self.wait_all(e, toks)

    def emit_block(self):
        nc = self.nc
        sems = self.sems
        for e, ne in N_EPOCH.items():
            assert self.count[e] <= ne * EPOCH, (e, self.count[e])
        with nc.Block() as block:
            def run(engname):
                lst = self.ops[engname]

                def body(eng):
                    waited = {}
                    for kind, fn, deps, tok in lst:
                        for sk, v in deps.items():
                            if waited.get(sk, 0) < v:
                                eng.wait_ge(sems[sk], v)
                                waited[sk] = v
                        if kind == "wait":
                            continue
                        ins = fn(eng)
                        ins.then_inc(sems[tok[0]], 16 if kind == "dma" else 1)
                return body

            block.tensor(run("pe"))
            block.vector(run("dve"))
            block.scalar(run("act"))
            block.gpsimd(run("pool"))
            block.sync(run("sp"))
        self.ops = {e: [] for e in ("pe", "dve", "act", "pool", "sp")}


class Arena:
    cnt = 0

    def __init__(self, nc, base, limit):
        self.nc, self.off, self.limit = nc, base, limit

    def alloc(self, name, shape, dt):
        esz = 4 if dt == F32 else 2
        nb = int(np.prod(shape[1:])) * esz
        nb = (nb + 63) // 64 * 64
        off = self.off
        self.off += nb
        assert self.off <= self.limit, (name, self.off, self.limit)
        Arena.cnt += 1
        return self.nc.alloc_sbuf_tensor_at(f"{name}_{Arena.cnt}", list(shape), dt, offset=off)

    def can(self, shape, dt):
        esz = 4 if dt == F32 else 2
        nb = (int(np.prod(shape[1:])) * esz + 63) // 64 * 64
        return self.off + nb <= self.limit

    def mark(self):
        return self.off

    def release(self, m):
        self.off = m


def _k(x):
    return x if isinstance(x, (str, tuple)) else x.name


def host_constants():
    c = {}
    c["ident"] = np.eye(128, dtype=np.float32)
    s = np.arange(128)[:, None]
    t = np.arange(128)[None, :]
    bank = np.zeros((2, 128, 512), np.float32)
    lvl = np.zeros((2, 7, 128, 128), np.float32)
    tri = np.zeros((2, 128, 3 * 128 + 2), np.float32)
    for d in range(2):
        if d == 0:
            strict = (s < t).astype(np.float32)
            incl = (s <= t).astype(np.float32)
        else:
            strict = (s > t).astype(np.float32)
            incl = (s >= t).astype(np.float32)
        bank[d] = np.concatenate([strict, incl, strict, incl], axis=1)
        for l in range(7):
            b = 1 << l
            tt_ = np.arange(128)[:, None]
            ss_ = np.arange(128)[None, :]
            same = (tt_ // (2 * b)) == (ss_ // (2 * b))
            t_second = (tt_ // b) % 2 == 1
            s_second = (ss_ // b) % 2 == 1
            if d == 0:
                m = same & t_second & (~s_second)
            else:
                m = same & (~t_second) & s_second
            lvl[d, l] = m.astype(np.float32)
        if d == 0:
            G = (s <= t).astype(np.float32) - (s <= 63).astype(np.float32)
            G1 = (s < t).astype(np.float32) - (s <= 63).astype(np.float32)
            Dm = (s > t).astype(np.float32)
            cm = (np.arange(128) <= 63).astype(np.float32)
        else:
            G = (s >= t).astype(np.float32) - (s >= 64).astype(np.float32)
            G1 = (s > t).astype(np.float32) - (s >= 64).astype(np.float32)
            Dm = (s < t).astype(np.float32)
            cm = (np.arange(128) >= 64).astype(np.float32)
        tri[d, :, 0:128] = DEC_C * G
        tri[d, :, 128:256] = DEC_C * G1
        tri[d, :, 256:384] = DEC_C * Dm
        tri[d, :, 384] = DEC_C * cm
        tri[d, :, 385] = DEC_C
    c["bankmask"] = bank
    c["lvlmask"] = lvl
    c["tri"] = tri
    hs = np.zeros((128, 2), np.float32)
    hs[:64, 0] = 1.0
    hs[64:, 1] = 1.0
    c["headsel"] = hs
    tok = np.arange(TS)
    row = (tok // 64).astype(np.float32)
    col = (tok % 64).astype(np.float32)
    tab = np.zeros((TS, 192), np.float32)

    def fill(base_c, base_s, half):
        inv = 10000.0 ** (-np.arange(half, dtype=np.float32) / half)
        for pi, pos in enumerate((row, col)):
            ang = pos[:, None] * inv[None, :]
            co, si = np.cos(ang), np.sin(ang)
            o = pi * 2 * half
            tab[:, base_c + o: base_c + o + half] = co
            tab[:, base_c + o + half: base_c + o + 2 * half] = co
            tab[:, base_s + o: base_s + o + half] = -si
            tab[:, base_s + o + half: base_s + o + 2 * half] = si

    fill(0, 64, 16)
    fill(128, 160, 8)
    c["rope"] = tab
    return c


class Builder:
    def __init__(self, stage=99):
        self.stage = stage
        self.nc = bass.Bass("TRN2", target_bir_lowering=False)
        self.S = Sched(self.nc)
        self.din = {}
        self.dout = {}

    def inp(self, name, shape):
        self.din[name] = self.nc.dram_tensor(name, list(shape), F32, kind="ExternalInput").ap()
        return self.din[name]

    def outp(self, name, shape):
        self.dout[name] = self.nc.dram_tensor(name, list(shape), F32, kind="ExternalOutput").ap()
        return self.dout[name]

    def scratch(self, name, shape, dt):
        return self.nc.dram_tensor(name, list(shape), dt).ap()

    def dma(self, out, in_, rk=None, wk=None, q="sp"):
        r = [_k(in_)] if rk is None else rk
        w = [_k(out)] if wk is None else wk
        self.S.dma(q, lambda e: e.dma_start(out=out, in_=in_), r, w)

    def mm(self, out, lhsT, rhs, start=True, stop=True, xr=()):
        self.S.op("pe", lambda e: e.matmul(out, lhsT, rhs, start=start, stop=stop),
                  [_k(lhsT), _k(rhs)] + [_k(x) for x in xr], [_k(out)])

    def tr(self, out, in_):
        p = in_.shape[0]
        idn = self.ident[0:p, 0:p]
        self.S.op("pe", lambda e: e.transpose(out, in_, idn), [_k(in_), _k(self.ident)], [_k(out)])

    def act(self, out, in_, func, bias=None, scale=None, accum=None):
        r = [_k(in_)]
        kw = {}
        if bias is not None:
            kw["bias"] = bias
            if not isinstance(bias, float):
                r.append(_k(bias))
        if scale is not None:
            kw["scale"] = scale
            if not isinstance(scale, float):
                r.append(_k(scale))
        w = [_k(out)]
        if accum is not None:
            kw["accum_out"] = accum
            w.append(_k(accum))
        self.S.op("act", lambda e: e.activation(out, in_, func, **kw), r, w)

    def tt(self, eng, out, a, b, op):
        self.S.op(eng, lambda e: e.tensor_tensor(out, a, b, op), [_k(a), _k(b)], [_k(out)])

    def ts(self, eng, out, a, s1, s2, op0, op1=None):
        r = [_k(a)] + [_k(s) for s in (s1, s2) if s is not None and not isinstance(s, (float, int))]
        if op1 is None:
            self.S.op(eng, lambda e: e.tensor_scalar(out, a, s1, s2, op0), r, [_k(out)])
        else:
            self.S.op(eng, lambda e: e.tensor_scalar(out, a, s1, s2, op0, op1), r, [_k(out)])

    def stt(self, out, a, scalar, b, op0, op1):
        r = [_k(a), _k(b)] + ([] if isinstance(scalar, (float, int)) else [_k(scalar)])
        self.S.op("dve", lambda e: e.scalar_tensor_tensor(out, a, scalar, b, op0, op1), r, [_k(out)])

    def cp(self, eng, out, in_):
        if eng == "act":
            self.S.op("act", lambda e: e.copy(out, in_), [_k(in_)], [_k(out)])
        else:
            self.S.op(eng, lambda e: e.tensor_copy(out, in_), [_k(in_)], [_k(out)])

    def memset(self, eng, out, val):
        self.S.op(eng, lambda e: e.memset(out, val), [], [_k(out)])

    def red(self, out, in_, op=ALU.add):
        self.S.op("dve", lambda e: e.tensor_reduce(out, in_, AX.X, op), [_k(in_)], [_k(out)])

    def recip(self, out, in_):
        self.S.op("dve", lambda e: e.reciprocal(out, in_), [_k(in_)], [_k(out)])

    def rstd(self, out, ss, n, eps, tmp):
        self.act(tmp, ss, AF.Sqrt, bias=float(eps), scale=1.0 / n)
        self.recip(out, tmp)

    def declare(self):
        i = self.inp
        self.x_prompt = i("x_prompt", [NPS, TP, D])
        self.x_sample = i("x_sample", [TS, D])
        self.cache_a_k = i("cache_a_k", [DEPTH, PAST, 128])
        self.cache_a_v = i("cache_a_v", [DEPTH, PAST, 128])
        self.cache_ckv = i("cache_b_ckv", [DEPTH, PAST, 128])
        self.cache_kr = i("cache_b_krope", [DEPTH, PAST, 32])
        self.state_f = i("state_c_fwd", [DEPTH, 8, 64, 64])
        self.state_b = i("state_c_bwd", [DEPTH, 8, 64, 64])
        self.cond = i("cond", [16, 128])
        self.nb_rows = i("nb_rows", [128, 128])
        self.w_mod = i("w_mod", [DEPTH, D, 3 * D])
        self.w_in = i("w_in", [DEPTH, D, IN_DIM])
        self.qkn = i("qkn", [DEPTH, 256])
        self.b_w_uk = i("b_w_uk", [DEPTH, 128, 512])
        self.b_w_uv = i("b_w_uv", [DEPTH, 128, 512])
        self.lfm_rows = i("lfm_rows", [DEPTH, 48, 128])
        self.c_w0 = i("c_w0", [DEPTH, 2, 512])
        self.c_w_up = i("c_w_up", [DEPTH, 128, 512])
        self.c_a_up = i("c_a_up", [DEPTH, 128, 512])
        self.lnx = i("lnx", [DEPTH, 1024])
        self.w_oa = i("w_oa", [DEPTH, 512, D])
        self.w_ob = i("w_ob", [DEPTH, 512, D])
        self.w_oc = i("w_oc", [DEPTH, 512, D])
        self.w_out = i("w_out", [DEPTH, D, D])
        self.fnw = i("final_norm_w", [D])
        self.c_ident = i("c_ident", [128, 128])
        self.c_bank = i("c_bank", [2, 128, 512])
        self.c_lvl = i("c_lvl", [2, 7, 128, 128])
        self.c_tri = i("c_tri", [2, 128, 386])
        self.c_hsel = i("c_hsel", [128, 2])
        self.c_rope = i("c_rope", [TS, 192])
        o = self.outp
        self.y_prompt = o("y_prompt", [NPS, TP, D])
        self.y_sample = o("y_sample", [TS, D])
        self.new_a_k = o("new_a_k", [NPS, DEPTH, TP, 128])
        self.new_a_v = o("new_a_v", [NPS, DEPTH, TP, 128])
        self.new_ckv = o("new_b_ckv", [NPS, DEPTH, TP, 128])
        self.new_kr = o("new_b_krope", [NPS, DEPTH, TP, 32])
        self.new_sf = o("new_c_state_fwd", [NPS, DEPTH, 8, 64, 64])
        self.new_sb = o("new_c_state_bwd", [NPS, DEPTH, 8, 64, 64])
        sc = self.scratch
        self.wsc = [sc(f"wsc{l}", [128, 8, IN_DIM], BF16) for l in range(DEPTH)]
        self.wosc = [[sc(f"wo{l}_{b}", ([64, 8, D] if b < 2 else [128, 4, D]), BF16) for b in range(3)] for l in range(DEPTH)]
        self.woutsc = [sc(f"wout{l}", [128, 8, D], BF16) for l in range(DEPTH)]
        self.xs_p = sc("xs_p", [NPS, TP, D], F32)
        self.xs_s = sc("xs_s", [TS, D], F32)
        self.yb_sc = sc("yb_sc", [TS, 512], F32)
        self.yc_sc = sc("yc_sc", [TS, 512], F32)

    def build(self):
        nc, S = self.nc, self.S
        self.declare()
        with contextlib.ExitStack() as es:
            S.alloc_sems(es)
            self.ps = [es.enter_context(nc.psum_tensor(f"ps{i}", [128, 512], F32)) for i in range(8)]
            self.carena = Arena(nc, SB_BASE, SB_BASE + 24576)
            self.arena = Arena(nc, SB_BASE + 24576, SB_LIMIT)
            self.prologue()
            seqs = [("p", s) for s in range(NPS)] + [("s", 0)]
            if self.stage < 50:
                seqs = seqs[:NPS]
            nlayers = DEPTH if self.stage >= 40 else 1
            for l in range(nlayers):
                self.layer_prologue(l)
                for kind, si in seqs:
                    self.run_seq(l, kind, si)
            S.barrier()
            S.emit_block()
        return nc

    def prologue(self):
        A = self.carena
        S = self.S
        self.ident = A.alloc("ident", [128, 128], F32)
        self.identb = A.alloc("identb", [128, 128], BF16)
        self.ones = A.alloc("ones", [128, 128], F32)
        self.onesb = A.alloc("onesb", [128, 128], BF16)
        self.bank = A.alloc("bank", [128, 2, 512], BF16)
        self.lvl = A.alloc("lvl", [128, 14, 128], BF16)
        self.tri = A.alloc("tri", [128, 2, 386], F32)
        self.hsel = A.alloc("hsel", [128, 2], F32)
        self.modA = A.alloc("modA", [128, DEPTH, 2, 8], F32)
        self.modB = A.alloc("modB", [128, DEPTH, 2, 8], F32)
        self.modG = A.alloc("modG", [128, DEPTH, 2, 8], F32)
        self.qkn_t = A.alloc("qkn_t", [128, 256], F32)
        self.uk = A.alloc("uk", [128, 512], BF16)
        self.uv = A.alloc("uv", [128, 512], BF16)
        self.wup = A.alloc("wup", [128, 512], BF16)
        self.aup = A.alloc("aup", [128, 512], BF16)
        self.w0row = A.alloc("w0row", [1, 2, 512], BF16)
        self.lfm = A.alloc("lfm", [128, 48], F32)
        self.c0fm = A.alloc("c0fm", [128, 14], F32)
        self.bonb = A.alloc("bonb", [128, 32, 8], F32)
        self.fnw_t = A.alloc("fnw_t", [128, D], F32)
        self.bones = A.alloc("bones", [128, 128], F32)
        self.omka = A.alloc("omka", [128, 4], F32)
        ar = self.arena
        m0 = ar.mark()
        st = ar.alloc("st", [128, 128], F32)
        st2 = ar.alloc("st2", [128, 512], F32)
        lv = ar.alloc("lv", [128, 14, 128], F32)
        cst = ar.alloc("cst", [16, 128], F32)
        scond = ar.alloc("scond", [128, 16], F32)
        nbfm = ar.alloc("nbfm", [128, 128], F32)
        mods = ar.alloc("mods", [128, DEPTH, 24, 2], F32)
        wst = [ar.alloc(f"wst{i}", [128, 8, 512], F32) for i in range(2)]
        self.dma(self.ident[:], self.c_ident)
        self.cp("dve", self.identb[:], self.ident[:])
        self.memset("pool", self.ones[:], 1.0)
        self.memset("pool", self.onesb[:], 1.0)
        self.memset("pool", self.bones[:], 0.0)
        self.memset("pool", self.bones[0:64, 0:64], 1.0)
        self.memset("pool", self.bones[64:128, 64:128], 1.0)
        for d in range(2):
            self.dma(st2[:], self.c_bank[d])
            self.cp("dve", self.bank[:, d, :], st2[:])
        for d in range(2):
            self.dma(lv[:, 0:7, :], self.c_lvl[d].rearrange("l t s -> t l s"))
            self.cp("dve", self.lvl[:, d * 7:(d + 1) * 7, :], lv[:, 0:7, :])
        self.dma(self.tri[:], self.c_tri.rearrange("d s c -> s d c"))
        self.dma(self.hsel[:], self.c_hsel)
        self.dma(self.fnw_t[:], self.fnw.partition_broadcast(128))
        self.dma(cst[:], self.cond)
        self.tr(self.ps[0][:, 0:16], cst[:])
        self.act(scond[:], self.ps[0][:, 0:16], AF.Silu)
        self.dma(st[:], self.nb_rows)
        self.tr(self.ps[1][:, 0:128], st[:])
        self.cp("dve", nbfm[:], self.ps[1][:, 0:128])
        nwm = 0
        for l in range(DEPTH):
            pm = self.ps[2 + (l % 2)]
            for j in range(6):
                w = wst[nwm % 2]
                nwm += 1
                self.dma(w[:], self.w_mod[l].rearrange("(kc p) n -> p kc n", p=128)[:, :, j * 512:(j + 1) * 512])
                for f in range(4):
                    fc = j * 4 + f
                    for kc in range(8):
                        self.mm(pm[:, fc * 2:fc * 2 + 2], w[:, kc, f * 128:(f + 1) * 128],
                                scond[:, kc:16:8], start=(kc == 0), stop=(kc == 7))
            self.tt("dve", mods[:, l, :, :], pm[:, 0:48].rearrange("p (a b) -> p a b", b=2),
                    nbfm[:, l * 32 + 8:l * 32 + 32].unsqueeze(2).broadcast_to([128, 24, 2]), ALU.add)
            for c in range(2):
                self.ts("dve", self.modA[:, l, c, :], mods[:, l, 8:16, c], 1.0, None, ALU.add)
                self.tt("dve", self.modA[:, l, c, :], self.modA[:, l, c, :], nbfm[:, l * 32:l * 32 + 8], ALU.mult)
                self.cp("dve", self.modB[:, l, c, :], mods[:, l, 0:8, c])
                self.cp("dve", self.modG[:, l, c, :], mods[:, l, 16:24, c])
        S.barrier()
        S.emit_block()
        ar.release(m0)

    def layer_prologue(self, l):
        S, ar = self.S, self.arena
        m0 = ar.mark()
        wst = [ar.alloc(f"cst{i}", [128, 8, 512], F32) for i in range(2)]
        wbf = [ar.alloc(f"cbf{i}", [128, 8, 512], BF16) for i in range(2)]
        st = ar.alloc("lst", [128, 512], F32)
        st48 = ar.alloc("lst48", [48, 128], F32)
        n = 0
        engs = ["dve", "pool", "act"]
        src = self.w_in[l].rearrange("(kc p) n -> p kc n", p=128)
        for c0 in range(0, IN_DIM, 512):
            w = min(512, IN_DIM - c0)
            a, b = wst[n % 2], wbf[n % 2]
            self.dma(a[:, :, 0:w], src[:, :, c0:c0 + w])
            self.cp(engs[n % 3], b[:, :, 0:w], a[:, :, 0:w])
            self.dma(self.wsc[l][:, :, c0:c0 + w], b[:, :, 0:w])
            n += 1
        src = self.w_out[l].rearrange("(kc p) n -> p kc n", p=128)
        for c0 in range(0, D, 512):
            a, b = wst[n % 2], wbf[n % 2]
            self.dma(a[:], src[:, :, c0:c0 + 512])
            self.cp(engs[n % 3], b[:], a[:])
            self.dma(self.woutsc[l][:, :, c0:c0 + 512], b[:])
            n += 1
        for bi, wsrc in enumerate((self.w_oa, self.w_ob, self.w_oc)):
            if bi < 2:
                src = wsrc[l].rearrange("(h p) n -> p h n", p=64)
                np_, nh_ = 64, 8
            else:
                src = wsrc[l].rearrange("(kc p) n -> p kc n", p=128)
                np_, nh_ = 128, 4
            for c0 in range(0, D, 512):
                a, b = wst[n % 2], wbf[n % 2]
                self.dma(a[0:np_, 0:nh_, :], src[:, :, c0:c0 + 512])
                self.cp(engs[n % 3], b[0:np_, 0:nh_, :], a[0:np_, 0:nh_, :])
                self.dma(self.wosc[l][bi][:, :, c0:c0 + 512], b[0:np_, 0:nh_, :])
                n += 1
        for dst, srcw in ((self.uk, self.b_w_uk), (self.uv, self.b_w_uv), (self.wup, self.c_w_up), (self.aup, self.c_a_up)):
            self.dma(st[:], srcw[l])
            self.cp("dve", dst[:], st[:])
        self.dma(st[0:1, 0:512], self.c_w0[l, 0:1, :])
        self.cp("dve", self.w0row[:, 0, :], st[0:1, 0:512])
        self.dma(st[0:1, 0:512], self.c_w0[l, 1:2, :])
        self.cp("dve", self.w0row[:, 1, :], st[0:1, 0:512])
        self.dma(self.qkn_t[:], self.qkn[l].partition_broadcast(128))
        self.dma(st48[:], self.lfm_rows[l])
        self.tr(self.ps[0][:, 0:48], st48[:])
        self.cp("dve", self.lfm[:], self.ps[0][:, 0:48])
        self.tt("dve", self.c0fm[:], self.lfm[:, 12:26], self.lfm[:, 26:40], ALU.add)
        self.ts("dve", self.c0fm[:], self.c0fm[:], -1.0, 1.0, ALU.mult, ALU.add)
        self.ts("dve", self.omka[:], self.lfm[:, 4:8], -1.0, 1.0, ALU.mult, ALU.add)
        S.barrier()
        S.emit_block()
        ar.release(m0)

    def run_seq(self, l, kind, si):
        S, ar = self.S, self.arena
        T = TP if kind == "p" else TS
        cond = 0 if kind == "p" else 1
        last = (l == DEPTH - 1)
        if l == 0:
            xin = self.x_prompt[si] if kind == "p" else self.x_sample
        else:
            xin = self.xs_p[si] if kind == "p" else self.xs_s
        xout = self.xs_p[si] if kind == "p" else self.xs_s
        ctx = dict(l=l, kind=kind, si=si, T=T, cond=cond, xin=xin, xout=xout, last=last)
        m0 = ar.mark()
        hT = ar.alloc("hT", [128, 8, T + 2], BF16)
        mH = ar.mark()
        ctx["hT"] = hT
        self.phase_H(ctx)
        S.barrier()
        S.emit_block()
        if self.stage >= 20:
            self.phase_rwkv(ctx)
        self.phase_KV(ctx)
        S.barrier()
        S.emit_block()
        if self.stage >= 30:
            lo = Arena(self.nc, m0, mH)
            self.phase_Q(ctx, lo, ar)
            S.barrier()
            S.emit_block()
        ar.release(m0)

    def compute_hT(self, ctx, hT, col0, x_ap, nst, work):
        l, cond = ctx["l"], ctx["cond"]
        xt, xn, ss, tmp, rs = work
        for st in range(nst):
            b = st % 2
            self.dma(xt[b][:], x_ap[st * 128:(st + 1) * 128, :])
            self.act(xn[b][:], xt[b][:], AF.Square, accum=ss[:, b:b + 1])
            self.rstd(rs[:, b:b + 1], ss[:, b:b + 1], D, NORM_EPS, tmp[:, b:b + 1])
            self.ts("dve", xn[b][:], xt[b][:], rs[:, b:b + 1], None, ALU.mult)
            for half in range(2):
                pb = self.ps[(2 * st + half) % 4]
                for q in range(4):
                    kc = half * 4 + q
                    self.tr(pb[:, q * 128:(q + 1) * 128], xn[b][:, kc * 128:(kc + 1) * 128])
                for q in range(4):
                    kc = half * 4 + q
                    dst = hT[:, kc, col0 + st * 128: col0 + (st + 1) * 128]
                    if q % 2 == 0:
                        self.act(dst, pb[:, q * 128:(q + 1) * 128], AF.Identity,
                                 bias=self.modB[:, l, cond, kc:kc + 1], scale=self.modA[:, l, cond, kc:kc + 1])
                    else:
                        self.ts("dve", dst, pb[:, q * 128:(q + 1) * 128], self.modA[:, l, cond, kc:kc + 1],
                                self.modB[:, l, cond, kc:kc + 1], ALU.mult, ALU.add)

    def hT_work(self, ar):
        xt = [ar.alloc(f"xt{i}", [128, D], F32) for i in range(2)]
        xn = [ar.alloc(f"xn{i}", [128, D], F32) for i in range(2)]
        ss = ar.alloc("ss", [128, 2], F32)
        tmp = ar.alloc("tmpr", [128, 2], F32)
        rs = ar.alloc("rs", [128, 2], F32)
        return (xt, xn, ss, tmp, rs)

    def phase_H(self, ctx):
        ar = self.arena
        T, hT = ctx["T"], ctx["hT"]
        m = ar.mark()
        work = self.hT_work(ar)
        self.memset("pool", hT[:, :, 0:1], 0.0)
        self.memset("pool", hT[:, :, T + 1:T + 2], 0.0)
        self.compute_hT(ctx, hT, 1, ctx["xin"], T // 128, work)
        ar.release(m)

    def phase_KV(self, ctx):
        ar = self.arena
        l, kind, si, T, hT = ctx["l"], ctx["kind"], ctx["si"], ctx["T"], ctx["hT"]
        nk = T + (PAST if kind == "s" else 0)
        nkt = nk // 128
        KaT = ar.alloc("KaT", [128, nk], BF16)
        Vaf = ar.alloc("Va", [128, nkt * 130 + 64], BF16)
        ckvT = ar.alloc("ckvT", [128, nk], BF16)
        KbT = ar.alloc("KbT", [128, 2, nk], BF16)
        Vbf = ar.alloc("Vb", [128, nkt * 520 + 64], BF16)
        Va = Vaf[:, 0:nkt * 130].rearrange("p (t k c) -> p t k c", k=2, c=65)
        Vb = Vbf[:, 0:nkt * 520].rearrange("p (t k c) -> p t k c", k=8, c=65)
        ctx.update(Vaf=Vaf, Vbf=Vbf)
        ctx.update(KaT=KaT, Va=Va, ckvT=ckvT, KbT=KbT, Vb=Vb, nk=nk, nkt=nkt)
        m = ar.mark()
        wkv = ar.alloc("wkv", [128, 8, 416], BF16)
        kn = [ar.alloc(f"kn{i}", [128, 128], F32) for i in range(2)]
        vf = [ar.alloc(f"vf{i}", [128, 128], F32) for i in range(2)]
        cn = [ar.alloc(f"cn{i}", [128, 128], F32) for i in range(2)]
        krf = [ar.alloc(f"krf{i}", [128, 32], F32) for i in range(2)]
        junk = ar.alloc("kvjunk", [128, 128], F32)
        ss = ar.alloc("kvss", [128, 4], F32)
        tm = ar.alloc("kvtm", [128, 4], F32)
        rs = ar.alloc("kvrs", [128, 4], F32)
        rp = [ar.alloc(f"kvrp{i}", [128, 192], F32) for i in range(2)]
        rtmp = ar.alloc("kvrtmp", [128, 128], F32)
        self.memset("pool", ss[:], 0.0)
        self.memset("pool", Vaf[:], 0.0)
        self.memset("pool", Vbf[:, nkt * 520:nkt * 520 + 64], 0.0)
        self.memset("pool", KbT[:], 0.0)
        self.memset("pool", Va[:, :, :, 64:65], 1.0)
        self.memset("pool", Vb[:, :, :, 64:65], 1.0)
        self.dma(wkv[:, :, 0:256], self.wsc[l][:, :, O_AK:O_AK + 256])
        self.dma(wkv[:, :, 256:416], self.wsc[l][:, :, O_CKV:O_CKV + 160])
        qk = self.qkn_t
        for st in range(nkt):
            b = st % 2
            new = st < T // 128
            pA, pB, pT, pV = self.ps[0 + 4 * b], self.ps[1 + 4 * b], self.ps[2 + 4 * b], self.ps[3 + 4 * b]
            if new:
                hs = hT[:, :, 1 + st * 128: 1 + (st + 1) * 128]
                for kc in range(8):
                    self.mm(pA[:, 0:256], hs[:, kc, :], wkv[:, kc, 0:256], start=(kc == 0), stop=(kc == 7))
                for kc in range(8):
                    self.mm(pB[:, 0:160], hs[:, kc, :], wkv[:, kc, 256:416], start=(kc == 0), stop=(kc == 7))
                for h in range(2):
                    self.act(junk[:, 0:64], pA[:, h * 64:(h + 1) * 64], AF.Square, accum=ss[:, h:h + 1])
                self.rstd(rs[:, 0:2], ss[:, 0:2], 64, NORM_EPS, tm[:, 0:2])
                self.tt("dve", kn[b][:].rearrange("p (a c) -> p a c", a=2), pA[:, 0:128].rearrange("p (a c) -> p a c", a=2),
                        rs[:, 0:2].unsqueeze(2).broadcast_to([128, 2, 64]), ALU.mult)
                self.tt("pool", kn[b][:].rearrange("p (a c) -> p a c", a=2), kn[b][:].rearrange("p (a c) -> p a c", a=2),
                        qk[:, 64:128].unsqueeze(1).broadcast_to([128, 2, 64]), ALU.mult)
                self.cp("act", vf[b][:], pA[:, 128:256])
                self.act(junk[:], pB[:, 0:128], AF.Square, accum=ss[:, 2:3])
                self.rstd(rs[:, 2:3], ss[:, 2:3], 128, NORM_EPS, tm[:, 2:3])
                self.ts("dve", cn[b][:], pB[:, 0:128], rs[:, 2:3], None, ALU.mult)
                self.tt("pool", cn[b][:], cn[b][:], qk[:, 128:256], ALU.mult)
                self.cp("act", krf[b][:], pB[:, 128:160])
                if kind == "p":
                    self.dma(self.new_a_k[si, l, st * 128:(st + 1) * 128, :], kn[b][:], wk=[("nak", si, l, st)])
                    self.dma(self.new_a_v[si, l, st * 128:(st + 1) * 128, :], vf[b][:], wk=[("nav", si, l, st)])
                    self.dma(self.new_ckv[si, l, st * 128:(st + 1) * 128, :], cn[b][:], wk=[("nck", si, l, st)])
                    self.dma(self.new_kr[si, l, st * 128:(st + 1) * 128, :], krf[b][:], wk=[("nkr", si, l, st)])
                else:
                    self.dma(rp[b][:], self.c_rope[st * 128:(st + 1) * 128, :])
                    self.rope(kn[b][:].rearrange("p (h c) -> p h c", h=2), rp[b][:, 0:64], rp[b][:, 64:128], 2, 16, rtmp)
                    self.rope(krf[b][:].rearrange("p (h c) -> p h c", h=1), rp[b][:, 128:160], rp[b][:, 160:192], 1, 8, rtmp)
            else:
                c0 = (st - T // 128) * 128
                self.dma(kn[b][:], self.cache_a_k[l, c0:c0 + 128, :])
                self.dma(vf[b][:], self.cache_a_v[l, c0:c0 + 128, :])
                self.dma(cn[b][:], self.cache_ckv[l, c0:c0 + 128, :])
                self.dma(krf[b][:], self.cache_kr[l, c0:c0 + 128, :])
            ks = slice(st * 128, (st + 1) * 128)
            self.tr(pT[:, 0:128], kn[b][:])
            self.cp("act", KaT[:, ks], pT[:, 0:128])
            self.tr(pT[:, 128:256], cn[b][:])
            self.cp("dve", ckvT[:, ks], pT[:, 128:256])
            self.mm(pT[64:96, 256:384], krf[b][:], self.ident[:], start=True, stop=True)
            self.cp("act", KbT[64:96, 0, ks], pT[64:96, 256:384])
            self.cp("pool", KbT[64:96, 1, ks], KbT[64:96, 0, ks])
            self.cp("pool", Va[:, st, :, 0:64], vf[b][:].rearrange("p (a c) -> p a c", a=2))
            self.mm(pV[:, :], ckvT[:, ks], self.uv[:], start=True, stop=True)
            self.cp("act" if st % 2 else "dve", Vb[:, st, :, 0:64], pV[:, :].rearrange("p (a c) -> p a c", a=8))
        ar.release(m)

    def rope(self, x3, cc, ss_, nh, half, tmp):
        w = 4 * half
        t3 = tmp[:, 0:nh * w].rearrange("p (h c) -> p h c", h=nh)
        for a in range(2):
            for hb in range(2):
                o = a * 2 * half + hb * half
                o2 = a * 2 * half + (1 - hb) * half
                self.tt("pool", t3[:, :, o:o + half], x3[:, :, o2:o2 + half],
                        ss_[:, o:o + half].unsqueeze(1).broadcast_to([128, nh, half]), ALU.mult)
        self.tt("dve", x3, x3, cc.unsqueeze(1).broadcast_to([128, nh, w]), ALU.mult)
        self.tt("dve", x3, x3, t3, ALU.add)

    def phase_rwkv(self, ctx):
        S, ar = self.S, self.arena
        l, kind, si, T, hT = ctx["l"], ctx["kind"], ctx["si"], ctx["T"], ctx["hT"]
        nsb = T // 256
        m = ar.mark()
        W = {}
        a = lambda n, shp, dt=F32: W.__setitem__(n, ar.alloc(n, shp, dt))
        a("wc", [128, 8, 1792], BF16)
        a("wcz", [128, 8, 512], BF16)
        a("lnx", [128, 1024])
        for n in ("rT", "kT", "vT"):
            a(n, [128, 4, 256])
        for n in ("kq", "sq", "tmpf", "kk", "bT", "kdT", "asg", "u"):
            a(n, [128, 4, 128])
        for n in ("sg", "Ep", "Em", "E1", "ED", "ybt"):
            a(n, [128, 512])
        W["ysb"] = W["Ep"]; W["sz"] = W["Em"]; W["vtok"] = W["E1"]
        a("t1a", [128, 256]); a("t1b", [128, 256]); a("wdT", [128, 256]); a("adT", [128, 256])
        a("tw", [128, 256], BF16); a("adb", [128, 256], BF16)
        a("OPS", [128, 4, 4, 128], BF16)
        a("Bt", [128, 512], BF16); a("Kt", [128, 512], BF16); a("Vt", [128, 512], BF16)
        a("AT", [128, 8, 512], BF16); a("ATr", [128, 512], BF16)
        a("Tm", [128, 8, 128], BF16); a("Zm", [128, 8, 128], BF16); a("Qm", [128, 8, 128], BF16)
        a("S", [128, 4, 64]); a("S0s", [128, 4, 64], BF16); a("emwl", [128, 4, 2])
        a("Xb", [128, 512], BF16); a("Ub", [128, 512], BF16)
        a("st8", [128, 6, 8]); a("sti", [64, 4, 128])
        for c0 in range(0, 1792, 448):
            self.dma(W["wc"][:, :, c0:c0 + 448], self.wsc[l][:, :, O_CIN + c0:O_CIN + c0 + 448])
        self.dma(W["wcz"][:], self.wsc[l][:, :, O_CZ:O_CZ + 512])
        self.dma(W["lnx"][:], self.lnx[l].partition_broadcast(128))
        for d in (1, 0):
            Sst = W["S"]
            if kind == "p":
                self.memset("pool", Sst[:], 0.0)
            else:
                src = (self.state_f if d == 0 else self.state_b)[l]
                self.dma(W["sti"][:].rearrange("i p (a j) -> i p a j", a=2), src.rearrange("(p a) i j -> i p a j", a=2))
                for p in range(4):
                    self.tr(self.ps[0][:, p * 64:(p + 1) * 64], W["sti"][:, p, :])
                self.cp("dve", Sst[:].rearrange("q p i -> q (p i)"), self.ps[0][:, 0:256])
            order = range(nsb - 1, -1, -1) if d == 1 else range(nsb)
            for sb in order:
                self.rwkv_super(ctx, d, sb, W)
            if kind == "p":
                dst = (self.new_sf if d == 0 else self.new_sb)[si, l]
                for p in range(4):
                    self.tr(self.ps[0][0:64, p * 128:(p + 1) * 128], Sst[:, p, :])
                self.cp("dve", W["sti"][:].rearrange("i p c -> i (p c)"), self.ps[0][0:64, :])
                self.dma(dst.rearrange("(p a) i j -> i p a j", a=2), W["sti"][:].rearrange("i p (a j) -> i p a j", a=2),
                         wk=[("nst", d, si, l)])
            S.barrier()
            S.emit_block()
        ar.release(m)

    def rwkv_super(self, ctx, d, sb, W):
        hT = ctx["hT"]
        ps, lfm = self.ps, self.lfm
        ts0 = sb * 256
        rT, kT, vT = W["rT"], W["kT"], W["vT"]
        for ci in range(14):
            pb = ps[ci % 2]
            for kc in range(8):
                self.mm(pb[:, 0:258], W["wc"][:, kc, ci * 128:(ci + 1) * 128], hT[:, kc, ts0:ts0 + 258],
                        start=(kc == 0), stop=(kc == 7))
            if ci < 12:
                dst = (rT, kT, vT)[ci // 4][:, ci % 4, :]
            else:
                dst = (W["wdT"], W["adT"])[ci - 12][:]
            t1 = W["t1a" if ci % 2 == 0 else "t1b"]
            self.act(t1[:], pb[:, 1:257], AF.Identity, scale=self.c0fm[:, ci:ci + 1])
            self.stt(t1[:], pb[:, 0:256], lfm[:, 12 + ci:13 + ci], t1[:], ALU.mult, ALU.add)
            self.stt(dst, pb[:, 2:258], lfm[:, 26 + ci:27 + ci], t1[:], ALU.mult, ALU.add)
        self.act(W["tw"][:], W["wdT"][:], AF.Tanh)
        self.cp("pool", W["adb"][:], W["adT"][:])
        for blk in ((1, 0) if d == 1 else (0, 1)):
            self.rwkv_block(ctx, d, sb * 2 + blk, blk, W)

    def rwkv_block(self, ctx, d, bi, blk, W):
        l, kind, T, hT = ctx["l"], ctx["kind"], ctx["T"], ctx["hT"]
        ps = self.ps
        t0 = bi * 128
        bs = slice(blk * 128, (blk + 1) * 128)
        lfm = self.lfm
        bc3 = lambda ap, n: ap.unsqueeze(2).broadcast_to([128, ap.shape[1], n])
        v3 = lambda t: t[:].rearrange("q (p c) -> q p c", p=4)
        rT, kT, vT = W["rT"][:, :, bs], W["kT"][:, :, bs], W["vT"][:, :, bs]
        tw, adb = W["tw"][:, bs], W["adb"][:, bs]
        if d == 0:
            self.dma(W["ybt"][:], self.yb_sc[t0:t0 + 128, :])
        hd = slice(d * 64, (d + 1) * 64)
        self.mm(ps[2][:, :], tw[hd, :], self.wup[hd, :], start=True, stop=False)
        self.mm(ps[2][:, :], self.onesb[0:1, 0:128], self.w0row[0:1, d, :], start=False, stop=True)
        self.act(W["sg"][:], ps[2][:, :], AF.Sigmoid)
        for p in range(4):
            self.mm(ps[3][:, p * 128:(p + 1) * 128], self.aup[hd, p * 128:(p + 1) * 128], adb[hd, :])
        for p in range(4):
            self.act(W["asg"][:, p, :], ps[3][:, p * 128:(p + 1) * 128], AF.Sigmoid, bias=lfm[:, 40 + d * 4 + p:41 + d * 4 + p])
        self.tt("pool", W["kq"][:], kT, bc3(lfm[:, 0:4], 128), ALU.mult)
        self.tt("pool", W["sq"][:], W["kq"][:], W["kq"][:], ALU.mult)
        self.mm(ps[4][:, :], self.bones[:], W["sq"][:].rearrange("q p c -> q (p c)"))
        self.act(W["tmpf"][:].rearrange("q p c -> q (p c)"), ps[4][:, :], AF.Sqrt)
        self.ts("dve", W["tmpf"][:], W["tmpf"][:], 1e-12, None, ALU.max)
        self.recip(W["tmpf"][:], W["tmpf"][:])
        self.tt("pool", W["kk"][:], W["kq"][:], W["tmpf"][:], ALU.mult)
        self.tt("pool", W["bT"][:], W["kk"][:], W["asg"][:], ALU.mult)
        self.tt("pool", W["u"][:], W["asg"][:], bc3(lfm[:, 4:8], 128), ALU.mult)
        self.tt("pool", W["u"][:], W["u"][:], bc3(self.omka[:, 0:4], 128), ALU.add)
        self.tt("pool", W["kdT"][:], kT, W["u"][:], ALU.mult)
        if CUT == 1:
            return
        tri = self.tri
        first, lastt = (0, 127) if d == 0 else (127, 0)
        for p in range(4):
            pc = ps[5 + p // 2]
            self.mm(pc[:, (p % 2) * 256:(p % 2) * 256 + 256], W["sg"][:, p * 128:(p + 1) * 128], tri[:, d, 0:256])
        self.mm(ps[7][:, :], tri[:, d, 256:384], W["sg"][:])
        emwl = W["emwl"]
        for hf in range(2):
            pc = ps[5 + hf]
            pv_ = pc[:, :].rearrange("q (p k t) -> q p k t", p=2, k=2)
            hs_ = slice(hf * 256, (hf + 1) * 256)
            self.act(v3(W["Ep"])[:, hf * 2:hf * 2 + 2, :], pv_[:, :, 0, :], AF.Exp)
            self.act(v3(W["Em"])[:, hf * 2:hf * 2 + 2, :], pv_[:, :, 0, :], AF.Exp, scale=-1.0)
            self.act(v3(W["E1"])[:, hf * 2:hf * 2 + 2, :], pv_[:, :, 1, :], AF.Exp)
            self.act(emwl[:, hf * 2:hf * 2 + 2, 0], pv_[:, :, 1, first], AF.Exp, scale=-1.0)
        self.act(W["ED"][:], ps[7][:, :], AF.Exp)
        self.tt("dve", emwl[:, :, 1], emwl[:, :, 0], v3(W["Ep"])[:, :, lastt], ALU.mult)
        OPS = W["OPS"]
        self.stt(OPS[:, :, 0, :], W["kk"][:], -1.0, v3(W["E1"]), ALU.mult, ALU.mult)
        self.tt("pool", OPS[:, :, 1, :], rT, v3(W["Ep"]), ALU.mult)
        self.tt("pool", OPS[:, :, 2, :], W["bT"][:], v3(W["Em"]), ALU.mult)
        self.tt("dve", OPS[:, :, 3, :], W["kdT"][:], v3(W["Em"]), ALU.mult)
        prod = W["kq"]
        self.tt("pool", prod[:], rT, W["kdT"][:], ALU.mult)
        self.tt("pool", prod[:], prod[:], bc3(lfm[:, 8:12], 128), ALU.mult)
        if CUT == 2:
            return
        for p in range(4):
            self.tr(ps[5][:, p * 128:(p + 1) * 128], W["bT"][:, p, :])
        self.tt("dve", W["Bt"][:], ps[5][:, :], W["ED"][:], ALU.mult)
        for p in range(4):
            self.tr(ps[6][:, p * 128:(p + 1) * 128], W["kdT"][:, p, :])
        self.tt("dve", W["Kt"][:], ps[6][:, :], W["ED"][:], ALU.mult)
        for p in range(4):
            self.tr(ps[7][:, p * 128:(p + 1) * 128], vT[:, p, :])
        self.cp("act", W["Vt"][:], ps[7][:, :])
        if d == 0:
            self.cp("act", W["vtok"][:], ps[7][:, :])
        AT = W["AT"]
        for h in range(8):
            p, h2 = h // 2, h % 2
            hp = slice(h2 * 64, h2 * 64 + 64)
            pa = ps[h % 2]
            self.mm(pa[:, 0:256], OPS[hp, p, 2, :], OPS[hp, p, 0:2, :])
            self.mm(pa[:, 256:512], OPS[hp, p, 3, :], OPS[hp, p, 0:2, :])
            if h % 2 == 0:
                self.tt("dve", AT[:, h, :], pa[:, :], self.bank[:, d, :], ALU.mult)
            else:
                self.cp("act", W["ATr"][:], pa[:, :])
                self.tt("pool", AT[:, h, :], W["ATr"][:], self.bank[:, d, :], ALU.mult)
        if CUT == 3:
            return
        Tm, Zm, Qm = W["Tm"], W["Zm"], W["Qm"]
        idb = self.identb[:].unsqueeze(1).broadcast_to([128, 8, 128])
        self.cp("pool", Tm[:], idb)
        self.cp("pool", Zm[:], idb)
        for lev in range(7):
            for g in range(2):
                pq, pt, pz = ps[2 + 3 * g], ps[3 + 3 * g], ps[4 + 3 * g]
                gs = slice(4 * g, 4 * g + 4)
                hs = range(4 * g, 4 * g + 4)
                for c, h in enumerate(hs):
                    self.mm(pq[:, c * 128:(c + 1) * 128], AT[:, h, 0:128], Tm[:, h, :])
                self.tt("dve", Qm[:, gs, :], pq[:, :].rearrange("q (c s) -> q c s", c=4),
                        self.lvl[:, d * 7 + lev, :].unsqueeze(1).broadcast_to([128, 4, 128]), ALU.mult)
                if lev < 6:
                    for c, h in enumerate(hs):
                        self.mm(pt[:, c * 128:(c + 1) * 128], Zm[:, h, :], Qm[:, h, :])
                for c, h in enumerate(hs):
                    self.mm(pz[:, c * 128:(c + 1) * 128], Qm[:, h, :], Zm[:, h, :])
                if lev < 6:
                    self.tt("dve", Tm[:, gs, :], pt[:, :].rearrange("q (c s) -> q c s", c=4), Tm[:, gs, :], ALU.add)
                self.tt("dve", Zm[:, gs, :], pz[:, :].rearrange("q (c s) -> q c s", c=4), Zm[:, gs, :], ALU.add)
        if CUT == 4:
            return
        Sst, S0s = W["S"], W["S0s"]
        Vt, Xb, Ub = W["Vt"], W["Xb"], W["Ub"]
        self.tt("dve", S0s[:], Sst[:], emwl[:, :, 0:1].broadcast_to([128, 4, 64]), ALU.mult)
        for h in range(8):
            p, h2 = h // 2, h % 2
            hp = slice(h2 * 64, h2 * 64 + 64)
            hc = slice(h * 64, (h + 1) * 64)
            self.mm(ps[0][:, hc], OPS[hp, p, 0, :], S0s[hp, p, :], start=True, stop=False)
            self.mm(ps[0][:, hc], AT[:, h, 256:384], Vt[:, hc], start=False, stop=True)
        self.cp("act", Xb[:], ps[0][:, :])
        for h in range(8):
            hc = slice(h * 64, (h + 1) * 64)
            self.mm(ps[1][:, hc], Zm[:, h, :], Xb[:, hc])
        self.cp("act", Ub[:], ps[1][:, :])
        for h in range(8):
            p, h2 = h // 2, h % 2
            hp = slice(h2 * 64, h2 * 64 + 64)
            hc = slice(h * 64, (h + 1) * 64)
            self.mm(ps[2][:, hc], OPS[hp, p, 1, :], S0s[hp, p, :], start=True, stop=False)
            self.mm(ps[2][:, hc], AT[:, h, 128:256], Ub[:, hc], start=False, stop=False)
            self.mm(ps[2][:, hc], AT[:, h, 384:512], Vt[:, hc], start=False, stop=True)
        for h in range(8):
            p, h2 = h // 2, h % 2
            hp = slice(h2 * 64, h2 * 64 + 64)
            hc = slice(h * 64, (h + 1) * 64)
            self.mm(ps[3][hp, p * 64:(p + 1) * 64], W["Bt"][:, hc], Ub[:, hc], start=True, stop=False)
            self.mm(ps[3][hp, p * 64:(p + 1) * 64], W["Kt"][:, hc], Vt[:, hc], start=False, stop=True)
        self.tt("dve", Sst[:], Sst[:], emwl[:, :, 1:2].broadcast_to([128, 4, 64]), ALU.mult)
        self.tt("dve", Sst[:], Sst[:], ps[3][:, 0:256].rearrange("q (p i) -> q p i", p=4), ALU.add)
        if CUT == 5:
            return
        for p in range(4):
            self.mm(ps[4][:, p * 2:p * 2 + 2], prod[:, p, :], self.hsel[:, :])
        st8 = W["st8"]
        if d == 1:
            self.cp("dve", self.bonb[:, bi, :], ps[4][:, 0:8])
            self.cp("act", W["ysb"][:], ps[2][:, :])
            self.dma(self.yb_sc[t0:t0 + 128, :], W["ysb"][:])
            return
        ysb = W["ysb"]
        b8 = lambda ap: ap.unsqueeze(2).broadcast_to([128, 8, 64])
        y3 = lambda t: t[:].rearrange("q (h i) -> q h i", h=8)
        self.tt("dve", st8[:, 5, :], ps[4][:, 0:8], self.bonb[:, bi, :], ALU.add)
        self.tt("dve", ysb[:], ps[2][:, :], W["ybt"][:], ALU.add)
        for kc in range(8):
            self.mm(ps[5][:, :], hT[:, kc, 1 + t0:1 + t0 + 128], W["wcz"][:, kc, :], start=(kc == 0), stop=(kc == 7))
        self.act(W["sz"][:], ps[5][:, :], AF.Silu)
        sqy = W["sq"][:].rearrange("q p c -> q (p c)")
        yn = W["u"][:].rearrange("q p c -> q (p c)")
        yn3 = W["u"][:].rearrange("q p (a i) -> q (p a) i", a=2)
        self.red(st8[:, 0, :], y3(ysb))
        self.tt("pool", sqy, ysb[:], ysb[:], ALU.mult)
        self.red(st8[:, 1, :], W["sq"][:].rearrange("q p (a i) -> q (p a) i", a=2))
        self.ts("dve", st8[:, 0, :], st8[:, 0, :], 1.0 / 64, None, ALU.mult)
        self.tt("dve", st8[:, 2, :], st8[:, 0, :], st8[:, 0, :], ALU.mult)
        self.stt(st8[:, 3, :], st8[:, 1, :], 1.0 / 64, st8[:, 2, :], ALU.mult, ALU.subtract)
        self.act(st8[:, 4, :], st8[:, 3, :], AF.Sqrt, bias=float(GN_EPS))
        self.recip(st8[:, 4, :], st8[:, 4, :])
        self.tt("pool", yn3, y3(ysb), b8(st8[:, 0, :]), ALU.subtract)
        self.tt("pool", yn3, yn3, b8(st8[:, 4, :]), ALU.mult)
        self.tt("pool", yn, yn, W["lnx"][:, 0:512], ALU.mult)
        self.tt("pool", yn, yn, W["lnx"][:, 512:1024], ALU.add)
        self.tt("dve", y3(ysb), y3(W["vtok"]), b8(st8[:, 5, :]), ALU.mult)
        self.tt("pool", yn, yn, ysb[:], ALU.add)
        self.tt("dve", ysb[:], yn, W["sz"][:], ALU.mult)
        self.dma(self.yc_sc[t0:t0 + 128, :], ysb[:])

    def phase_Q(self, ctx, lo, hi):
        S = self.S
        l, kind, si, T, cond, last = ctx["l"], ctx["kind"], ctx["si"], ctx["T"], ctx["cond"], ctx["last"]
        KaT, Vaf, ckvT, KbT, Vbf, nk, nkt = (ctx[k] for k in ("KaT", "Vaf", "ckvT", "KbT", "Vbf", "nk", "nkt"))
        ps = self.ps
        TQ = min(T, 512)
        nst = TQ // 128
        ntile = T // TQ
        xin, xout = ctx["xin"], ctx["xout"]
        yout = (self.y_prompt[si] if kind == "p" else self.y_sample)

        def qa(name, shape, dt=F32):
            if lo.can(shape, dt):
                return lo.alloc(name, shape, dt)
            return hi.alloc(name, shape, dt)

        hTq = qa("hTq", [128, 8, TQ], BF16)
        yaT = qa("yaT", [64, 8, TQ], BF16)
        ybT = qa("ybT", [64, 8, TQ], BF16)
        ycT = qa("ycT", [128, 4, TQ], BF16)
        mlo_c, mhi_c = lo.mark(), hi.mark()
        for ti in range(ntile):
            tok0 = ti * TQ
            x_ap = xin[tok0:tok0 + TQ, :]
            qaT = qa("qaT", [128, 8, TQ], BF16)
            qbT = qa("qbT", [128, 8, TQ], BF16)
            self.memset("pool", qaT[:], 0.0)
            self.memset("pool", qbT[:], 0.0)
            mlo, mhi = lo.mark(), hi.mark()
            xt = [qa(f"qxt{i}", [128, D]) for i in range(2)]
            xn = [qa(f"qxn{i}", [128, D]) for i in range(2)]
            ss2 = qa("qss", [128, 2]); tm2 = qa("qtm", [128, 2]); rs2 = qa("qrs", [128, 2])
            waq = qa("waq", [128, 8, 512], BF16)
            wbq = qa("wbq", [128, 8, 768], BF16)
            qn = qa("qn", [128, 512]); qsq = qa("qsq", [128, 512]); qb = qa("qbf", [128, 768])
            yct = [qa(f"yct{i}", [128, 512]) for i in range(2)]
            rp = [qa(f"qrp{i}", [128, 192]) for i in range(2)]
            rtmp = qa("qrtmp", [128, 512])
            s8 = qa("qs8", [128, 3, 8])
            self.dma(waq[:], self.wsc[l][:, :, O_AQ:O_AQ + 512])
            self.dma(wbq[:], self.wsc[l][:, :, O_BQ:O_BQ + 768])
            self.compute_hT(ctx, hTq, 0, x_ap, nst, (xt, xn, ss2, tm2, rs2))
            for st in range(nst):
                sc = slice(st * 128, (st + 1) * 128)
                b = st % 2
                if kind == "s":
                    self.dma(rp[b][:], self.c_rope[tok0 + st * 128: tok0 + (st + 1) * 128, :])
                for kc in range(8):
                    self.mm(ps[0][:, :], hTq[:, kc, sc], waq[:, kc, :], start=(kc == 0), stop=(kc == 7))
                self.act(qsq[:], ps[0][:, :], AF.Square)
                self.red(s8[:, 0, :], qsq[:].rearrange("p (h d) -> p h d", h=8))
                self.rstd(s8[:, 1, :], s8[:, 0, :], 64, NORM_EPS, s8[:, 2, :])
                self.tt("dve", qn[:].rearrange("p (g kv d) -> p kv g d", g=4, kv=2),
                        ps[0][:, :].rearrange("p (kv g d) -> p kv g d", kv=2, g=4),
                        s8[:, 1, :].rearrange("p (kv g) -> p kv g", kv=2).unsqueeze(3).broadcast_to([128, 2, 4, 64]), ALU.mult)
                qn3 = qn[:].rearrange("p (h d) -> p h d", h=8)
                self.tt("pool", qn3, qn3, self.qkn_t[:, 0:64].unsqueeze(1).broadcast_to([128, 8, 64]), ALU.mult)
                if kind == "s":
                    self.rope(qn3, rp[b][:, 0:64], rp[b][:, 64:128], 8, 16, rtmp)
                for g in range(4):
                    self.tr(ps[1][:, g * 128:(g + 1) * 128], qn[:, g * 128:(g + 1) * 128])
                self.cp("act", qaT[0:64, 0:4, sc], ps[1][0:64, :].rearrange("q (g t) -> q g t", g=4))
                self.cp("dve", qaT[64:128, 4:8, sc], ps[1][64:128, :].rearrange("q (g t) -> q g t", g=4))
                for half in range(2):
                    for kc in range(8):
                        self.mm(ps[2 + half][:, 0:384], hTq[:, kc, sc], wbq[:, kc, half * 384:(half + 1) * 384],
                                start=(kc == 0), stop=(kc == 7))
                self.cp("act", qb[:, 0:384], ps[2][:, 0:384])
                self.cp("dve", qb[:, 384:768], ps[3][:, 0:384])
                qb3 = qb[:].rearrange("p (h c) -> p h c", h=8)
                if kind == "s":
                    self.rope(qb3[:, :, 64:96], rp[b][:, 128:160], rp[b][:, 160:192], 8, 8, rtmp)
                for h in range(8):
                    self.tr(ps[4 + h // 4][0:96, (h % 4) * 128:(h % 4 + 1) * 128], qb[:, h * 96:(h + 1) * 96])
                self.cp("act", qbT[0:96, 0:4, sc], ps[4][0:96, :].rearrange("q (g t) -> q g t", g=4))
                self.cp("dve", qbT[0:96, 4:8, sc], ps[5][0:96, :].rearrange("q (g t) -> q g t", g=4))
                self.dma(yct[b][:], self.yc_sc[tok0 + st * 128: tok0 + (st + 1) * 128, :])
                for p in range(4):
                    self.tr(ps[6][:, p * 128:(p + 1) * 128], yct[b][:, p * 128:(p + 1) * 128])
                self.cp("dve", ycT[:, :, sc], ps[6][:, :].rearrange("q (g t) -> q g t", g=4))
            S.barrier()
            lo.release(mlo); hi.release(mhi)
            PT = [qa(f"PT{i}", [128, TQ], BF16) for i in range(3)]
            den = qa("den", [65, TQ]); rden = qa("rden", [64, TQ]); szq = qa("szq", [64, TQ]); gq = qa("gq", [64, TQ])
            wz = [qa(f"wz{i}", [128, 8, 64], BF16) for i in range(2)]

            def finalize(gh, po, zoff, yT, h):
                self.cp("act", den[64:65, :], po[64:65, 0:TQ])
                self.mm(ps[6][0:64, 0:TQ], self.ones[64:65, 0:64], den[64:65, :])
                self.recip(rden[:, :], ps[6][0:64, 0:TQ])
                for kc in range(8):
                    self.mm(ps[7][0:64, 0:TQ], wz[gh % 2][:, kc, :], hTq[:, kc, :], start=(kc == 0), stop=(kc == 7))
                self.act(szq[:, :], ps[7][0:64, 0:TQ], AF.Silu)
                self.tt("pool", gq[:, :], rden[:, :], szq[:, :], ALU.mult)
                self.tt("dve", yT[:, h, :], po[0:64, 0:TQ], gq[:, :], ALU.mult)

            bscale = float(96 ** -0.5)
            steps = [("A", h, kt) for h in range(8) for kt in range(nkt)] + [("B", h, kt) for h in range(8) for kt in range(nkt)]
            N = len(steps)

            def build_kb(h):
                nkb = 0
                for kb in range(0, nk, 512):
                    w = min(512, nk - kb)
                    self.mm(ps[5][0:64, 0:w], self.uk[:, h * 64:(h + 1) * 64], ckvT[:, kb:kb + w])
                    self.cp("dve" if nkb % 2 else "act", KbT[0:64, h % 2, kb:kb + w], ps[5][0:64, 0:w])
                    nkb += 1

            def qk(i):
                mx, h, kt = steps[i]
                pS = ps[i % 3]
                gh = h if mx == "A" else 8 + h
                if kt == 0:
                    zo = O_AZ if mx == "A" else O_BZ
                    self.dma(wz[gh % 2][:], self.wsc[l][:, :, zo + h * 64:zo + (h + 1) * 64])
                    if mx == "A" and h == 7:
                        build_kb(0)
                    elif mx == "B" and h < 7:
                        build_kb(h + 1)
                if mx == "A":
                    self.mm(pS[:, 0:TQ], KaT[:, kt * 128:(kt + 1) * 128], qaT[:, h, :])
                else:
                    self.mm(pS[:, 0:TQ], KbT[:, h % 2, kt * 128:(kt + 1) * 128], qbT[:, h, :])

            def ex(i):
                mx = steps[i][0]
                self.act(PT[i % 3][:, :], ps[i % 3][:, 0:TQ], AF.Exp, scale=(0.125 if mx == "A" else bscale))

            def pv(i):
                mx, h, kt = steps[i]
                gh = h if mx == "A" else 8 + h
                po = ps[3 + gh % 2]
                if mx == "A":
                    o0 = kt * 130 + (h // 4) * 65
                    vv = Vaf[:, o0:o0 + 128]
                else:
                    o0 = kt * 520 + h * 65
                    vv = Vbf[:, o0:o0 + 128]
                self.mm(po[:, 0:TQ], vv, PT[i % 3][:, :], start=(kt == 0), stop=(kt == nkt - 1))
                if kt == nkt - 1:
                    finalize(gh, po, None, yaT if mx == "A" else ybT, h)

            qk(0)
            if N > 1:
                qk(1)
            for i in range(N):
                ex(i)
                if i + 2 < N:
                    qk(i + 2)
                pv(i)
            S.barrier()
            lo.release(mlo_c); hi.release(mhi_c)
            woa = [qa(f"woa{i}", [64, 8, 128], BF16) for i in range(2)]
            wob = [qa(f"wob{i}", [64, 8, 128], BF16) for i in range(2)]
            woc = [qa(f"woc{i}", [128, 4, 128], BF16) for i in range(2)]
            wg = [qa(f"wg{i}", [128, 8, 3, 128], BF16) for i in range(2)]
            wo_ = [qa(f"wout{i}", [128, 8, 128], BF16) for i in range(2)]
            sgm = [qa(f"sgm{i}", [128, TQ]) for i in range(2)]
            mix = qa("mix", [128, TQ]); mtmp = qa("mtmp", [128, TQ])
            mixT = qa("mixT", [128, 8, TQ], BF16)
            oT = [qa(f"oT{i}", [128, TQ]) for i in range(2)]
            xres = qa("xres", [128, nst, D])
            for st in range(nst):
                self.dma(xres[:, st, :], x_ap[st * 128:(st + 1) * 128, :])
            for dc in range(8):
                b = dc % 2
                dcs = slice(dc * 128, (dc + 1) * 128)
                self.dma(woa[b][:], self.wosc[l][0][:, :, dcs])
                self.dma(wob[b][:], self.wosc[l][1][:, :, dcs])
                self.dma(woc[b][:], self.wosc[l][2][:, :, dcs])
                for br in range(3):
                    self.dma(wg[b][:, :, br, :], self.wsc[l][:, :, O_G + br * 1024 + dc * 128:O_G + br * 1024 + (dc + 1) * 128])
                for br in range(3):
                    pp, pg = ps[br], ps[3 + br]
                    if br < 2:
                        wo, yT = (woa, wob)[br][b], (yaT, ybT)[br]
                        for h in range(8):
                            self.mm(pp[:, 0:TQ], wo[:, h, :], yT[:, h, :], start=(h == 0), stop=(h == 7))
                    else:
                        for p in range(4):
                            self.mm(pp[:, 0:TQ], woc[b][:, p, :], ycT[:, p, :], start=(p == 0), stop=(p == 3))
                    for kc in range(8):
                        self.mm(pg[:, 0:TQ], wg[b][:, kc, br, :], hTq[:, kc, :], start=(kc == 0), stop=(kc == 7))
                    sg_ = sgm[br % 2]
                    self.act(sg_[:, :], pg[:, 0:TQ], AF.Sigmoid)
                    if br == 0:
                        self.tt("dve", mix[:, :], pp[:, 0:TQ], sg_[:, :], ALU.mult)
                    else:
                        self.tt("dve", mtmp[:, :], pp[:, 0:TQ], sg_[:, :], ALU.mult)
                        dst = mix[:, :] if br == 1 else mixT[:, dc, :]
                        self.tt("pool", dst, mix[:, :], mtmp[:, :], ALU.add)
            for dc in range(8):
                b = dc % 2
                dcs = slice(dc * 128, (dc + 1) * 128)
                self.dma(wo_[b][:], self.woutsc[l][:, :, dcs])
                po = ps[6 + b]
                for kc in range(8):
                    self.mm(po[:, 0:TQ], wo_[b][:, kc, :], mixT[:, kc, :], start=(kc == 0), stop=(kc == 7))
                gsc = self.modG[:, l, cond, dc:dc + 1]
                if b == 0:
                    self.act(oT[b][:, :], po[:, 0:TQ], AF.Identity, scale=gsc)
                else:
                    self.ts("dve", oT[b][:, :], po[:, 0:TQ], gsc, None, ALU.mult)
                pt = ps[dc % 4]
                for st in range(nst):
                    self.tr(pt[:, st * 128:(st + 1) * 128], oT[b][:, st * 128:(st + 1) * 128])
                self.tt("dve", xres[:, :, dcs], xres[:, :, dcs],
                        pt[:, 0:nst * 128].rearrange("q (s f) -> q s f", s=nst), ALU.add)
            for st in range(nst):
                rows = slice(tok0 + st * 128, tok0 + (st + 1) * 128)
                if not last:
                    self.dma(xout[rows, :], xres[:, st, :])
                else:
                    fs = qa(f"fss{st}", [128, 4])
                    jv = mixT[:, 0:1024 // TQ, :].rearrange("q a t -> q (a t)")
                    self.act(jv, xres[:, st, :], AF.Square, accum=fs[:, 0:1])
                    self.rstd(fs[:, 2:3], fs[:, 0:1], D, NORM_EPS, fs[:, 1:2])
                    self.ts("dve", xres[:, st, :], xres[:, st, :], fs[:, 2:3], None, ALU.mult)
                    self.tt("pool", xres[:, st, :], xres[:, st, :], self.fnw_t[:], ALU.mult)
                    self.dma(yout[rows, :], xres[:, st, :], wk=[("yout", kind, si, tok0, st)])
            S.barrier()
            lo.release(mlo_c); hi.release(mhi_c)


def _in_maps(inputs):
    c = host_constants()
    f = lambda a: np.ascontiguousarray(np.asarray(a, dtype=np.float32))
    g = {k: f(v) for k, v in inputs.items()}
    nb = np.zeros((128, 128), np.float32)
    for l in range(DEPTH):
        nb[l * 32:l * 32 + 8] = g["norm_w"][l].reshape(8, 128)
        nb[l * 32 + 8:l * 32 + 32] = g["b_mod"][l].reshape(24, 128)
    lfm = np.zeros((DEPTH, 48, 128), np.float32)
    for l in range(DEPTH):
        lfm[l, 0:4] = g["c_k_k"][l].reshape(4, 128)
        lfm[l, 4:8] = g["c_k_a"][l].reshape(4, 128)
        lfm[l, 8:12] = g["c_r_k"][l].reshape(4, 128)
        lfm[l, 12:26] = g["c_mu_prev"][l].reshape(14, 128)
        lfm[l, 26:40] = g["c_mu_next"][l].reshape(14, 128)
        lfm[l, 40:48] = g["c_a0"][l].reshape(8, 128)
    qkn = np.concatenate([g["a_qnorm_w"], g["a_knorm_w"], g["b_kvnorm_w"]], axis=1)
    lnx = np.concatenate([g["c_lnx_w"], g["c_lnx_b"]], axis=1)
    shared = {
        "nb_rows": nb, "w_mod": g["w_mod"], "w_in": g["w_in"], "qkn": f(qkn),
        "b_w_uk": g["b_w_uk"], "b_w_uv": g["b_w_uv"], "lfm_rows": lfm, "c_w0": g["c_w0"],
        "c_w_up": f(g["c_w_up"].reshape(DEPTH, 128, 512)), "c_a_up": f(g["c_a_up"].reshape(DEPTH, 128, 512)),
        "lnx": f(lnx), "w_oa": g["w_oa"], "w_ob": g["w_ob"], "w_oc": g["w_oc"], "w_out": g["w_out"],
        "final_norm_w": g["final_norm_w"],
        "c_ident": c["ident"], "c_bank": c["bankmask"], "c_lvl": c["lvlmask"], "c_tri": c["tri"],
        "c_hsel": c["headsel"], "c_rope": c["rope"],
    }
    maps = []
    for i in range(NCORES):
        m = dict(shared)
        m["x_prompt"] = f(g["x_prompt"][i * NPS:(i + 1) * NPS])
        m["x_sample"] = f(g["x_sample"][i])
        m["cache_a_k"] = f(g["cache_a_k"][i].reshape(DEPTH, PAST, 128))
        m["cache_a_v"] = f(g["cache_a_v"][i].reshape(DEPTH, PAST, 128))
        m["cache_b_ckv"] = f(g["cache_b_ckv"][i])
        m["cache_b_krope"] = f(g["cache_b_krope"][i])
        m["state_c_fwd"] = f(g["state_c_fwd"][i])
        m["state_c_bwd"] = f(g["state_c_bwd"][i])
        m["cond"] = f(np.concatenate([g["c_ctx"].reshape(8, 128), g["c"][i].reshape(8, 128)], axis=0))
        maps.append(m)
    return maps


_NC_CACHE = {}


def kernel(**inputs):
    stage = int(os.environ.get("MK_STAGE", "99"))
    if stage not in _NC_CACHE:
        _NC_CACHE[stage] = Builder(stage).build()
    nc = _NC_CACHE[stage]
    maps = _in_maps(inputs)
    res = run_bass_kernel_spmd(nc, maps, core_ids=list(range(NCORES)))
    r = res.results
    cat = lambda name: np.concatenate([np.asarray(x[name]) for x in r], axis=0)
    y_prompt = cat("y_prompt")
    y_sample = np.stack([np.asarray(x["y_sample"]) for x in r], axis=0)
    new_a_k = cat("new_a_k").reshape(32, DEPTH, TP, 2, 64)
    new_a_v = cat("new_a_v").reshape(32, DEPTH, TP, 2, 64)
    new_ckv = cat("new_b_ckv")
    new_kr = cat("new_b_krope")
    new_sf = cat("new_c_state_fwd")
    new_sb = cat("new_c_state_bwd")
    return (y_prompt, y_sample, new_a_k, new_a_v, new_ckv, new_kr, new_sf, new_sb)
```

```python
import os
import contextlib
import numpy as np
import concourse.bass as bass
import concourse.mybir as mybir
from concourse.bass_utils import run_bass_kernel_spmd

F32 = mybir.dt.float32
BF16 = mybir.dt.bfloat16
AF = mybir.ActivationFunctionType
ALU = mybir.AluOpType
AX = mybir.AxisListType

D = 1024
DEPTH = 4
NPS = 4
TP = 256
TS = 4096
PAST = 512
IN_DIM = 8096
NCORES = 8
O_AQ, O_AK, O_AV, O_AZ, O_BQ, O_CKV, O_KR, O_BZ, O_CIN, O_CZ, O_G = 0, 512, 640, 768, 1280, 2048, 2176, 2208, 2720, 4512, 5024
NORM_EPS = 1e-6
GN_EPS = 64e-5
DEC_C = -float(np.exp(-0.5))

EPOCH = 30000
N_EPOCH = {"pe": 16, "dve": 12, "act": 10, "pool": 8}
N_DMA_SEMS = 40
SB_BASE = 16512
SB_LIMIT = 229344
CUT = int(os.environ.get('MK_CUT', '0'))


class Sched:
    def __init__(self, nc):
        self.nc = nc
        self.ops = {e: [] for e in ("pe", "dve", "act", "pool", "sp")}
        self.count = {e: 0 for e in ("pe", "dve", "act", "pool")}
        self.last_w = {}
        self.readers = {}
        self.dma_rr = 0
        self.dma_tot = [0] * N_DMA_SEMS
        self.sems = {}

    def _deps(self, eng, reads, writes):
        d = {}

        def add(tok, raw):
            sk, v, src = tok
            if src == eng and sk[0] != "dma":
                if eng == "pe":
                    return
            if d.get(sk, 0) < v:
                d[sk] = v

        for k in reads:
            w = self.last_w.get(k)
            if w is not None:
                add(w, True)
        for k in writes:
            w = self.last_w.get(k)
            if w is not None:
                add(w, False)
            for r in self.readers.get(k, {}).values():
                add(r, False)
        return d

    def _commit(self, tok, reads, writes):
        sk = tok[0]
        for k in reads:
            self.readers.setdefault(k, {})[sk] = tok
        for k in writes:
            self.last_w[k] = tok
            self.readers[k] = {}

    def op(self, eng, fn, reads=(), writes=()):
        deps = self._deps(eng, reads, writes)
        n = self.count[eng]
        self.count[eng] = n + 1
        ep, v = divmod(n, EPOCH)
        tok = ((eng, ep), v + 1, eng)
        self._commit(tok, reads, writes)
        self.ops[eng].append(("op", fn, deps, tok))
        return tok

    def dma(self, q, fn, reads=(), writes=()):
        deps = self._deps("dmaq", reads, writes)
        i = self.dma_rr
        self.dma_rr = (i + 1) % N_DMA_SEMS
        prev = self.dma_tot[i]
        if prev:
            sk = ("dma", i)
            if deps.get(sk, 0) < prev:
                deps[sk] = prev
        self.dma_tot[i] = prev + 16
        tok = (("dma", i), prev + 16, "dmaq")
        self._commit(tok, reads, writes)
        self.ops[q].append(("dma", fn, deps, tok))
        return tok

    def wait_all(self, eng, toks):
        deps = {}
        for sk, v, _ in toks:
            deps[sk] = max(deps.get(sk, 0), v)
        self.ops[eng].append(("wait", None, deps, None))

    def alloc_sems(self, es):
        nc = self.nc
        for e, ne in N_EPOCH.items():
            for ep in range(ne):
                self.sems[(e, ep)] = es.enter_context(nc.semaphore(f"s_{e}{ep}"))
        for i in range(N_DMA_SEMS):
            self.sems[("dma", i)] = es.enter_context(nc.semaphore(f"s_dma{i}"))

    def barrier(self):
        toks = []
        for e, n in self.count.items():
            if n:
                ep, v = divmod(n - 1, EPOCH)
                toks.append(((e, ep), v + 1, e))
        for i in range(N_DMA_SEMS):
            if self.dma_tot[i]:
                toks.append((("dma", i), self.dma_tot[i], "dmaq"))
        for e in ("pe", "dve", "act", "pool", "sp"):
            self.wait_all(e, toks)

    def emit_block(self):
        nc = self.nc
        sems = self.sems
        for e, ne in N_EPOCH.items():
            assert self.count[e] <= ne * EPOCH, (e, self.count[e])
        with nc.Block() as block:
            def run(engname):
                lst = self.ops[engname]

                def body(eng):
                    waited = {}
                    for kind, fn, deps, tok in lst:
                        for sk, v in deps.items():
                            if waited.get(sk, 0) < v:
                                eng.wait_ge(sems[sk], v)
                                waited[sk] = v
                        if kind == "wait":
                            continue
                        ins = fn(eng)
                        ins.then_inc(sems[tok[0]], 16 if kind == "dma" else 1)
                return body

            block.tensor(run("pe"))
            block.vector(run("dve"))
            block.scalar(run("act"))
            block.gpsimd(run("pool"))
            block.sync(run("sp"))
        self.ops = {e: [] for e in ("pe", "dve", "act", "pool", "sp")}


class Arena:
    cnt = 0

    def __init__(self, nc, base, limit):
        self.nc, self.off, self.limit = nc, base, limit

    def alloc(self, name, shape, dt):
        esz = 4 if dt == F32 else 2
        nb = int(np.prod(shape[1:])) * esz
        nb = (nb + 63) // 64 * 64
        off = self.off
        self.off += nb
        assert self.off <= self.limit, (name, self.off, self.limit)
        Arena.cnt += 1
        return self.nc.alloc_sbuf_tensor_at(f"{name}_{Arena.cnt}", list(shape), dt, offset=off)

    def can(self, shape, dt):
        esz = 4 if dt == F32 else 2
        nb = (int(np.prod(shape[1:])) * esz + 63) // 64 * 64
        return self.off + nb <= self.limit

    def mark(self):
        return self.off

    def release(self, m):
        self.off = m


def _k(x):
    return x if isinstance(x, (str, tuple)) else x.name


def host_constants():
    c = {}
    c["ident"] = np.eye(128, dtype=np.float32)
    s = np.arange(128)[:, None]
    t = np.arange(128)[None, :]
    bank = np.zeros((2, 128, 512), np.float32)
    lvl = np.zeros((2, 7, 128, 128), np.float32)
    tri = np.zeros((2, 128, 3 * 128 + 2), np.float32)
    for d in range(2):
        if d == 0:
            strict = (s < t).astype(np.float32)
            incl = (s <= t).astype(np.float32)
        else:
            strict = (s > t).astype(np.float32)
            incl = (s >= t).astype(np.float32)
        bank[d] = np.concatenate([strict, incl, strict, incl], axis=1)
        for l in range(7):
            b = 1 << l
            tt_ = np.arange(128)[:, None]
            ss_ = np.arange(128)[None, :]
            same = (tt_ // (2 * b)) == (ss_ // (2 * b))
            t_second = (tt_ // b) % 2 == 1
            s_second = (ss_ // b) % 2 == 1
            if d == 0:
                m = same & t_second & (~s_second)
            else:
                m = same & (~t_second) & s_second
            lvl[d, l] = m.astype(np.float32)
        if d == 0:
            G = (s <= t).astype(np.float32) - (s <= 63).astype(np.float32)
            G1 = (s < t).astype(np.float32) - (s <= 63).astype(np.float32)
            Dm = (s > t).astype(np.float32)
            cm = (np.arange(128) <= 63).astype(np.float32)
        else:
            G = (s >= t).astype(np.float32) - (s >= 64).astype(np.float32)
            G1 = (s > t).astype(np.float32) - (s >= 64).astype(np.float32)
            Dm = (s < t).astype(np.float32)
            cm = (np.arange(128) >= 64).astype(np.float32)
        tri[d, :, 0:128] = DEC_C * G
        tri[d, :, 128:256] = DEC_C * G1
        tri[d, :, 256:384] = DEC_C * Dm
        tri[d, :, 384] = DEC_C * cm
        tri[d, :, 385] = DEC_C
    c["bankmask"] = bank
    c["lvlmask"] = lvl
    c["tri"] = tri
    hs = np.zeros((128, 2), np.float32)
    hs[:64, 0] = 1.0
    hs[64:, 1] = 1.0
    c["headsel"] = hs
    tok = np.arange(TS)
    row = (tok // 64).astype(np.float32)
    col = (tok % 64).astype(np.float32)
    tab = np.zeros((TS, 192), np.float32)

    def fill(base_c, base_s, half):
        inv = 10000.0 ** (-np.arange(half, dtype=np.float32) / half)
        for pi, pos in enumerate((row, col)):
            ang = pos[:, None] * inv[None, :]
            co, si = np.cos(ang), np.sin(ang)
            o = pi * 2 * half
            tab[:, base_c + o: base_c + o + half] = co
            tab[:, base_c + o + half: base_c + o + 2 * half] = co
            tab[:, base_s + o: base_s + o + half] = -si
            tab[:, base_s + o + half: base_s + o + 2 * half] = si

    fill(0, 64, 16)
    fill(128, 160, 8)
    c["rope"] = tab
    return c


class Builder:
    def __init__(self, stage=99):
        self.stage = stage
        self.nc = bass.Bass("TRN2", target_bir_lowering=False)
        self.S = Sched(self.nc)
        self.din = {}
        self.dout = {}

    def inp(self, name, shape):
        self.din[name] = self.nc.dram_tensor(name, list(shape), F32, kind="ExternalInput").ap()
        return self.din[name]

    def outp(self, name, shape):
        self.dout[name] = self.nc.dram_tensor(name, list(shape), F32, kind="ExternalOutput").ap()
        return self.dout[name]

    def scratch(self, name, shape, dt):
        return self.nc.dram_tensor(name, list(shape), dt).ap()

    def dma(self, out, in_, rk=None, wk=None, q="sp"):
        r = [_k(in_)] if rk is None else rk
        w = [_k(out)] if wk is None else wk
        self.S.dma(q, lambda e: e.dma_start(out=out, in_=in_), r, w)

    def mm(self, out, lhsT, rhs, start=True, stop=True, xr=()):
        self.S.op("pe", lambda e: e.matmul(out, lhsT, rhs, start=start, stop=stop),
                  [_k(lhsT), _k(rhs)] + [_k(x) for x in xr], [_k(out)])

    def tr(self, out, in_):
        p = in_.shape[0]
        idn = self.ident[0:p, 0:p]
        self.S.op("pe", lambda e: e.transpose(out, in_, idn), [_k(in_), _k(self.ident)], [_k(out)])

    def act(self, out, in_, func, bias=None, scale=None, accum=None):
        r = [_k(in_)]
        kw = {}
        if bias is not None:
            kw["bias"] = bias
            if not isinstance(bias, float):
                r.append(_k(bias))
        if scale is not None:
            kw["scale"] = scale
            if not isinstance(scale, float):
                r.append(_k(scale))
        w = [_k(out)]
        if accum is not None:
            kw["accum_out"] = accum
            w.append(_k(accum))
        self.S.op("act", lambda e: e.activation(out, in_, func, **kw), r, w)

    def tt(self, eng, out, a, b, op):
        self.S.op(eng, lambda e: e.tensor_tensor(out, a, b, op), [_k(a), _k(b)], [_k(out)])

    def ts(self, eng, out, a, s1, s2, op0, op1=None):
        r = [_k(a)] + [_k(s) for s in (s1, s2) if s is not None and not isinstance(s, (float, int))]
        if op1 is None:
            self.S.op(eng, lambda e: e.tensor_scalar(out, a, s1, s2, op0), r, [_k(out)])
        else:
            self.S.op(eng, lambda e: e.tensor_scalar(out, a, s1, s2, op0, op1), r, [_k(out)])

    def stt(self, out, a, scalar, b, op0, op1):
        r = [_k(a), _k(b)] + ([] if isinstance(scalar, (float, int)) else [_k(scalar)])
        self.S.op("dve", lambda e: e.scalar_tensor_tensor(out, a, scalar, b, op0, op1), r, [_k(out)])

    def cp(self, eng, out, in_):
        if eng == "act":
            self.S.op("act", lambda e: e.copy(out, in_), [_k(in_)], [_k(out)])
        else:
            self.S.op(eng, lambda e: e.tensor_copy(out, in_), [_k(in_)], [_k(out)])

    def memset(self, eng, out, val):
        self.S.op(eng, lambda e: e.memset(out, val), [], [_k(out)])

    def red(self, out, in_, op=ALU.add):
        self.S.op("dve", lambda e: e.tensor_reduce(out, in_, AX.X, op), [_k(in_)], [_k(out)])

    def recip(self, out, in_):
        self.S.op("dve", lambda e: e.reciprocal(out, in_), [_k(in_)], [_k(out)])

    def rstd(self, out, ss, n, eps, tmp):
        self.act(tmp, ss, AF.Sqrt, bias=float(eps), scale=1.0 / n)
        self.recip(out, tmp)

    def declare(self):
        i = self.inp
        self.x_prompt = i("x_prompt", [NPS, TP, D])
        self.x_sample = i("x_sample", [TS, D])
        self.cache_a_k = i("cache_a_k", [DEPTH, PAST, 128])
        self.cache_a_v = i("cache_a_v", [DEPTH, PAST, 128])
        self.cache_ckv = i("cache_b_ckv", [DEPTH, PAST, 128])
        self.cache_kr = i("cache_b_krope", [DEPTH, PAST, 32])
        self.state_f = i("state_c_fwd", [DEPTH, 8, 64, 64])
        self.state_b = i("state_c_bwd", [DEPTH, 8, 64, 64])
        self.cond = i("cond", [16, 128])
        self.nb_rows = i("nb_rows", [128, 128])
        self.w_mod = i("w_mod", [DEPTH, D, 3 * D])
        self.w_in = i("w_in", [DEPTH, D, IN_DIM])
        self.qkn = i("qkn", [DEPTH, 256])
        self.b_w_uk = i("b_w_uk", [DEPTH, 128, 512])
        self.b_w_uv = i("b_w_uv", [DEPTH, 128, 512])
        self.lfm_rows = i("lfm_rows", [DEPTH, 48, 128])
        self.c_w0 = i("c_w0", [DEPTH, 2, 512])
        self.c_w_up = i("c_w_up", [DEPTH, 128, 512])
        self.c_a_up = i("c_a_up", [DEPTH, 128, 512])
        self.lnx = i("lnx", [DEPTH, 1024])
        self.w_oa = i("w_oa", [DEPTH, 512, D])
        self.w_ob = i("w_ob", [DEPTH, 512, D])
        self.w_oc = i("w_oc", [DEPTH, 512, D])
        self.w_out = i("w_out", [DEPTH, D, D])
        self.fnw = i("final_norm_w", [D])
        self.c_ident = i("c_ident", [128, 128])
        self.c_bank = i("c_bank", [2, 128, 512])
        self.c_lvl = i("c_lvl", [2, 7, 128, 128])
        self.c_tri = i("c_tri", [2, 128, 386])
        self.c_hsel = i("c_hsel", [128, 2])
        self.c_rope = i("c_rope", [TS, 192])
        o = self.outp
        self.y_prompt = o("y_prompt", [NPS, TP, D])
        self.y_sample = o("y_sample", [TS, D])
        self.new_a_k = o("new_a_k", [NPS, DEPTH, TP, 128])
        self.new_a_v = o("new_a_v", [NPS, DEPTH, TP, 128])
        self.new_ckv = o("new_b_ckv", [NPS, DEPTH, TP, 128])
        self.new_kr = o("new_b_krope", [NPS, DEPTH, TP, 32])
        self.new_sf = o("new_c_state_fwd", [NPS, DEPTH, 8, 64, 64])
        self.new_sb = o("new_c_state_bwd", [NPS, DEPTH, 8, 64, 64])
        sc = self.scratch
        self.wsc = [sc(f"wsc{l}", [128, 8, IN_DIM], BF16) for l in range(DEPTH)]
        self.wosc = [[sc(f"wo{l}_{b}", ([64, 8, D] if b < 2 else [128, 4, D]), BF16) for b in range(3)] for l in range(DEPTH)]
        self.woutsc = [sc(f"wout{l}", [128, 8, D], BF16) for l in range(DEPTH)]
        self.xs_p = sc("xs_p", [NPS, TP, D], F32)
        self.xs_s = sc("xs_s", [TS, D], F32)
        self.yb_sc = sc("yb_sc", [TS, 512], F32)
        self.yc_sc = sc("yc_sc", [TS, 512], F32)

    def build(self):
        nc, S = self.nc, self.S
        self.declare()
        with contextlib.ExitStack() as es:
            S.alloc_sems(es)
            self.ps = [es.enter_context(nc.psum_tensor(f"ps{i}", [128, 512], F32)) for i in range(8)]
            self.carena = Arena(nc, SB_BASE, SB_BASE + 24576)
            self.arena = Arena(nc, SB_BASE + 24576, SB_LIMIT)
            self.prologue()
            seqs = [("p", s) for s in range(NPS)] + [("s", 0)]
            if self.stage < 50:
                seqs = seqs[:NPS]
            nlayers = DEPTH if self.stage >= 40 else 1
            for l in range(nlayers):
                self.layer_prologue(l)
                for kind, si in seqs:
                    self.run_seq(l, kind, si)
            S.barrier()
            S.emit_block()
        return nc

    def prologue(self):
        A = self.carena
        S = self.S
        self.ident = A.alloc("ident", [128, 128], F32)
        self.identb = A.alloc("identb", [128, 128], BF16)
        self.ones = A.alloc("ones", [128, 128], F32)
        self.onesb = A.alloc("onesb", [128, 128], BF16)
        self.bank = A.alloc("bank", [128, 2, 512], BF16)
        self.lvl = A.alloc("lvl", [128, 14, 128], BF16)
        self.tri = A.alloc("tri", [128, 2, 386], F32)
        self.hsel = A.alloc("hsel", [128, 2], F32)
        self.modA = A.alloc("modA", [128, DEPTH, 2, 8], F32)
        self.modB = A.alloc("modB", [128, DEPTH, 2, 8], F32)
        self.modG = A.alloc("modG", [128, DEPTH, 2, 8], F32)
        self.qkn_t = A.alloc("qkn_t", [128, 256], F32)
        self.uk = A.alloc("uk", [128, 512], BF16)
        self.uv = A.alloc("uv", [128, 512], BF16)
        self.wup = A.alloc("wup", [128, 512], BF16)
        self.aup = A.alloc("aup", [128, 512], BF16)
        self.w0row = A.alloc("w0row", [1, 2, 512], BF16)
        self.lfm = A.alloc("lfm", [128, 48], F32)
        self.c0fm = A.alloc("c0fm", [128, 14], F32)
        self.bonb = A.alloc("bonb", [128, 32, 8], F32)
        self.fnw_t = A.alloc("fnw_t", [128, D], F32)
        self.bones = A.alloc("bones", [128, 128], F32)
        self.omka = A.alloc("omka", [128, 4], F32)
        ar = self.arena
        m0 = ar.mark()
        st = ar.alloc("st", [128, 128], F32)
        st2 = ar.alloc("st2", [128, 512], F32)
        lv = ar.alloc("lv", [128, 14, 128], F32)
        cst = ar.alloc("cst", [16, 128], F32)
        scond = ar.alloc("scond", [128, 16], F32)
        nbfm = ar.alloc("nbfm", [128, 128], F32)
        mods = ar.alloc("mods", [128, DEPTH, 24, 2], F32)
        wst = [ar.alloc(f"wst{i}", [128, 8, 512], F32) for i in range(2)]
        self.dma(self.ident[:], self.c_ident)
        self.cp("dve", self.identb[:], self.ident[:])
        self.memset("pool", self.ones[:], 1.0)
        self.memset("pool", self.onesb[:], 1.0)
        self.memset("pool", self.bones[:], 0.0)
        self.memset("pool", self.bones[0:64, 0:64], 1.0)
        self.memset("pool", self.bones[64:128, 64:128], 1.0)
        for d in range(2):
            self.dma(st2[:], self.c_bank[d])
            self.cp("dve", self.bank[:, d, :], st2[:])
        for d in range(2):
            self.dma(lv[:, 0:7, :], self.c_lvl[d].rearrange("l t s -> t l s"))
            self.cp("dve", self.lvl[:, d * 7:(d + 1) * 7, :], lv[:, 0:7, :])
        self.dma(self.tri[:], self.c_tri.rearrange("d s c -> s d c"))
        self.dma(self.hsel[:], self.c_hsel)
        self.dma(self.fnw_t[:], self.fnw.partition_broadcast(128))
        self.dma(cst[:], self.cond)
        self.tr(self.ps[0][:, 0:16], cst[:])
        self.act(scond[:], self.ps[0][:, 0:16], AF.Silu)
        self.dma(st[:], self.nb_rows)
        self.tr(self.ps[1][:, 0:128], st[:])
        self.cp("dve", nbfm[:], self.ps[1][:, 0:128])
        nwm = 0
        for l in range(DEPTH):
            pm = self.ps[2 + (l % 2)]
            for j in range(6):
                w = wst[nwm % 2]
                nwm += 1
                self.dma(w[:], self.w_mod[l].rearrange("(kc p) n -> p kc n", p=128)[:, :, j * 512:(j + 1) * 512])
                for f in range(4):
                    fc = j * 4 + f
                    for kc in range(8):
                        self.mm(pm[:, fc * 2:fc * 2 + 2], w[:, kc, f * 128:(f + 1) * 128],
                                scond[:, kc:16:8], start=(kc == 0), stop=(kc == 7))
            self.tt("dve", mods[:, l, :, :], pm[:, 0:48].rearrange("p (a b) -> p a b", b=2),
                    nbfm[:, l * 32 + 8:l * 32 + 32].unsqueeze(2).broadcast_to([128, 24, 2]), ALU.add)
            for c in range(2):
                self.ts("dve", self.modA[:, l, c, :], mods[:, l, 8:16, c], 1.0, None, ALU.add)
                self.tt("dve", self.modA[:, l, c, :], self.modA[:, l, c, :], nbfm[:, l * 32:l * 32 + 8], ALU.mult)
                self.cp("dve", self.modB[:, l, c, :], mods[:, l, 0:8, c])
                self.cp("dve", self.modG[:, l, c, :], mods[:, l, 16:24, c])
        S.barrier()
        S.emit_block()
        ar.release(m0)

    def layer_prologue(self, l):
        S, ar = self.S, self.arena
        m0 = ar.mark()
        wst = [ar.alloc(f"cst{i}", [128, 8, 512], F32) for i in range(2)]
        wbf = [ar.alloc(f"cbf{i}", [128, 8, 512], BF16) for i in range(2)]
        st = ar.alloc("lst", [128, 512], F32)
        st48 = ar.alloc("lst48", [48, 128], F32)
        n = 0
        engs = ["dve", "pool", "act"]
        src = self.w_in[l].rearrange("(kc p) n -> p kc n", p=128)
        for c0 in range(0, IN_DIM, 512):
            w = min(512, IN_DIM - c0)
            a, b = wst[n % 2], wbf[n % 2]
            self.dma(a[:, :, 0:w], src[:, :, c0:c0 + w])
            self.cp(engs[n % 3], b[:, :, 0:w], a[:, :, 0:w])
            self.dma(self.wsc[l][:, :, c0:c0 + w], b[:, :, 0:w])
            n += 1
        src = self.w_out[l].rearrange("(kc p) n -> p kc n", p=128)
        for c0 in range(0, D, 512):
            a, b = wst[n % 2], wbf[n % 2]
            self.dma(a[:], src[:, :, c0:c0 + 512])
            self.cp(engs[n % 3], b[:], a[:])
            self.dma(self.woutsc[l][:, :, c0:c0 + 512], b[:])
            n += 1
        for bi, wsrc in enumerate((self.w_oa, self.w_ob, self.w_oc)):
            if bi < 2:
                src = wsrc[l].rearrange("(h p) n -> p h n", p=64)
                np_, nh_ = 64, 8
            else:
                src = wsrc[l].rearrange("(kc p) n -> p kc n", p=128)
                np_, nh_ = 128, 4
            for c0 in range(0, D, 512):
                a, b = wst[n % 2], wbf[n % 2]
                self.dma(a[0:np_, 0:nh_, :], src[:, :, c0:c0 + 512])
                self.cp(engs[n % 3], b[0:np_, 0:nh_, :], a[0:np_, 0:nh_, :])
                self.dma(self.wosc[l][bi][:, :, c0:c0 + 512], b[0:np_, 0:nh_, :])
                n += 1
        for dst, srcw in ((self.uk, self.b_w_uk), (self.uv, self.b_w_uv), (self.wup, self.c_w_up), (self.aup, self.c_a_up)):
            self.dma(st[:], srcw[l])
            self.cp("dve", dst[:], st[:])
        self.dma(st[0:1, 0:512], self.c_w0[l, 0:1, :])
        self.cp("dve", self.w0row[:, 0, :], st[0:1, 0:512])
        self.dma(st[0:1, 0:512], self.c_w0[l, 1:2, :])
        self.cp("dve", self.w0row[:, 1, :], st[0:1, 0:512])
        self.dma(self.qkn_t[:], self.qkn[l].partition_broadcast(128))
        self.dma(st48[:], self.lfm_rows[l])
        self.tr(self.ps[0][:, 0:48], st48[:])
        self.cp("dve", self.lfm[:], self.ps[0][:, 0:48])
        self.tt("dve", self.c0fm[:], self.lfm[:, 12:26], self.lfm[:, 26:40], ALU.add)
        self.ts("dve", self.c0fm[:], self.c0fm[:], -1.0, 1.0, ALU.mult, ALU.add)
        self.ts("dve", self.omka[:], self.lfm[:, 4:8], -1.0, 1.0, ALU.mult, ALU.add)
        S.barrier()
        S.emit_block()
        ar.release(m0)

    def run_seq(self, l, kind, si):
        S, ar = self.S, self.arena
        T = TP if kind == "p" else TS
        cond = 0 if kind == "p" else 1
        last = (l == DEPTH - 1)
        if l == 0:
            xin = self.x_prompt[si] if kind == "p" else self.x_sample
        else:
            xin = self.xs_p[si] if kind == "p" else self.xs_s
        xout = self.xs_p[si] if kind == "p" else self.xs_s
        ctx = dict(l=l, kind=kind, si=si, T=T, cond=cond, xin=xin, xout=xout, last=last)
        m0 = ar.mark()
        hT = ar.alloc("hT", [128, 8, T + 2], BF16)
        mH = ar.mark()
        ctx["hT"] = hT
        self.phase_H(ctx)
        S.barrier()
        S.emit_block()
        if self.stage >= 20:
            self.phase_rwkv(ctx)
        self.phase_KV(ctx)
        S.barrier()
        S.emit_block()
        if self.stage >= 30:
            lo = Arena(self.nc, m0, mH)
            self.phase_Q(ctx, lo, ar)
            S.barrier()
            S.emit_block()
        ar.release(m0)

    def compute_hT(self, ctx, hT, col0, x_ap, nst, work):
        l, cond = ctx["l"], ctx["cond"]
        xt, xn, ss, tmp, rs = work
        for st in range(nst):
            b = st % 2
            self.dma(xt[b][:], x_ap[st * 128:(st + 1) * 128, :])
            self.act(xn[b][:], xt[b][:], AF.Square, accum=ss[:, b:b + 1])
            self.rstd(rs[:, b:b + 1], ss[:, b:b + 1], D, NORM_EPS, tmp[:, b:b + 1])
            self.ts("dve", xn[b][:], xt[b][:], rs[:, b:b + 1], None, ALU.mult)
            for half in range(2):
                pb = self.ps[(2 * st + half) % 4]
                for q in range(4):
                    kc = half * 4 + q
                    self.tr(pb[:, q * 128:(q + 1) * 128], xn[b][:, kc * 128:(kc + 1) * 128])
                for q in range(4):
                    kc = half * 4 + q
                    dst = hT[:, kc, col0 + st * 128: col0 + (st + 1) * 128]
                    if q % 2 == 0:
                        self.act(dst, pb[:, q * 128:(q + 1) * 128], AF.Identity,
                                 bias=self.modB[:, l, cond, kc:kc + 1], scale=self.modA[:, l, cond, kc:kc + 1])
                    else:
                        self.ts("dve", dst, pb[:, q * 128:(q + 1) * 128], self.modA[:, l, cond, kc:kc + 1],
                                self.modB[:, l, cond, kc:kc + 1], ALU.mult, ALU.add)

    def hT_work(self, ar):
        xt = [ar.alloc(f"xt{i}", [128, D], F32) for i in range(2)]
        xn = [ar.alloc(f"xn{i}", [128, D], F32) for i in range(2)]
        ss = ar.alloc("ss", [128, 2], F32)
        tmp = ar.alloc("tmpr", [128, 2], F32)
        rs = ar.alloc("rs", [128, 2], F32)
        return (xt, xn, ss, tmp, rs)

    def phase_H(self, ctx):
        ar = self.arena
        T, hT = ctx["T"], ctx["hT"]
        m = ar.mark()
        work = self.hT_work(ar)
        self.memset("pool", hT[:, :, 0:1], 0.0)
        self.memset("pool", hT[:, :, T + 1:T + 2], 0.0)
        self.compute_hT(ctx, hT, 1, ctx["xin"], T // 128, work)
        ar.release(m)

    def phase_KV(self, ctx):
        ar = self.arena
        l, kind, si, T, hT = ctx["l"], ctx["kind"], ctx["si"], ctx["T"], ctx["hT"]
        nk = T + (PAST if kind == "s" else 0)
        nkt = nk // 128
        KaT = ar.alloc("KaT", [128, nk], BF16)
        Vaf = ar.alloc("Va", [128, nkt * 130 + 64], BF16)
        ckvT = ar.alloc("ckvT", [128, nk], BF16)
        KbT = ar.alloc("KbT", [128, 2, nk], BF16)
        Vbf = ar.alloc("Vb", [128, nkt * 520 + 64], BF16)
        Va = Vaf[:, 0:nkt * 130].rearrange("p (t k c) -> p t k c", k=2, c=65)
        Vb = Vbf[:, 0:nkt * 520].rearrange("p (t k c) -> p t k c", k=8, c=65)
        ctx.update(Vaf=Vaf, Vbf=Vbf)
        ctx.update(KaT=KaT, Va=Va, ckvT=ckvT, KbT=KbT, Vb=Vb, nk=nk, nkt=nkt)
        m = ar.mark()
        wkv = ar.alloc("wkv", [128, 8, 416], BF16)
        kn = [ar.alloc(f"kn{i}", [128, 128], F32) for i in range(2)]
        vf = [ar.alloc(f"vf{i}", [128, 128], F32) for i in range(2)]
        cn = [ar.alloc(f"cn{i}", [128, 128], F32) for i in range(2)]
        krf = [ar.alloc(f"krf{i}", [128, 32], F32) for i in range(2)]
        junk = ar.alloc("kvjunk", [128, 128], F32)
        ss = ar.alloc("kvss", [128, 4], F32)
        tm = ar.alloc("kvtm", [128, 4], F32)
        rs = ar.alloc("kvrs", [128, 4], F32)
        rp = [ar.alloc(f"kvrp{i}", [128, 192], F32) for i in range(2)]
        rtmp = ar.alloc("kvrtmp", [128, 128], F32)
        self.memset("pool", ss[:], 0.0)
        self.memset("pool", Vaf[:], 0.0)
        self.memset("pool", Vbf[:, nkt * 520:nkt * 520 + 64], 0.0)
        self.memset("pool", KbT[:], 0.0)
        self.memset("pool", Va[:, :, :, 64:65], 1.0)
        self.memset("pool", Vb[:, :, :, 64:65], 1.0)
        self.dma(wkv[:, :, 0:256], self.wsc[l][:, :, O_AK:O_AK + 256])
        self.dma(wkv[:, :, 256:416], self.wsc[l][:, :, O_CKV:O_CKV + 160])
        qk = self.qkn_t
        for st in range(nkt):
            b = st % 2
            new = st < T // 128
            pA, pB, pT, pV = self.ps[0 + 4 * b], self.ps[1 + 4 * b], self.ps[2 + 4 * b], self.ps[3 + 4 * b]
            if new:
                hs = hT[:, :, 1 + st * 128: 1 + (st + 1) * 128]
                for kc in range(8):
                    self.mm(pA[:, 0:256], hs[:, kc, :], wkv[:, kc, 0:256], start=(kc == 0), stop=(kc == 7))
                for kc in range(8):
                    self.mm(pB[:, 0:160], hs[:, kc, :], wkv[:, kc, 256:416], start=(kc == 0), stop=(kc == 7))
                for h in range(2):
                    self.act(junk[:, 0:64], pA[:, h * 64:(h + 1) * 64], AF.Square, accum=ss[:, h:h + 1])
                self.rstd(rs[:, 0:2], ss[:, 0:2], 64, NORM_EPS, tm[:, 0:2])
                self.tt("dve", kn[b][:].rearrange("p (a c) -> p a c", a=2), pA[:, 0:128].rearrange("p (a c) -> p a c", a=2),
                        rs[:, 0:2].unsqueeze(2).broadcast_to([128, 2, 64]), ALU.mult)
                self.tt("pool", kn[b][:].rearrange("p (a c) -> p a c", a=2), kn[b][:].rearrange("p (a c) -> p a c", a=2),
                        qk[:, 64:128].unsqueeze(1).broadcast_to([128, 2, 64]), ALU.mult)
                self.cp("act", vf[b][:], pA[:, 128:256])
                self.act(junk[:], pB[:, 0:128], AF.Square, accum=ss[:, 2:3])
                self.rstd(rs[:, 2:3], ss[:, 2:3], 128, NORM_EPS, tm[:, 2:3])
                self.ts("dve", cn[b][:], pB[:, 0:128], rs[:, 2:3], None, ALU.mult)
                self.tt("pool", cn[b][:], cn[b][:], qk[:, 128:256], ALU.mult)
                self.cp("act", krf[b][:], pB[:, 128:160])
                if kind == "p":
                    self.dma(self.new_a_k[si, l, st * 128:(st + 1) * 128, :], kn[b][:], wk=[("nak", si, l, st)])
                    self.dma(self.new_a_v[si, l, st * 128:(st + 1) * 128, :], vf[b][:], wk=[("nav", si, l, st)])
                    self.dma(self.new_ckv[si, l, st * 128:(st + 1) * 128, :], cn[b][:], wk=[("nck", si, l, st)])
                    self.dma(self.new_kr[si, l, st * 128:(st + 1) * 128, :], krf[b][:], wk=[("nkr", si, l, st)])
                else:
                    self.dma(rp[b][:], self.c_rope[st * 128:(st + 1) * 128, :])
                    self.rope(kn[b][:].rearrange("p (h c) -> p h c", h=2), rp[b][:, 0:64], rp[b][:, 64:128], 2, 16, rtmp)
                    self.rope(krf[b][:].rearrange("p (h c) -> p h c", h=1), rp[b][:, 128:160], rp[b][:, 160:192], 1, 8, rtmp)
            else:
                c0 = (st - T // 128) * 128
                self.dma(kn[b][:], self.cache_a_k[l, c0:c0 + 128, :])
                self.dma(vf[b][:], self.cache_a_v[l, c0:c0 + 128, :])
                self.dma(cn[b][:], self.cache_ckv[l, c0:c0 + 128, :])
                self.dma(krf[b][:], self.cache_kr[l, c0:c0 + 128, :])
            ks = slice(st * 128, (st + 1) * 128)
            self.tr(pT[:, 0:128], kn[b][:])
            self.cp("act", KaT[:, ks], pT[:, 0:128])
            self.tr(pT[:, 128:256], cn[b][:])
            self.cp("dve", ckvT[:, ks], pT[:, 128:256])
            self.mm(pT[64:96, 256:384], krf[b][:], self.ident[:], start=True, stop=True)
            self.cp("act", KbT[64:96, 0, ks], pT[64:96, 256:384])
            self.cp("pool", KbT[64:96, 1, ks], KbT[64:96, 0, ks])
            self.cp("pool", Va[:, st, :, 0:64], vf[b][:].rearrange("p (a c) -> p a c", a=2))
            self.mm(pV[:, :], ckvT[:, ks], self.uv[:], start=True, stop=True)
            self.cp("act" if st % 2 else "dve", Vb[:, st, :, 0:64], pV[:, :].rearrange("p (a c) -> p a c", a=8))
        ar.release(m)

    def rope(self, x3, cc, ss_, nh, half, tmp):
        w = 4 * half
        t3 = tmp[:, 0:nh * w].rearrange("p (h c) -> p h c", h=nh)
        for a in range(2):
            for hb in range(2):
                o = a * 2 * half + hb * half
                o2 = a * 2 * half + (1 - hb) * half
                self.tt("pool", t3[:, :, o:o + half], x3[:, :, o2:o2 + half],
                        ss_[:, o:o + half].unsqueeze(1).broadcast_to([128, nh, half]), ALU.mult)
        self.tt("dve", x3, x3, cc.unsqueeze(1).broadcast_to([128, nh, w]), ALU.mult)
        self.tt("dve", x3, x3, t3, ALU.add)

    def phase_rwkv(self, ctx):
        S, ar = self.S, self.arena
        l, kind, si, T, hT = ctx["l"], ctx["kind"], ctx["si"], ctx["T"], ctx["hT"]
        nsb = T // 256
        m = ar.mark()
        W = {}
        a = lambda n, shp, dt=F32: W.__setitem__(n, ar.alloc(n, shp, dt))
        a("wc", [128, 8, 1792], BF16)
        a("wcz", [128, 8, 512], BF16)
        a("lnx", [128, 1024])
        for n in ("rT", "kT", "vT"):
            a(n, [128, 4, 256])
        for n in ("kq", "sq", "tmpf", "kk", "bT", "kdT", "asg", "u"):
            a(n, [128, 4, 128])
        for n in ("sg", "Ep", "Em", "E1", "ED", "ybt"):
            a(n, [128, 512])
        W["ysb"] = W["Ep"]; W["sz"] = W["Em"]; W["vtok"] = W["E1"]
        a("t1a", [128, 256]); a("t1b", [128, 256]); a("wdT", [128, 256]); a("adT", [128, 256])
        a("tw", [128, 256], BF16); a("adb", [128, 256], BF16)
        a("OPS", [128, 4, 4, 128], BF16)
        a("Bt", [128, 512], BF16); a("Kt", [128, 512], BF16); a("Vt", [128, 512], BF16)
        a("AT", [128, 8, 512], BF16); a("ATr", [128, 512], BF16)
        a("Tm", [128, 8, 128], BF16); a("Zm", [128, 8, 128], BF16); a("Qm", [128, 8, 128], BF16)
        a("S", [128, 4, 64]); a("S0s", [128, 4, 64], BF16); a("emwl", [128, 4, 2])
        a("Xb", [128, 512], BF16); a("Ub", [128, 512], BF16)
        a("st8", [128, 6, 8]); a("sti", [64, 4, 128])
        for c0 in range(0, 1792, 448):
            self.dma(W["wc"][:, :, c0:c0 + 448], self.wsc[l][:, :, O_CIN + c0:O_CIN + c0 + 448])
        self.dma(W["wcz"][:], self.wsc[l][:, :, O_CZ:O_CZ + 512])
        self.dma(W["lnx"][:], self.lnx[l].partition_broadcast(128))
        for d in (1, 0):
            Sst = W["S"]
            if kind == "p":
                self.memset("pool", Sst[:], 0.0)
            else:
                src = (self.state_f if d == 0 else self.state_b)[l]
                self.dma(W["sti"][:].rearrange("i p (a j) -> i p a j", a=2), src.rearrange("(p a) i j -> i p a j", a=2))
                for p in range(4):
                    self.tr(self.ps[0][:, p * 64:(p + 1) * 64], W["sti"][:, p, :])
                self.cp("dve", Sst[:].rearrange("q p i -> q (p i)"), self.ps[0][:, 0:256])
            order = range(nsb - 1, -1, -1) if d == 1 else range(nsb)
            for sb in order:
                self.rwkv_super(ctx, d, sb, W)
            if kind == "p":
                dst = (self.new_sf if d == 0 else self.new_sb)[si, l]
                for p in range(4):
                    self.tr(self.ps[0][0:64, p * 128:(p + 1) * 128], Sst[:, p, :])
                self.cp("dve", W["sti"][:].rearrange("i p c -> i (p c)"), self.ps[0][0:64, :])
                self.dma(dst.rearrange("(p a) i j -> i p a j", a=2), W["sti"][:].rearrange("i p (a j) -> i p a j", a=2),
                         wk=[("nst", d, si, l)])
            S.barrier()
            S.emit_block()
        ar.release(m)

    def rwkv_super(self, ctx, d, sb, W):
        hT = ctx["hT"]
        ps, lfm = self.ps, self.lfm
        ts0 = sb * 256
        rT, kT, vT = W["rT"], W["kT"], W["vT"]
        for ci in range(14):
            pb = ps[ci % 2]
            for kc in range(8):
                self.mm(pb[:, 0:258], W["wc"][:, kc, ci * 128:(ci + 1) * 128], hT[:, kc, ts0:ts0 + 258],
                        start=(kc == 0), stop=(kc == 7))
            if ci < 12:
                dst = (rT, kT, vT)[ci // 4][:, ci % 4, :]
            else:
                dst = (W["wdT"], W["adT"])[ci - 12][:]
            t1 = W["t1a" if ci % 2 == 0 else "t1b"]
            self.act(t1[:], pb[:, 1:257], AF.Identity, scale=self.c0fm[:, ci:ci + 1])
            self.stt(t1[:], pb[:, 0:256], lfm[:, 12 + ci:13 + ci], t1[:], ALU.mult, ALU.add)
            self.stt(dst, pb[:, 2:258], lfm[:, 26 + ci:27 + ci], t1[:], ALU.mult, ALU.add)
        self.act(W["tw"][:], W["wdT"][:], AF.Tanh)
        self.cp("pool", W["adb"][:], W["adT"][:])
        for blk in ((1, 0) if d == 1 else (0, 1)):
            self.rwkv_block(ctx, d, sb * 2 + blk, blk, W)

    def rwkv_block(self, ctx, d, bi, blk, W):
        l, kind, T, hT = ctx["l"], ctx["kind"], ctx["T"], ctx["hT"]
        ps = self.ps
        t0 = bi * 128
        bs = slice(blk * 128, (blk + 1) * 128)
        lfm = self.lfm
        bc3 = lambda ap, n: ap.unsqueeze(2).broadcast_to([128, ap.shape[1], n])
        v3 = lambda t: t[:].rearrange("q (p c) -> q p c", p=4)
        rT, kT, vT = W["rT"][:, :, bs], W["kT"][:, :, bs], W["vT"][:, :, bs]
        tw, adb = W["tw"][:, bs], W["adb"][:, bs]
        if d == 0:
            self.dma(W["ybt"][:], self.yb_sc[t0:t0 + 128, :])
        hd = slice(d * 64, (d + 1) * 64)
        self.mm(ps[2][:, :], tw[hd, :], self.wup[hd, :], start=True, stop=False)
        self.mm(ps[2][:, :], self.onesb[0:1, 0:128], self.w0row[0:1, d, :], start=False, stop=True)
        self.act(W["sg"][:], ps[2][:, :], AF.Sigmoid)
        for p in range(4):
            self.mm(ps[3][:, p * 128:(p + 1) * 128], self.aup[hd, p * 128:(p + 1) * 128], adb[hd, :])
        for p in range(4):
            self.act(W["asg"][:, p, :], ps[3][:, p * 128:(p + 1) * 128], AF.Sigmoid, bias=lfm[:, 40 + d * 4 + p:41 + d * 4 + p])
        self.tt("pool", W["kq"][:], kT, bc3(lfm[:, 0:4], 128), ALU.mult)
        self.tt("pool", W["sq"][:], W["kq"][:], W["kq"][:], ALU.mult)
        self.mm(ps[4][:, :], self.bones[:], W["sq"][:].rearrange("q p c -> q (p c)"))
        self.act(W["tmpf"][:].rearrange("q p c -> q (p c)"), ps[4][:, :], AF.Sqrt)
        self.ts("dve", W["tmpf"][:], W["tmpf"][:], 1e-12, None, ALU.max)
        self.recip(W["tmpf"][:], W["tmpf"][:])
        self.tt("pool", W["kk"][:], W["kq"][:], W["tmpf"][:], ALU.mult)
        self.tt("pool", W["bT"][:], W["kk"][:], W["asg"][:], ALU.mult)
        self.tt("pool", W["u"][:], W["asg"][:], bc3(lfm[:, 4:8], 128), ALU.mult)
        self.tt("pool", W["u"][:], W["u"][:], bc3(self.omka[:, 0:4], 128), ALU.add)
        self.tt("pool", W["kdT"][:], kT, W["u"][:], ALU.mult)
        if CUT == 1:
            return
        tri = self.tri
        first, lastt = (0, 127) if d == 0 else (127, 0)
        for p in range(4):
            pc = ps[5 + p // 2]
            self.mm(pc[:, (p % 2) * 256:(p % 2) * 256 + 256], W["sg"][:, p * 128:(p + 1) * 128], tri[:, d, 0:256])
        self.mm(ps[7][:, :], tri[:, d, 256:384], W["sg"][:])
        emwl = W["emwl"]
        for hf in range(2):
            pc = ps[5 + hf]
            pv_ = pc[:, :].rearrange("q (p k t) -> q p k t", p=2, k=2)
            hs_ = slice(hf * 256, (hf + 1) * 256)
            self.act(v3(W["Ep"])[:, hf * 2:hf * 2 + 2, :], pv_[:, :, 0, :], AF.Exp)
            self.act(v3(W["Em"])[:, hf * 2:hf * 2 + 2, :], pv_[:, :, 0, :], AF.Exp, scale=-1.0)
            self.act(v3(W["E1"])[:, hf * 2:hf * 2 + 2, :], pv_[:, :, 1, :], AF.Exp)
            self.act(emwl[:, hf * 2:hf * 2 + 2, 0], pv_[:, :, 1, first], AF.Exp, scale=-1.0)
        self.act(W["ED"][:], ps[7][:, :], AF.Exp)
        self.tt("dve", emwl[:, :, 1], emwl[:, :, 0], v3(W["Ep"])[:, :, lastt], ALU.mult)
        OPS = W["OPS"]
        self.stt(OPS[:, :, 0, :], W["kk"][:], -1.0, v3(W["E1"]), ALU.mult, ALU.mult)
        self.tt("pool", OPS[:, :, 1, :], rT, v3(W["Ep"]), ALU.mult)
        self.tt("pool", OPS[:, :, 2, :], W["bT"][:], v3(W["Em"]), ALU.mult)
        self.tt("dve", OPS[:, :, 3, :], W["kdT"][:], v3(W["Em"]), ALU.mult)
        prod = W["kq"]
        self.tt("pool", prod[:], rT, W["kdT"][:], ALU.mult)
        self.tt("pool", prod[:], prod[:], bc3(lfm[:, 8:12], 128), ALU.mult)
        if CUT == 2:
            return
        for p in range(4):
            self.tr(ps[5][:, p * 128:(p + 1) * 128], W["bT"][:, p, :])
        self.tt("dve", W["Bt"][:], ps[5][:, :], W["ED"][:], ALU.mult)
        for p in range(4):
            self.tr(ps[6][:, p * 128:(p + 1) * 128], W["kdT"][:, p, :])
        self.tt("dve", W["Kt"][:], ps[6][:, :], W["ED"][:], ALU.mult)
        for p in range(4):
            self.tr(ps[7][:, p * 128:(p + 1) * 128], vT[:, p, :])
        self.cp("act", W["Vt"][:], ps[7][:, :])
        if d == 0:
            self.cp("act", W["vtok"][:], ps[7][:, :])
        AT = W["AT"]
        for h in range(8):
            p, h2 = h // 2, h % 2
            hp = slice(h2 * 64, h2 * 64 + 64)
            pa = ps[h % 2]
            self.mm(pa[:, 0:256], OPS[hp, p, 2, :], OPS[hp, p, 0:2, :])
            self.mm(pa[:, 256:512], OPS[hp, p, 3, :], OPS[hp, p, 0:2, :])
            if h % 2 == 0:
                self.tt("dve", AT[:, h, :], pa[:, :], self.bank[:, d, :], ALU.mult)
            else:
                self.cp("act", W["ATr"][:], pa[:, :])
                self.tt("pool", AT[:, h, :], W["ATr"][:], self.bank[:, d, :], ALU.mult)
        if CUT == 3:
            return
        Tm, Zm, Qm = W["Tm"], W["Zm"], W["Qm"]
        idb = self.identb[:].unsqueeze(1).broadcast_to([128, 8, 128])
        self.cp("pool", Tm[:], idb)
        self.cp("pool", Zm[:], idb)
        lvb = lambda lev: self.lvl[:, d * 7 + lev, :].unsqueeze(1).broadcast_to([128, 4, 128])
        v4 = lambda pb: pb[:, :].rearrange("q (c s) -> q c s", c=4)
        for lev in range(7):
            for g in range(2):
                pq = ps[2 + 3 * g]
                for c in range(4):
                    h = 4 * g + c
                    self.mm(pq[:, c * 128:(c + 1) * 128], AT[:, h, 0:128], Tm[:, h, :])
            for g in range(2):
                gs = slice(4 * g, 4 * g + 4)
                self.tt("dve", Qm[:, gs, :], v4(ps[2 + 3 * g]), lvb(lev), ALU.mult)
            for g in range(2):
                pt, pz = ps[3 + 3 * g], ps[4 + 3 * g]
                for c in range(4):
                    h = 4 * g + c
                    if lev < 6:
                        self.mm(pt[:, c * 128:(c + 1) * 128], Zm[:, h, :], Qm[:, h, :])
                    self.mm(pz[:, c * 128:(c + 1) * 128], Qm[:, h, :], Zm[:, h, :])
            for g in range(2):
                gs = slice(4 * g, 4 * g + 4)
                if lev < 6:
                    self.tt("dve", Tm[:, gs, :], v4(ps[3 + 3 * g]), Tm[:, gs, :], ALU.add)
                self.tt("dve", Zm[:, gs, :], v4(ps[4 + 3 * g]), Zm[:, gs, :], ALU.add)
        if CUT == 4:
            return
        Sst, S0s = W["S"], W["S0s"]
        Vt, Xb, Ub = W["Vt"], W["Xb"], W["Ub"]
        self.tt("dve", S0s[:], Sst[:], emwl[:, :, 0:1].broadcast_to([128, 4, 64]), ALU.mult)
        for h in range(8):
            p, h2 = h // 2, h % 2
            hp = slice(h2 * 64, h2 * 64 + 64)
            hc = slice(h * 64, (h + 1) * 64)
            self.mm(ps[0][:, hc], OPS[hp, p, 0, :], S0s[hp, p, :], start=True, stop=False)
            self.mm(ps[0][:, hc], AT[:, h, 256:384], Vt[:, hc], start=False, stop=True)
        self.cp("act", Xb[:], ps[0][:, :])
        for h in range(8):
            hc = slice(h * 64, (h + 1) * 64)
            self.mm(ps[1][:, hc], Zm[:, h, :], Xb[:, hc])
        self.cp("act", Ub[:], ps[1][:, :])
        for h in range(8):
            p, h2 = h // 2, h % 2
            hp = slice(h2 * 64, h2 * 64 + 64)
            hc = slice(h * 64, (h + 1) * 64)
            self.mm(ps[2][:, hc], OPS[hp, p, 1, :], S0s[hp, p, :], start=True, stop=False)
            self.mm(ps[2][:, hc], AT[:, h, 128:256], Ub[:, hc], start=False, stop=False)
            self.mm(ps[2][:, hc], AT[:, h, 384:512], Vt[:, hc], start=False, stop=True)
        for h in range(8):
            p, h2 = h // 2, h % 2
            hp = slice(h2 * 64, h2 * 64 + 64)
            hc = slice(h * 64, (h + 1) * 64)
            self.mm(ps[3][hp, p * 64:(p + 1) * 64], W["Bt"][:, hc], Ub[:, hc], start=True, stop=False)
            self.mm(ps[3][hp, p * 64:(p + 1) * 64], W["Kt"][:, hc], Vt[:, hc], start=False, stop=True)
        self.tt("dve", Sst[:], Sst[:], emwl[:, :, 1:2].broadcast_to([128, 4, 64]), ALU.mult)
        self.tt("dve", Sst[:], Sst[:], ps[3][:, 0:256].rearrange("q (p i) -> q p i", p=4), ALU.add)
        if CUT == 5:
            return
        for p in range(4):
            self.mm(ps[4][:, p * 2:p * 2 + 2], prod[:, p, :], self.hsel[:, :])
        st8 = W["st8"]
        if d == 1:
            self.cp("dve", self.bonb[:, bi, :], ps[4][:, 0:8])
            self.cp("act", W["ysb"][:], ps[2][:, :])
            self.dma(self.yb_sc[t0:t0 + 128, :], W["ysb"][:])
            return
        ysb = W["ysb"]
        b8 = lambda ap: ap.unsqueeze(2).broadcast_to([128, 8, 64])
        y3 = lambda t: t[:].rearrange("q (h i) -> q h i", h=8)
        self.tt("dve", st8[:, 5, :], ps[4][:, 0:8], self.bonb[:, bi, :], ALU.add)
        self.tt("dve", ysb[:], ps[2][:, :], W["ybt"][:], ALU.add)
        for kc in range(8):
            self.mm(ps[5][:, :], hT[:, kc, 1 + t0:1 + t0 + 128], W["wcz"][:, kc, :], start=(kc == 0), stop=(kc == 7))
        self.act(W["sz"][:], ps[5][:, :], AF.Silu)
        sqy = W["sq"][:].rearrange("q p c -> q (p c)")
        yn = W["u"][:].rearrange("q p c -> q (p c)")
        yn3 = W["u"][:].rearrange("q p (a i) -> q (p a) i", a=2)
        self.red(st8[:, 0, :], y3(ysb))
        self.tt("pool", sqy, ysb[:], ysb[:], ALU.mult)
        self.red(st8[:, 1, :], W["sq"][:].rearrange("q p (a i) -> q (p a) i", a=2))
        self.ts("dve", st8[:, 0, :], st8[:, 0, :], 1.0 / 64, None, ALU.mult)
        self.tt("dve", st8[:, 2, :], st8[:, 0, :], st8[:, 0, :], ALU.mult)
        self.stt(st8[:, 3, :], st8[:, 1, :], 1.0 / 64, st8[:, 2, :], ALU.mult, ALU.subtract)
        self.act(st8[:, 4, :], st8[:, 3, :], AF.Sqrt, bias=float(GN_EPS))
        self.recip(st8[:, 4, :], st8[:, 4, :])
        self.tt("pool", yn3, y3(ysb), b8(st8[:, 0, :]), ALU.subtract)
        self.tt("pool", yn3, yn3, b8(st8[:, 4, :]), ALU.mult)
        self.tt("pool", yn, yn, W["lnx"][:, 0:512], ALU.mult)
        self.tt("pool", yn, yn, W["lnx"][:, 512:1024], ALU.add)
        self.tt("dve", y3(ysb), y3(W["vtok"]), b8(st8[:, 5, :]), ALU.mult)
        self.tt("pool", yn, yn, ysb[:], ALU.add)
        self.tt("dve", ysb[:], yn, W["sz"][:], ALU.mult)
        self.dma(self.yc_sc[t0:t0 + 128, :], ysb[:])

    def phase_Q(self, ctx, lo, hi):
        S = self.S
        l, kind, si, T, cond, last = ctx["l"], ctx["kind"], ctx["si"], ctx["T"], ctx["cond"], ctx["last"]
        KaT, Vaf, ckvT, KbT, Vbf, nk, nkt = (ctx[k] for k in ("KaT", "Vaf", "ckvT", "KbT", "Vbf", "nk", "nkt"))
        ps = self.ps
        TQ = min(T, 512)
        nst = TQ // 128
        ntile = T // TQ
        xin, xout = ctx["xin"], ctx["xout"]
        yout = (self.y_prompt[si] if kind == "p" else self.y_sample)

        def qa(name, shape, dt=F32):
            if lo.can(shape, dt):
                return lo.alloc(name, shape, dt)
            return hi.alloc(name, shape, dt)

        hTq = qa("hTq", [128, 8, TQ], BF16)
        yaT = qa("yaT", [64, 8, TQ], BF16)
        ybT = qa("ybT", [64, 8, TQ], BF16)
        ycT = qa("ycT", [128, 4, TQ], BF16)
        mlo_c, mhi_c = lo.mark(), hi.mark()
        for ti in range(ntile):
            tok0 = ti * TQ
            x_ap = xin[tok0:tok0 + TQ, :]
            qaT = qa("qaT", [128, 8, TQ], BF16)
            qbT = qa("qbT", [128, 8, TQ], BF16)
            self.memset("pool", qaT[:], 0.0)
            self.memset("pool", qbT[:], 0.0)
            mlo, mhi = lo.mark(), hi.mark()
            xt = [qa(f"qxt{i}", [128, D]) for i in range(2)]
            xn = [qa(f"qxn{i}", [128, D]) for i in range(2)]
            ss2 = qa("qss", [128, 2]); tm2 = qa("qtm", [128, 2]); rs2 = qa("qrs", [128, 2])
            waq = qa("waq", [128, 8, 512], BF16)
            wbq = qa("wbq", [128, 8, 768], BF16)
            qn = qa("qn", [128, 512]); qsq = qa("qsq", [128, 512]); qb = qa("qbf", [128, 768])
            yct = [qa(f"yct{i}", [128, 512]) for i in range(2)]
            rp = [qa(f"qrp{i}", [128, 192]) for i in range(2)]
            rtmp = qa("qrtmp", [128, 512])
            s8 = qa("qs8", [128, 3, 8])
            self.dma(waq[:], self.wsc[l][:, :, O_AQ:O_AQ + 512])
            self.dma(wbq[:], self.wsc[l][:, :, O_BQ:O_BQ + 768])
            self.compute_hT(ctx, hTq, 0, x_ap, nst, (xt, xn, ss2, tm2, rs2))
            for st in range(nst):
                sc = slice(st * 128, (st + 1) * 128)
                b = st % 2
                if kind == "s":
                    self.dma(rp[b][:], self.c_rope[tok0 + st * 128: tok0 + (st + 1) * 128, :])
                for kc in range(8):
                    self.mm(ps[0][:, :], hTq[:, kc, sc], waq[:, kc, :], start=(kc == 0), stop=(kc == 7))
                self.act(qsq[:], ps[0][:, :], AF.Square)
                self.red(s8[:, 0, :], qsq[:].rearrange("p (h d) -> p h d", h=8))
                self.rstd(s8[:, 1, :], s8[:, 0, :], 64, NORM_EPS, s8[:, 2, :])
                self.tt("dve", qn[:].rearrange("p (g kv d) -> p kv g d", g=4, kv=2),
                        ps[0][:, :].rearrange("p (kv g d) -> p kv g d", kv=2, g=4),
                        s8[:, 1, :].rearrange("p (kv g) -> p kv g", kv=2).unsqueeze(3).broadcast_to([128, 2, 4, 64]), ALU.mult)
                qn3 = qn[:].rearrange("p (h d) -> p h d", h=8)
                self.tt("pool", qn3, qn3, self.qkn_t[:, 0:64].unsqueeze(1).broadcast_to([128, 8, 64]), ALU.mult)
                if kind == "s":
                    self.rope(qn3, rp[b][:, 0:64], rp[b][:, 64:128], 8, 16, rtmp)
                for g in range(4):
                    self.tr(ps[1][:, g * 128:(g + 1) * 128], qn[:, g * 128:(g + 1) * 128])
                self.cp("act", qaT[0:64, 0:4, sc], ps[1][0:64, :].rearrange("q (g t) -> q g t", g=4))
                self.cp("dve", qaT[64:128, 4:8, sc], ps[1][64:128, :].rearrange("q (g t) -> q g t", g=4))
                for half in range(2):
                    for kc in range(8):
                        self.mm(ps[2 + half][:, 0:384], hTq[:, kc, sc], wbq[:, kc, half * 384:(half + 1) * 384],
                                start=(kc == 0), stop=(kc == 7))
                self.cp("act", qb[:, 0:384], ps[2][:, 0:384])
                self.cp("dve", qb[:, 384:768], ps[3][:, 0:384])
                qb3 = qb[:].rearrange("p (h c) -> p h c", h=8)
                if kind == "s":
                    self.rope(qb3[:, :, 64:96], rp[b][:, 128:160], rp[b][:, 160:192], 8, 8, rtmp)
                for h in range(8):
                    self.tr(ps[4 + h // 4][0:96, (h % 4) * 128:(h % 4 + 1) * 128], qb[:, h * 96:(h + 1) * 96])
                self.cp("act", qbT[0:96, 0:4, sc], ps[4][0:96, :].rearrange("q (g t) -> q g t", g=4))
                self.cp("dve", qbT[0:96, 4:8, sc], ps[5][0:96, :].rearrange("q (g t) -> q g t", g=4))
                self.dma(yct[b][:], self.yc_sc[tok0 + st * 128: tok0 + (st + 1) * 128, :])
                for p in range(4):
                    self.tr(ps[6][:, p * 128:(p + 1) * 128], yct[b][:, p * 128:(p + 1) * 128])
                self.cp("dve", ycT[:, :, sc], ps[6][:, :].rearrange("q (g t) -> q g t", g=4))
            S.barrier()
            lo.release(mlo); hi.release(mhi)
            PT = [qa(f"PT{i}", [128, TQ], BF16) for i in range(3)]
            den = qa("den", [65, TQ]); rden = qa("rden", [64, TQ]); szq = qa("szq", [64, TQ]); gq = qa("gq", [64, TQ])
            wz = [qa(f"wz{i}", [128, 8, 64], BF16) for i in range(2)]

            def finalize(gh, po, zoff, yT, h):
                self.cp("dve", den[64:65, :], po[64:65, 0:TQ])
                self.mm(ps[6][0:64, 0:TQ], self.ones[64:65, 0:64], den[64:65, :])
                for kc in range(8):
                    self.mm(ps[7][0:64, 0:TQ], wz[gh % 2][:, kc, :], hTq[:, kc, :], start=(kc == 0), stop=(kc == 7))
                self.act(szq[:, :], ps[7][0:64, 0:TQ], AF.Exp, scale=-1.0)
                self.stt(rden[:, :], szq[:, :], 1.0, ps[6][0:64, 0:TQ], ALU.add, ALU.mult)
                self.recip(rden[:, :], rden[:, :])
                self.tt("dve", gq[:, :], ps[7][0:64, 0:TQ], rden[:, :], ALU.mult)
                self.tt("dve", yT[:, h, :], po[0:64, 0:TQ], gq[:, :], ALU.mult)

            bscale = float(96 ** -0.5)
            steps = [("A", h, kt) for h in range(8) for kt in range(nkt)] + [("B", h, kt) for h in range(8) for kt in range(nkt)]
            N = len(steps)

            def build_kb(h):
                nkb = 0
                for kb in range(0, nk, 512):
                    w = min(512, nk - kb)
                    self.mm(ps[5][0:64, 0:w], self.uk[:, h * 64:(h + 1) * 64], ckvT[:, kb:kb + w])
                    self.cp("dve", KbT[0:64, h % 2, kb:kb + w], ps[5][0:64, 0:w])
                    nkb += 1

            def qk(i):
                mx, h, kt = steps[i]
                pS = ps[i % 3]
                gh = h if mx == "A" else 8 + h
                if kt == 0:
                    zo = O_AZ if mx == "A" else O_BZ
                    self.dma(wz[gh % 2][:], self.wsc[l][:, :, zo + h * 64:zo + (h + 1) * 64])
                    if mx == "A" and h == 7:
                        build_kb(0)
                    elif mx == "B" and h < 7:
                        build_kb(h + 1)
                if mx == "A":
                    self.mm(pS[:, 0:TQ], KaT[:, kt * 128:(kt + 1) * 128], qaT[:, h, :])
                else:
                    self.mm(pS[:, 0:TQ], KbT[:, h % 2, kt * 128:(kt + 1) * 128], qbT[:, h, :])

            def ex(i):
                mx = steps[i][0]
                self.act(PT[i % 3][:, :], ps[i % 3][:, 0:TQ], AF.Exp, scale=(0.125 if mx == "A" else bscale))

            def pv(i):
                mx, h, kt = steps[i]
                gh = h if mx == "A" else 8 + h
                po = ps[3 + gh % 2]
                if mx == "A":
                    o0 = kt * 130 + (h // 4) * 65
                    vv = Vaf[:, o0:o0 + 128]
                else:
                    o0 = kt * 520 + h * 65
                    vv = Vbf[:, o0:o0 + 128]
                self.mm(po[:, 0:TQ], vv, PT[i % 3][:, :], start=(kt == 0), stop=(kt == nkt - 1))
                if kt == nkt - 1:
                    finalize(gh, po, None, yaT if mx == "A" else ybT, h)

            qk(0)
            if N > 1:
                qk(1)
            for i in range(N):
                ex(i)
                if i + 2 < N:
                    qk(i + 2)
                pv(i)
            S.barrier()
            lo.release(mlo_c); hi.release(mhi_c)
            woa = [qa(f"woa{i}", [64, 8, 128], BF16) for i in range(2)]
            wob = [qa(f"wob{i}", [64, 8, 128], BF16) for i in range(2)]
            woc = [qa(f"woc{i}", [128, 4, 128], BF16) for i in range(2)]
            wg = [qa(f"wg{i}", [128, 8, 3, 128], BF16) for i in range(2)]
            wo_ = [qa(f"wout{i}", [128, 8, 128], BF16) for i in range(2)]
            sgm = [qa(f"sgm{i}", [128, TQ]) for i in range(2)]
            mix = qa("mix", [128, TQ]); mtmp = qa("mtmp", [128, TQ])
            mixT = qa("mixT", [128, 8, TQ], BF16)
            oT = [qa(f"oT{i}", [128, TQ]) for i in range(2)]
            xres = qa("xres", [128, nst, D])
            for st in range(nst):
                self.dma(xres[:, st, :], x_ap[st * 128:(st + 1) * 128, :])
            for dc in range(8):
                b = dc % 2
                dcs = slice(dc * 128, (dc + 1) * 128)
                self.dma(woa[b][:], self.wosc[l][0][:, :, dcs])
                self.dma(wob[b][:], self.wosc[l][1][:, :, dcs])
                self.dma(woc[b][:], self.wosc[l][2][:, :, dcs])
                for br in range(3):
                    self.dma(wg[b][:, :, br, :], self.wsc[l][:, :, O_G + br * 1024 + dc * 128:O_G + br * 1024 + (dc + 1) * 128])
                for br in range(3):
                    pp, pg = ps[br], ps[3 + br]
                    if br < 2:
                        wo, yT = (woa, wob)[br][b], (yaT, ybT)[br]
                        for h in range(8):
                            self.mm(pp[:, 0:TQ], wo[:, h, :], yT[:, h, :], start=(h == 0), stop=(h == 7))
                    else:
                        for p in range(4):
                            self.mm(pp[:, 0:TQ], woc[b][:, p, :], ycT[:, p, :], start=(p == 0), stop=(p == 3))
                    for kc in range(8):
                        self.mm(pg[:, 0:TQ], wg[b][:, kc, br, :], hTq[:, kc, :], start=(kc == 0), stop=(kc == 7))
                    sg_ = sgm[br % 2]
                    self.act(sg_[:, :], pg[:, 0:TQ], AF.Sigmoid)
                    if br == 0:
                        self.tt("dve", mix[:, :], pp[:, 0:TQ], sg_[:, :], ALU.mult)
                    else:
                        self.tt("dve", mtmp[:, :], pp[:, 0:TQ], sg_[:, :], ALU.mult)
                        dst = mix[:, :] if br == 1 else mixT[:, dc, :]
                        self.tt("pool", dst, mix[:, :], mtmp[:, :], ALU.add)
            for dc in range(8):
                b = dc % 2
                dcs = slice(dc * 128, (dc + 1) * 128)
                self.dma(wo_[b][:], self.woutsc[l][:, :, dcs])
                po = ps[6 + b]
                for kc in range(8):
                    self.mm(po[:, 0:TQ], wo_[b][:, kc, :], mixT[:, kc, :], start=(kc == 0), stop=(kc == 7))
                gsc = self.modG[:, l, cond, dc:dc + 1]
                if b == 0:
                    self.act(oT[b][:, :], po[:, 0:TQ], AF.Identity, scale=gsc)
                else:
                    self.ts("dve", oT[b][:, :], po[:, 0:TQ], gsc, None, ALU.mult)
                pt = ps[dc % 4]
                for st in range(nst):
                    self.tr(pt[:, st * 128:(st + 1) * 128], oT[b][:, st * 128:(st + 1) * 128])
                self.tt("dve", xres[:, :, dcs], xres[:, :, dcs],
                        pt[:, 0:nst * 128].rearrange("q (s f) -> q s f", s=nst), ALU.add)
            for st in range(nst):
                rows = slice(tok0 + st * 128, tok0 + (st + 1) * 128)
                if not last:
                    self.dma(xout[rows, :], xres[:, st, :])
                else:
                    fs = qa(f"fss{st}", [128, 4])
                    jv = mixT[:, 0:1024 // TQ, :].rearrange("q a t -> q (a t)")
                    self.act(jv, xres[:, st, :], AF.Square, accum=fs[:, 0:1])
                    self.rstd(fs[:, 2:3], fs[:, 0:1], D, NORM_EPS, fs[:, 1:2])
                    self.ts("dve", xres[:, st, :], xres[:, st, :], fs[:, 2:3], None, ALU.mult)
                    self.tt("pool", xres[:, st, :], xres[:, st, :], self.fnw_t[:], ALU.mult)
                    self.dma(yout[rows, :], xres[:, st, :], wk=[("yout", kind, si, tok0, st)])
            S.barrier()
            lo.release(mlo_c); hi.release(mhi_c)


def _in_maps(inputs):
    c = host_constants()
    f = lambda a: np.ascontiguousarray(np.asarray(a, dtype=np.float32))
    g = {k: f(v) for k, v in inputs.items()}
    nb = np.zeros((128, 128), np.float32)
    for l in range(DEPTH):
        nb[l * 32:l * 32 + 8] = g["norm_w"][l].reshape(8, 128)
        nb[l * 32 + 8:l * 32 + 32] = g["b_mod"][l].reshape(24, 128)
    lfm = np.zeros((DEPTH, 48, 128), np.float32)
    for l in range(DEPTH):
        lfm[l, 0:4] = g["c_k_k"][l].reshape(4, 128)
        lfm[l, 4:8] = g["c_k_a"][l].reshape(4, 128)
        lfm[l, 8:12] = g["c_r_k"][l].reshape(4, 128)
        lfm[l, 12:26] = g["c_mu_prev"][l].reshape(14, 128)
        lfm[l, 26:40] = g["c_mu_next"][l].reshape(14, 128)
        lfm[l, 40:48] = g["c_a0"][l].reshape(8, 128)
    qkn = np.concatenate([g["a_qnorm_w"], g["a_knorm_w"], g["b_kvnorm_w"]], axis=1)
    lnx = np.concatenate([g["c_lnx_w"], g["c_lnx_b"]], axis=1)
    shared = {
        "nb_rows": nb, "w_mod": g["w_mod"], "w_in": g["w_in"], "qkn": f(qkn),
        "b_w_uk": g["b_w_uk"], "b_w_uv": g["b_w_uv"], "lfm_rows": lfm, "c_w0": g["c_w0"],
        "c_w_up": f(g["c_w_up"].reshape(DEPTH, 128, 512)), "c_a_up": f(g["c_a_up"].reshape(DEPTH, 128, 512)),
        "lnx": f(lnx), "w_oa": g["w_oa"], "w_ob": g["w_ob"], "w_oc": g["w_oc"], "w_out": g["w_out"],
        "final_norm_w": g["final_norm_w"],
        "c_ident": c["ident"], "c_bank": c["bankmask"], "c_lvl": c["lvlmask"], "c_tri": c["tri"],
        "c_hsel": c["headsel"], "c_rope": c["rope"],
    }
    maps = []
    for i in range(NCORES):
        m = dict(shared)
        m["x_prompt"] = f(g["x_prompt"][i * NPS:(i + 1) * NPS])
        m["x_sample"] = f(g["x_sample"][i])
        m["cache_a_k"] = f(g["cache_a_k"][i].reshape(DEPTH, PAST, 128))
        m["cache_a_v"] = f(g["cache_a_v"][i].reshape(DEPTH, PAST, 128))
        m["cache_b_ckv"] = f(g["cache_b_ckv"][i])
        m["cache_b_krope"] = f(g["cache_b_krope"][i])
        m["state_c_fwd"] = f(g["state_c_fwd"][i])
        m["state_c_bwd"] = f(g["state_c_bwd"][i])
        m["cond"] = f(np.concatenate([g["c_ctx"].reshape(8, 128), g["c"][i].reshape(8, 128)], axis=0))
        maps.append(m)
    return maps


_NC_CACHE = {}


def kernel(**inputs):
    stage = int(os.environ.get("MK_STAGE", "99"))
    if stage not in _NC_CACHE:
        _NC_CACHE[stage] = Builder(stage).build()
    nc = _NC_CACHE[stage]
    maps = _in_maps(inputs)
    res = run_bass_kernel_spmd(nc, maps, core_ids=list(range(NCORES)))
    r = res.results
    cat = lambda name: np.concatenate([np.asarray(x[name]) for x in r], axis=0)
    y_prompt = cat("y_prompt")
    y_sample = np.stack([np.asarray(x["y_sample"]) for x in r], axis=0)
    new_a_k = cat("new_a_k").reshape(32, DEPTH, TP, 2, 64)
    new_a_v = cat("new_a_v").reshape(32, DEPTH, TP, 2, 64)
    new_ckv = cat("new_b_ckv")
    new_kr = cat("new_b_krope")
    new_sf = cat("new_c_state_fwd")
    new_sb = cat("new_c_state_bwd")
    return (y_prompt, y_sample, new_a_k, new_a_v, new_ckv, new_kr, new_sf, new_sb)
```

```python
import os
import contextlib
import numpy as np
import concourse.bass as bass
import concourse.mybir as mybir
from concourse.bass_utils import run_bass_kernel_spmd

F32 = mybir.dt.float32
BF16 = mybir.dt.bfloat16
AF = mybir.ActivationFunctionType
ALU = mybir.AluOpType
AX = mybir.AxisListType

D = 1024
DEPTH = 4
NPS = 4
TP = 256
TS = 4096
PAST = 512
IN_DIM = 8096
NCORES = 8
O_AQ, O_AK, O_AV, O_AZ, O_BQ, O_CKV, O_KR, O_BZ, O_CIN, O_CZ, O_G = 0, 512, 640, 768, 1280, 2048, 2176, 2208, 2720, 4512, 5024
NORM_EPS = 1e-6
GN_EPS = 64e-5
DEC_C = -float(np.exp(-0.5))

EPOCH = 30000
N_EPOCH = {"pe": 16, "dve": 12, "act": 10, "pool": 8}
N_DMA_SEMS = 40
SB_BASE = 16512
SB_LIMIT = 229344
CUT = int(os.environ.get('MK_CUT', '0'))


class Sched:
    def __init__(self, nc):
        self.nc = nc
        self.ops = {e: [] for e in ("pe", "dve", "act", "pool", "sp")}
        self.count = {e: 0 for e in ("pe", "dve", "act", "pool")}
        self.last_w = {}
        self.readers = {}
        self.dma_rr = 0
        self.dma_tot = [0] * N_DMA_SEMS
        self.sems = {}

    def _deps(self, eng, reads, writes):
        d = {}

        def add(tok, raw):
            sk, v, src = tok
            if src == eng and sk[0] != "dma":
                if eng == "pe":
                    return
            if d.get(sk, 0) < v:
                d[sk] = v

        for k in reads:
            w = self.last_w.get(k)
            if w is not None:
                add(w, True)
        for k in writes:
            w = self.last_w.get(k)
            if w is not None:
                add(w, False)
            for r in self.readers.get(k, {}).values():
                add(r, False)
        return d

    def _commit(self, tok, reads, writes):
        sk = tok[0]
        for k in reads:
            self.readers.setdefault(k, {})[sk] = tok
        for k in writes:
            self.last_w[k] = tok
            self.readers[k] = {}

    def op(self, eng, fn, reads=(), writes=()):
        deps = self._deps(eng, reads, writes)
        n = self.count[eng]
        self.count[eng] = n + 1
        ep, v = divmod(n, EPOCH)
        tok = ((eng, ep), v + 1, eng)
        self._commit(tok, reads, writes)
        self.ops[eng].append(("op", fn, deps, tok))
        return tok

    def dma(self, q, fn, reads=(), writes=()):
        deps = self._deps("dmaq", reads, writes)
        i = self.dma_rr
        self.dma_rr = (i + 1) % N_DMA_SEMS
        prev = self.dma_tot[i]
        if prev:
            sk = ("dma", i)
            if deps.get(sk, 0) < prev:
                deps[sk] = prev
        self.dma_tot[i] = prev + 16
        tok = (("dma", i), prev + 16, "dmaq")
        self._commit(tok, reads, writes)
        self.ops[q].append(("dma", fn, deps, tok))
        return tok

    def wait_all(self, eng, toks):
        deps = {}
        for sk, v, _ in toks:
            deps[sk] = max(deps.get(sk, 0), v)
        self.ops[eng].append(("wait", None, deps, None))

    def alloc_sems(self, es):
        nc = self.nc
        for e, ne in N_EPOCH.items():
            for ep in range(ne):
                self.sems[(e, ep)] = es.enter_context(nc.semaphore(f"s_{e}{ep}"))
        for i in range(N_DMA_SEMS):
            self.sems[("dma", i)] = es.enter_context(nc.semaphore(f"s_dma{i}"))

    def barrier(self):
        toks = []
        for e, n in self.count.items():
            if n:
                ep, v = divmod(n - 1, EPOCH)
                toks.append(((e, ep), v + 1, e))
        for i in range(N_DMA_SEMS):
            if self.dma_tot[i]:
                toks.append((("dma", i), self.dma_tot[i], "dmaq"))
        for e in ("pe", "dve", "act", "pool", "sp"):
            self.wait_all(e, toks)

    def emit_block(self):
        nc = self.nc
        sems = self.sems
        for e, ne in N_EPOCH.items():
            assert self.count[e] <= ne * EPOCH, (e, self.count[e])
        with nc.Block() as block:
            def run(engname):
                lst = self.ops[engname]

                def body(eng):
                    waited = {}
                    for kind, fn, deps, tok in lst:
                        for sk, v in deps.items():
                            if waited.get(sk, 0) < v:
                                eng.wait_ge(sems[sk], v)
                                waited[sk] = v
                        if kind == "wait":
                            continue
                        ins = fn(eng)
                        ins.then_inc(sems[tok[0]], 16 if kind == "dma" else 1)
                return body

            block.tensor(run("pe"))
            block.vector(run("dve"))
            block.scalar(run("act"))
            block.gpsimd(run("pool"))
            block.sync(run("sp"))
        self.ops = {e: [] for e in ("pe", "dve", "act", "pool", "sp")}


class Arena:
    cnt = 0

    def __init__(self, nc, base, limit):
        self.nc, self.off, self.limit = nc, base, limit

    def alloc(self, name, shape, dt):
        esz = 4 if dt == F32 else 2
        nb = int(np.prod(shape[1:])) * esz
        nb = (nb + 63) // 64 * 64
        off = self.off
        self.off += nb
        assert self.off <= self.limit, (name, self.off, self.limit)
        Arena.cnt += 1
        return self.nc.alloc_sbuf_tensor_at(f"{name}_{Arena.cnt}", list(shape), dt, offset=off)

    def can(self, shape, dt):
        esz = 4 if dt == F32 else 2
        nb = (int(np.prod(shape[1:])) * esz + 63) // 64 * 64
        return self.off + nb <= self.limit

    def mark(self):
        return self.off

    def release(self, m):
        self.off = m


def _k(x):
    return x if isinstance(x, (str, tuple)) else x.name


def host_constants():
    c = {}
    c["ident"] = np.eye(128, dtype=np.float32)
    s = np.arange(128)[:, None]
    t = np.arange(128)[None, :]
    bank = np.zeros((2, 128, 512), np.float32)
    lvl = np.zeros((2, 7, 128, 128), np.float32)
    tri = np.zeros((2, 128, 3 * 128 + 2), np.float32)
    for d in range(2):
        if d == 0:
            strict = (s < t).astype(np.float32)
            incl = (s <= t).astype(np.float32)
        else:
            strict = (s > t).astype(np.float32)
            incl = (s >= t).astype(np.float32)
        bank[d] = np.concatenate([strict, incl, strict, incl], axis=1)
        for l in range(7):
            b = 1 << l
            tt_ = np.arange(128)[:, None]
            ss_ = np.arange(128)[None, :]
            same = (tt_ // (2 * b)) == (ss_ // (2 * b))
            t_second = (tt_ // b) % 2 == 1
            s_second = (ss_ // b) % 2 == 1
            if d == 0:
                m = same & t_second & (~s_second)
            else:
                m = same & (~t_second) & s_second
            lvl[d, l] = m.astype(np.float32)
        if d == 0:
            G = (s <= t).astype(np.float32) - (s <= 63).astype(np.float32)
            G1 = (s < t).astype(np.float32) - (s <= 63).astype(np.float32)
            Dm = (s > t).astype(np.float32)
            cm = (np.arange(128) <= 63).astype(np.float32)
        else:
            G = (s >= t).astype(np.float32) - (s >= 64).astype(np.float32)
            G1 = (s > t).astype(np.float32) - (s >= 64).astype(np.float32)
            Dm = (s < t).astype(np.float32)
            cm = (np.arange(128) >= 64).astype(np.float32)
        tri[d, :, 0:128] = DEC_C * G
        tri[d, :, 128:256] = DEC_C * G1
        tri[d, :, 256:384] = DEC_C * Dm
        tri[d, :, 384] = DEC_C * cm
        tri[d, :, 385] = DEC_C
    c["bankmask"] = bank
    c["lvlmask"] = lvl
    c["tri"] = tri
    hs = np.zeros((128, 2), np.float32)
    hs[:64, 0] = 1.0
    hs[64:, 1] = 1.0
    c["headsel"] = hs
    tok = np.arange(TS)
    row = (tok // 64).astype(np.float32)
    col = (tok % 64).astype(np.float32)
    tab = np.zeros((TS, 192), np.float32)

    def fill(base_c, base_s, half):
        inv = 10000.0 ** (-np.arange(half, dtype=np.float32) / half)
        for pi, pos in enumerate((row, col)):
            ang = pos[:, None] * inv[None, :]
            co, si = np.cos(ang), np.sin(ang)
            o = pi * 2 * half
            tab[:, base_c + o: base_c + o + half] = co
            tab[:, base_c + o + half: base_c + o + 2 * half] = co
            tab[:, base_s + o: base_s + o + half] = -si
            tab[:, base_s + o + half: base_s + o + 2 * half] = si

    fill(0, 64, 16)
    fill(128, 160, 8)
    c["rope"] = tab
    return c


class Builder:
    def __init__(self, stage=99):
        self.stage = stage
        self.nc = bass.Bass("TRN2", target_bir_lowering=False)
        self.S = Sched(self.nc)
        self.din = {}
        self.dout = {}

    def inp(self, name, shape):
        self.din[name] = self.nc.dram_tensor(name, list(shape), F32, kind="ExternalInput").ap()
        return self.din[name]

    def outp(self, name, shape):
        self.dout[name] = self.nc.dram_tensor(name, list(shape), F32, kind="ExternalOutput").ap()
        return self.dout[name]

    def scratch(self, name, shape, dt):
        return self.nc.dram_tensor(name, list(shape), dt).ap()

    def dma(self, out, in_, rk=None, wk=None, q="sp"):
        r = [_k(in_)] if rk is None else rk
        w = [_k(out)] if wk is None else wk
        self.S.dma(q, lambda e: e.dma_start(out=out, in_=in_), r, w)

    def mm(self, out, lhsT, rhs, start=True, stop=True, xr=()):
        self.S.op("pe", lambda e: e.matmul(out, lhsT, rhs, start=start, stop=stop),
                  [_k(lhsT), _k(rhs)] + [_k(x) for x in xr], [_k(out)])

    def tr(self, out, in_):
        p = in_.shape[0]
        idn = self.ident[0:p, 0:p]
        self.S.op("pe", lambda e: e.transpose(out, in_, idn), [_k(in_), _k(self.ident)], [_k(out)])

    def act(self, out, in_, func, bias=None, scale=None, accum=None):
        r = [_k(in_)]
        kw = {}
        if bias is not None:
            kw["bias"] = bias
            if not isinstance(bias, float):
                r.append(_k(bias))
        if scale is not None:
            kw["scale"] = scale
            if not isinstance(scale, float):
                r.append(_k(scale))
        w = [_k(out)]
        if accum is not None:
            kw["accum_out"] = accum
            w.append(_k(accum))
        self.S.op("act", lambda e: e.activation(out, in_, func, **kw), r, w)

    def tt(self, eng, out, a, b, op):
        self.S.op(eng, lambda e: e.tensor_tensor(out, a, b, op), [_k(a), _k(b)], [_k(out)])

    def ts(self, eng, out, a, s1, s2, op0, op1=None):
        r = [_k(a)] + [_k(s) for s in (s1, s2) if s is not None and not isinstance(s, (float, int))]
        if op1 is None:
            self.S.op(eng, lambda e: e.tensor_scalar(out, a, s1, s2, op0), r, [_k(out)])
        else:
            self.S.op(eng, lambda e: e.tensor_scalar(out, a, s1, s2, op0, op1), r, [_k(out)])

    def stt(self, out, a, scalar, b, op0, op1):
        r = [_k(a), _k(b)] + ([] if isinstance(scalar, (float, int)) else [_k(scalar)])
        self.S.op("dve", lambda e: e.scalar_tensor_tensor(out, a, scalar, b, op0, op1), r, [_k(out)])

    def cp(self, eng, out, in_):
        if eng == "act":
            self.S.op("act", lambda e: e.copy(out, in_), [_k(in_)], [_k(out)])
        else:
            self.S.op(eng, lambda e: e.tensor_copy(out, in_), [_k(in_)], [_k(out)])

    def memset(self, eng, out, val):
        self.S.op(eng, lambda e: e.memset(out, val), [], [_k(out)])

    def red(self, out, in_, op=ALU.add):
        self.S.op("dve", lambda e: e.tensor_reduce(out, in_, AX.X, op), [_k(in_)], [_k(out)])

    def recip(self, out, in_):
        self.S.op("dve", lambda e: e.reciprocal(out, in_), [_k(in_)], [_k(out)])

    def rstd(self, out, ss, n, eps, tmp):
        self.act(tmp, ss, AF.Sqrt, bias=float(eps), scale=1.0 / n)
        self.recip(out, tmp)

    def declare(self):
        i = self.inp
        self.x_prompt = i("x_prompt", [NPS, TP, D])
        self.x_sample = i("x_sample", [TS, D])
        self.cache_a_k = i("cache_a_k", [DEPTH, PAST, 128])
        self.cache_a_v = i("cache_a_v", [DEPTH, PAST, 128])
        self.cache_ckv = i("cache_b_ckv", [DEPTH, PAST, 128])
        self.cache_kr = i("cache_b_krope", [DEPTH, PAST, 32])
        self.state_f = i("state_c_fwd", [DEPTH, 8, 64, 64])
        self.state_b = i("state_c_bwd", [DEPTH, 8, 64, 64])
        self.cond = i("cond", [16, 128])
        self.nb_rows = i("nb_rows", [128, 128])
        self.w_mod = i("w_mod", [DEPTH, D, 3 * D])
        self.w_in = i("w_in", [DEPTH, D, IN_DIM])
        self.qkn = i("qkn", [DEPTH, 256])
        self.b_w_uk = i("b_w_uk", [DEPTH, 128, 512])
        self.b_w_uv = i("b_w_uv", [DEPTH, 128, 512])
        self.lfm_rows = i("lfm_rows", [DEPTH, 48, 128])
        self.c_w0 = i("c_w0", [DEPTH, 2, 512])
        self.c_w_up = i("c_w_up", [DEPTH, 128, 512])
        self.c_a_up = i("c_a_up", [DEPTH, 128, 512])
        self.lnx = i("lnx", [DEPTH, 1024])
        self.w_oa = i("w_oa", [DEPTH, 512, D])
        self.w_ob = i("w_ob", [DEPTH, 512, D])
        self.w_oc = i("w_oc", [DEPTH, 512, D])
        self.w_out = i("w_out", [DEPTH, D, D])
        self.fnw = i("final_norm_w", [D])
        self.c_ident = i("c_ident", [128, 128])
        self.c_bank = i("c_bank", [2, 128, 512])
        self.c_lvl = i("c_lvl", [2, 7, 128, 128])
        self.c_tri = i("c_tri", [2, 128, 386])
        self.c_hsel = i("c_hsel", [128, 2])
        self.c_rope = i("c_rope", [TS, 192])
        o = self.outp
        self.y_prompt = o("y_prompt", [NPS, TP, D])
        self.y_sample = o("y_sample", [TS, D])
        self.new_a_k = o("new_a_k", [NPS, DEPTH, TP, 128])
        self.new_a_v = o("new_a_v", [NPS, DEPTH, TP, 128])
        self.new_ckv = o("new_b_ckv", [NPS, DEPTH, TP, 128])
        self.new_kr = o("new_b_krope", [NPS, DEPTH, TP, 32])
        self.new_sf = o("new_c_state_fwd", [NPS, DEPTH, 8, 64, 64])
        self.new_sb = o("new_c_state_bwd", [NPS, DEPTH, 8, 64, 64])
        sc = self.scratch
        self.wsc = [sc(f"wsc{l}", [128, 8, IN_DIM], BF16) for l in range(DEPTH)]
        self.wosc = [[sc(f"wo{l}_{b}", ([64, 8, D] if b < 2 else [128, 4, D]), BF16) for b in range(3)] for l in range(DEPTH)]
        self.woutsc = [sc(f"wout{l}", [128, 8, D], BF16) for l in range(DEPTH)]
        self.xs_p = sc("xs_p", [NPS, TP, D], F32)
        self.xs_s = sc("xs_s", [TS, D], F32)
        self.yb_sc = sc("yb_sc", [TS, 512], F32)
        self.yc_sc = sc("yc_sc", [TS, 512], F32)

    def build(self):
        nc, S = self.nc, self.S
        self.declare()
        with contextlib.ExitStack() as es:
            S.alloc_sems(es)
            self.ps = [es.enter_context(nc.psum_tensor(f"ps{i}", [128, 512], F32)) for i in range(8)]
            self.carena = Arena(nc, SB_BASE, SB_BASE + 24576)
            self.arena = Arena(nc, SB_BASE + 24576, SB_LIMIT)
            self.prologue()
            seqs = [("p", s) for s in range(NPS)] + [("s", 0)]
            if self.stage < 50:
                seqs = seqs[:NPS]
            nlayers = DEPTH if self.stage >= 40 else 1
            for l in range(nlayers):
                self.layer_prologue(l)
                for kind, si in seqs:
                    self.run_seq(l, kind, si)
            S.barrier()
            S.emit_block()
        return nc

    def prologue(self):
        A = self.carena
        S = self.S
        self.ident = A.alloc("ident", [128, 128], F32)
        self.identb = A.alloc("identb", [128, 128], BF16)
        self.ones = A.alloc("ones", [128, 128], F32)
        self.onesb = A.alloc("onesb", [128, 128], BF16)
        self.bank = A.alloc("bank", [128, 2, 512], BF16)
        self.lvl = A.alloc("lvl", [128, 14, 128], BF16)
        self.tri = A.alloc("tri", [128, 2, 386], F32)
        self.hsel = A.alloc("hsel", [128, 2], F32)
        self.modA = A.alloc("modA", [128, DEPTH, 2, 8], F32)
        self.modB = A.alloc("modB", [128, DEPTH, 2, 8], F32)
        self.modG = A.alloc("modG", [128, DEPTH, 2, 8], F32)
        self.qkn_t = A.alloc("qkn_t", [128, 256], F32)
        self.uk = A.alloc("uk", [128, 512], BF16)
        self.uv = A.alloc("uv", [128, 512], BF16)
        self.wup = A.alloc("wup", [128, 512], BF16)
        self.aup = A.alloc("aup", [128, 512], BF16)
        self.w0row = A.alloc("w0row", [1, 2, 512], BF16)
        self.lfm = A.alloc("lfm", [128, 48], F32)
        self.c0fm = A.alloc("c0fm", [128, 14], F32)
        self.bonb = A.alloc("bonb", [128, 32, 8], F32)
        self.fnw_t = A.alloc("fnw_t", [128, D], F32)
        self.bones = A.alloc("bones", [128, 128], F32)
        self.omka = A.alloc("omka", [128, 4], F32)
        ar = self.arena
        m0 = ar.mark()
        st = ar.alloc("st", [128, 128], F32)
        st2 = ar.alloc("st2", [128, 512], F32)
        lv = ar.alloc("lv", [128, 14, 128], F32)
        cst = ar.alloc("cst", [16, 128], F32)
        scond = ar.alloc("scond", [128, 16], F32)
        nbfm = ar.alloc("nbfm", [128, 128], F32)
        mods = ar.alloc("mods", [128, DEPTH, 24, 2], F32)
        wst = [ar.alloc(f"wst{i}", [128, 8, 512], F32) for i in range(2)]
        self.dma(self.ident[:], self.c_ident)
        self.cp("dve", self.identb[:], self.ident[:])
        self.memset("pool", self.ones[:], 1.0)
        self.memset("pool", self.onesb[:], 1.0)
        self.memset("pool", self.bones[:], 0.0)
        self.memset("pool", self.bones[0:64, 0:64], 1.0)
        self.memset("pool", self.bones[64:128, 64:128], 1.0)
        for d in range(2):
            self.dma(st2[:], self.c_bank[d])
            self.cp("dve", self.bank[:, d, :], st2[:])
        for d in range(2):
            self.dma(lv[:, 0:7, :], self.c_lvl[d].rearrange("l t s -> t l s"))
            self.cp("dve", self.lvl[:, d * 7:(d + 1) * 7, :], lv[:, 0:7, :])
        self.dma(self.tri[:], self.c_tri.rearrange("d s c -> s d c"))
        self.dma(self.hsel[:], self.c_hsel)
        self.dma(self.fnw_t[:], self.fnw.partition_broadcast(128))
        self.dma(cst[:], self.cond)
        self.tr(self.ps[0][:, 0:16], cst[:])
        self.act(scond[:], self.ps[0][:, 0:16], AF.Silu)
        self.dma(st[:], self.nb_rows)
        self.tr(self.ps[1][:, 0:128], st[:])
        self.cp("dve", nbfm[:], self.ps[1][:, 0:128])
        nwm = 0
        for l in range(DEPTH):
            pm = self.ps[2 + (l % 2)]
            for j in range(6):
                w = wst[nwm % 2]
                nwm += 1
                self.dma(w[:], self.w_mod[l].rearrange("(kc p) n -> p kc n", p=128)[:, :, j * 512:(j + 1) * 512])
                for f in range(4):
                    fc = j * 4 + f
                    for kc in range(8):
                        self.mm(pm[:, fc * 2:fc * 2 + 2], w[:, kc, f * 128:(f + 1) * 128],
                                scond[:, kc:16:8], start=(kc == 0), stop=(kc == 7))
            self.tt("dve", mods[:, l, :, :], pm[:, 0:48].rearrange("p (a b) -> p a b", b=2),
                    nbfm[:, l * 32 + 8:l * 32 + 32].unsqueeze(2).broadcast_to([128, 24, 2]), ALU.add)
            for c in range(2):
                self.ts("dve", self.modA[:, l, c, :], mods[:, l, 8:16, c], 1.0, None, ALU.add)
                self.tt("dve", self.modA[:, l, c, :], self.modA[:, l, c, :], nbfm[:, l * 32:l * 32 + 8], ALU.mult)
                self.cp("dve", self.modB[:, l, c, :], mods[:, l, 0:8, c])
                self.cp("dve", self.modG[:, l, c, :], mods[:, l, 16:24, c])
        S.barrier()
        S.emit_block()
        ar.release(m0)

    def layer_prologue(self, l):
        S, ar = self.S, self.arena
        m0 = ar.mark()
        wst = [ar.alloc(f"cst{i}", [128, 8, 512], F32) for i in range(2)]
        wbf = [ar.alloc(f"cbf{i}", [128, 8, 512], BF16) for i in range(2)]
        st = ar.alloc("lst", [128, 512], F32)
        st48 = ar.alloc("lst48", [48, 128], F32)
        n = 0
        engs = ["dve", "pool", "act"]
        src = self.w_in[l].rearrange("(kc p) n -> p kc n", p=128)
        for c0 in range(0, IN_DIM, 512):
            w = min(512, IN_DIM - c0)
            a, b = wst[n % 2], wbf[n % 2]
            self.dma(a[:, :, 0:w], src[:, :, c0:c0 + w])
            self.cp(engs[n % 3], b[:, :, 0:w], a[:, :, 0:w])
            self.dma(self.wsc[l][:, :, c0:c0 + w], b[:, :, 0:w])
            n += 1
        src = self.w_out[l].rearrange("(kc p) n -> p kc n", p=128)
        for c0 in range(0, D, 512):
            a, b = wst[n % 2], wbf[n % 2]
            self.dma(a[:], src[:, :, c0:c0 + 512])
            self.cp(engs[n % 3], b[:], a[:])
            self.dma(self.woutsc[l][:, :, c0:c0 + 512], b[:])
            n += 1
        for bi, wsrc in enumerate((self.w_oa, self.w_ob, self.w_oc)):
            if bi < 2:
                src = wsrc[l].rearrange("(h p) n -> p h n", p=64)
                np_, nh_ = 64, 8
            else:
                src = wsrc[l].rearrange("(kc p) n -> p kc n", p=128)
                np_, nh_ = 128, 4
            for c0 in range(0, D, 512):
                a, b = wst[n % 2], wbf[n % 2]
                self.dma(a[0:np_, 0:nh_, :], src[:, :, c0:c0 + 512])
                self.cp(engs[n % 3], b[0:np_, 0:nh_, :], a[0:np_, 0:nh_, :])
                self.dma(self.wosc[l][bi][:, :, c0:c0 + 512], b[0:np_, 0:nh_, :])
                n += 1
        for dst, srcw in ((self.uk, self.b_w_uk), (self.uv, self.b_w_uv), (self.wup, self.c_w_up), (self.aup, self.c_a_up)):
            self.dma(st[:], srcw[l])
            self.cp("dve", dst[:], st[:])
        self.dma(st[0:1, 0:512], self.c_w0[l, 0:1, :])
        self.cp("dve", self.w0row[:, 0, :], st[0:1, 0:512])
        self.dma(st[0:1, 0:512], self.c_w0[l, 1:2, :])
        self.cp("dve", self.w0row[:, 1, :], st[0:1, 0:512])
        self.dma(self.qkn_t[:], self.qkn[l].partition_broadcast(128))
        self.dma(st48[:], self.lfm_rows[l])
        self.tr(self.ps[0][:, 0:48], st48[:])
        self.cp("dve", self.lfm[:], self.ps[0][:, 0:48])
        self.tt("dve", self.c0fm[:], self.lfm[:, 12:26], self.lfm[:, 26:40], ALU.add)
        self.ts("dve", self.c0fm[:], self.c0fm[:], -1.0, 1.0, ALU.mult, ALU.add)
        self.ts("dve", self.omka[:], self.lfm[:, 4:8], -1.0, 1.0, ALU.mult, ALU.add)
        S.barrier()
        S.emit_block()
        ar.release(m0)

    def run_seq(self, l, kind, si):
        S, ar = self.S, self.arena
        T = TP if kind == "p" else TS
        cond = 0 if kind == "p" else 1
        last = (l == DEPTH - 1)
        if l == 0:
            xin = self.x_prompt[si] if kind == "p" else self.x_sample
        else:
            xin = self.xs_p[si] if kind == "p" else self.xs_s
        xout = self.xs_p[si] if kind == "p" else self.xs_s
        ctx = dict(l=l, kind=kind, si=si, T=T, cond=cond, xin=xin, xout=xout, last=last)
        m0 = ar.mark()
        hT = ar.alloc("hT", [128, 8, T + 2], BF16)
        mH = ar.mark()
        ctx["hT"] = hT
        self.phase_H(ctx)
        S.barrier()
        S.emit_block()
        if self.stage >= 20:
            self.phase_rwkv(ctx)
        self.phase_KV(ctx)
        S.barrier()
        S.emit_block()
        if self.stage >= 30:
            lo = Arena(self.nc, m0, mH)
            self.phase_Q(ctx, lo, ar)
            S.barrier()
            S.emit_block()
        ar.release(m0)

    def compute_hT(self, ctx, hT, col0, x_ap, nst, work):
        l, cond = ctx["l"], ctx["cond"]
        xt, xn, ss, tmp, rs = work
        for st in range(nst):
            b = st % 2
            self.dma(xt[b][:], x_ap[st * 128:(st + 1) * 128, :])
            self.act(xn[b][:], xt[b][:], AF.Square, accum=ss[:, b:b + 1])
            self.rstd(rs[:, b:b + 1], ss[:, b:b + 1], D, NORM_EPS, tmp[:, b:b + 1])
            self.ts("dve", xn[b][:], xt[b][:], rs[:, b:b + 1], None, ALU.mult)
            for half in range(2):
                pb = self.ps[(2 * st + half) % 4]
                for q in range(4):
                    kc = half * 4 + q
                    self.tr(pb[:, q * 128:(q + 1) * 128], xn[b][:, kc * 128:(kc + 1) * 128])
                for q in range(4):
                    kc = half * 4 + q
                    dst = hT[:, kc, col0 + st * 128: col0 + (st + 1) * 128]
                    if q % 2 == 0:
                        self.act(dst, pb[:, q * 128:(q + 1) * 128], AF.Identity,
                                 bias=self.modB[:, l, cond, kc:kc + 1], scale=self.modA[:, l, cond, kc:kc + 1])
                    else:
                        self.ts("dve", dst, pb[:, q * 128:(q + 1) * 128], self.modA[:, l, cond, kc:kc + 1],
                                self.modB[:, l, cond, kc:kc + 1], ALU.mult, ALU.add)

    def hT_work(self, ar):
        xt = [ar.alloc(f"xt{i}", [128, D], F32) for i in range(2)]
        xn = [ar.alloc(f"xn{i}", [128, D], F32) for i in range(2)]
        ss = ar.alloc("ss", [128, 2], F32)
        tmp = ar.alloc("tmpr", [128, 2], F32)
        rs = ar.alloc("rs", [128, 2], F32)
        return (xt, xn, ss, tmp, rs)

    def phase_H(self, ctx):
        ar = self.arena
        T, hT = ctx["T"], ctx["hT"]
        m = ar.mark()
        work = self.hT_work(ar)
        self.memset("pool", hT[:, :, 0:1], 0.0)
        self.memset("pool", hT[:, :, T + 1:T + 2], 0.0)
        self.compute_hT(ctx, hT, 1, ctx["xin"], T // 128, work)
        ar.release(m)

    def phase_KV(self, ctx):
        ar = self.arena
        l, kind, si, T, hT = ctx["l"], ctx["kind"], ctx["si"], ctx["T"], ctx["hT"]
        nk = T + (PAST if kind == "s" else 0)
        nkt = nk // 128
        KaT = ar.alloc("KaT", [128, nk], BF16)
        Vaf = ar.alloc("Va", [128, nkt * 130 + 64], BF16)
        ckvT = ar.alloc("ckvT", [128, nk], BF16)
        KbT = ar.alloc("KbT", [128, 2, nk], BF16)
        Vbf = ar.alloc("Vb", [128, nkt * 520 + 64], BF16)
        Va = Vaf[:, 0:nkt * 130].rearrange("p (t k c) -> p t k c", k=2, c=65)
        Vb = Vbf[:, 0:nkt * 520].rearrange("p (t k c) -> p t k c", k=8, c=65)
        ctx.update(Vaf=Vaf, Vbf=Vbf)
        ctx.update(KaT=KaT, Va=Va, ckvT=ckvT, KbT=KbT, Vb=Vb, nk=nk, nkt=nkt)
        m = ar.mark()
        wkv = ar.alloc("wkv", [128, 8, 416], BF16)
        kn = [ar.alloc(f"kn{i}", [128, 128], F32) for i in range(2)]
        vf = [ar.alloc(f"vf{i}", [128, 128], F32) for i in range(2)]
        cn = [ar.alloc(f"cn{i}", [128, 128], F32) for i in range(2)]
        krf = [ar.alloc(f"krf{i}", [128, 32], F32) for i in range(2)]
        junk = ar.alloc("kvjunk", [128, 128], F32)
        ss = ar.alloc("kvss", [128, 4], F32)
        tm = ar.alloc("kvtm", [128, 4], F32)
        rs = ar.alloc("kvrs", [128, 4], F32)
        rp = [ar.alloc(f"kvrp{i}", [128, 192], F32) for i in range(2)]
        rtmp = ar.alloc("kvrtmp", [128, 128], F32)
        self.memset("pool", ss[:], 0.0)
        self.memset("pool", Vaf[:], 0.0)
        self.memset("pool", Vbf[:, nkt * 520:nkt * 520 + 64], 0.0)
        self.memset("pool", KbT[:], 0.0)
        self.memset("pool", Va[:, :, :, 64:65], 1.0)
        self.memset("pool", Vb[:, :, :, 64:65], 1.0)
        self.dma(wkv[:, :, 0:256], self.wsc[l][:, :, O_AK:O_AK + 256])
        self.dma(wkv[:, :, 256:416], self.wsc[l][:, :, O_CKV:O_CKV + 160])
        qk = self.qkn_t
        for st in range(nkt):
            b = st % 2
            new = st < T // 128
            pA, pB, pT, pV = self.ps[0 + 4 * b], self.ps[1 + 4 * b], self.ps[2 + 4 * b], self.ps[3 + 4 * b]
            if new:
                hs = hT[:, :, 1 + st * 128: 1 + (st + 1) * 128]
                for kc in range(8):
                    self.mm(pA[:, 0:256], hs[:, kc, :], wkv[:, kc, 0:256], start=(kc == 0), stop=(kc == 7))
                for kc in range(8):
                    self.mm(pB[:, 0:160], hs[:, kc, :], wkv[:, kc, 256:416], start=(kc == 0), stop=(kc == 7))
                for h in range(2):
                    self.act(junk[:, 0:64], pA[:, h * 64:(h + 1) * 64], AF.Square, accum=ss[:, h:h + 1])
                self.rstd(rs[:, 0:2], ss[:, 0:2], 64, NORM_EPS, tm[:, 0:2])
                self.tt("dve", kn[b][:].rearrange("p (a c) -> p a c", a=2), pA[:, 0:128].rearrange("p (a c) -> p a c", a=2),
                        rs[:, 0:2].unsqueeze(2).broadcast_to([128, 2, 64]), ALU.mult)
                self.tt("pool", kn[b][:].rearrange("p (a c) -> p a c", a=2), kn[b][:].rearrange("p (a c) -> p a c", a=2),
                        qk[:, 64:128].unsqueeze(1).broadcast_to([128, 2, 64]), ALU.mult)
                self.cp("act", vf[b][:], pA[:, 128:256])
                self.act(junk[:], pB[:, 0:128], AF.Square, accum=ss[:, 2:3])
                self.rstd(rs[:, 2:3], ss[:, 2:3], 128, NORM_EPS, tm[:, 2:3])
                self.ts("dve", cn[b][:], pB[:, 0:128], rs[:, 2:3], None, ALU.mult)
                self.tt("pool", cn[b][:], cn[b][:], qk[:, 128:256], ALU.mult)
                self.cp("act", krf[b][:], pB[:, 128:160])
                if kind == "p":
                    self.dma(self.new_a_k[si, l, st * 128:(st + 1) * 128, :], kn[b][:], wk=[("nak", si, l, st)])
                    self.dma(self.new_a_v[si, l, st * 128:(st + 1) * 128, :], vf[b][:], wk=[("nav", si, l, st)])
                    self.dma(self.new_ckv[si, l, st * 128:(st + 1) * 128, :], cn[b][:], wk=[("nck", si, l, st)])
                    self.dma(self.new_kr[si, l, st * 128:(st + 1) * 128, :], krf[b][:], wk=[("nkr", si, l, st)])
                else:
                    self.dma(rp[b][:], self.c_rope[st * 128:(st + 1) * 128, :])
                    self.rope(kn[b][:].rearrange("p (h c) -> p h c", h=2), rp[b][:, 0:64], rp[b][:, 64:128], 2, 16, rtmp)
                    self.rope(krf[b][:].rearrange("p (h c) -> p h c", h=1), rp[b][:, 128:160], rp[b][:, 160:192], 1, 8, rtmp)
            else:
                c0 = (st - T // 128) * 128
                self.dma(kn[b][:], self.cache_a_k[l, c0:c0 + 128, :])
                self.dma(vf[b][:], self.cache_a_v[l, c0:c0 + 128, :])
                self.dma(cn[b][:], self.cache_ckv[l, c0:c0 + 128, :])
                self.dma(krf[b][:], self.cache_kr[l, c0:c0 + 128, :])
            ks = slice(st * 128, (st + 1) * 128)
            self.tr(pT[:, 0:128], kn[b][:])
            self.cp("act", KaT[:, ks], pT[:, 0:128])
            self.tr(pT[:, 128:256], cn[b][:])
            self.cp("dve", ckvT[:, ks], pT[:, 128:256])
            self.mm(pT[64:96, 256:384], krf[b][:], self.ident[:], start=True, stop=True)
            self.cp("act", KbT[64:96, 0, ks], pT[64:96, 256:384])
            self.cp("pool", KbT[64:96, 1, ks], KbT[64:96, 0, ks])
            self.cp("pool", Va[:, st, :, 0:64], vf[b][:].rearrange("p (a c) -> p a c", a=2))
            self.mm(pV[:, :], ckvT[:, ks], self.uv[:], start=True, stop=True)
            self.cp("act" if st % 2 else "dve", Vb[:, st, :, 0:64], pV[:, :].rearrange("p (a c) -> p a c", a=8))
        ar.release(m)

    def rope(self, x3, cc, ss_, nh, half, tmp):
        w = 4 * half
        t3 = tmp[:, 0:nh * w].rearrange("p (h c) -> p h c", h=nh)
        for a in range(2):
            for hb in range(2):
                o = a * 2 * half + hb * half
                o2 = a * 2 * half + (1 - hb) * half
                self.tt("pool", t3[:, :, o:o + half], x3[:, :, o2:o2 + half],
                        ss_[:, o:o + half].unsqueeze(1).broadcast_to([128, nh, half]), ALU.mult)
        self.tt("dve", x3, x3, cc.unsqueeze(1).broadcast_to([128, nh, w]), ALU.mult)
        self.tt("dve", x3, x3, t3, ALU.add)

    def phase_rwkv(self, ctx):
        S, ar = self.S, self.arena
        l, kind, si, T, hT = ctx["l"], ctx["kind"], ctx["si"], ctx["T"], ctx["hT"]
        nsb = T // 256
        m = ar.mark()
        W = {}
        a = lambda n, shp, dt=F32: W.__setitem__(n, ar.alloc(n, shp, dt))
        a("wc", [128, 8, 1792], BF16)
        a("wcz", [128, 8, 512], BF16)
        a("lnx", [128, 1024])
        for n in ("rT", "kT", "vT"):
            a(n, [128, 4, 256])
        for n in ("kq", "sq", "tmpf", "kk", "bT", "kdT", "asg", "u"):
            a(n, [128, 4, 128])
        for n in ("sg", "Ep", "Em", "E1", "ED", "ybt"):
            a(n, [128, 512])
        W["ysb"] = W["Ep"]; W["sz"] = W["Em"]; W["vtok"] = W["E1"]
        a("t1a", [128, 256]); a("t1b", [128, 256]); a("wdT", [128, 256]); a("adT", [128, 256])
        a("tw", [128, 256], BF16); a("adb", [128, 256], BF16)
        a("OPS", [128, 4, 4, 128], BF16)
        a("Bt", [128, 512], BF16); a("Kt", [128, 512], BF16); a("Vt", [128, 512], BF16)
        a("AT", [128, 8, 512], BF16); a("ATr", [128, 512], BF16)
        a("Tm", [128, 8, 128], BF16); a("Zm", [128, 8, 128], BF16); a("Qm", [128, 8, 128], BF16)
        a("S", [128, 4, 64]); a("S0s", [128, 4, 64], BF16); a("emwl", [128, 4, 2])
        a("Xb0", [128, 256], BF16); a("Xb1", [128, 256], BF16); a("Ub0", [128, 256], BF16); a("Ub1", [128, 256], BF16)
        a("st8", [128, 6, 8]); a("sti", [64, 4, 128])
        for c0 in range(0, 1792, 448):
            self.dma(W["wc"][:, :, c0:c0 + 448], self.wsc[l][:, :, O_CIN + c0:O_CIN + c0 + 448])
        self.dma(W["wcz"][:], self.wsc[l][:, :, O_CZ:O_CZ + 512])
        self.dma(W["lnx"][:], self.lnx[l].partition_broadcast(128))
        for d in (1, 0):
            Sst = W["S"]
            if kind == "p":
                self.memset("pool", Sst[:], 0.0)
            else:
                src = (self.state_f if d == 0 else self.state_b)[l]
                self.dma(W["sti"][:].rearrange("i p (a j) -> i p a j", a=2), src.rearrange("(p a) i j -> i p a j", a=2))
                for p in range(4):
                    self.tr(self.ps[0][:, p * 64:(p + 1) * 64], W["sti"][:, p, :])
                self.cp("dve", Sst[:].rearrange("q p i -> q (p i)"), self.ps[0][:, 0:256])
            order = range(nsb - 1, -1, -1) if d == 1 else range(nsb)
            for sb in order:
                self.rwkv_super(ctx, d, sb, W)
            if kind == "p":
                dst = (self.new_sf if d == 0 else self.new_sb)[si, l]
                for p in range(4):
                    self.tr(self.ps[0][0:64, p * 128:(p + 1) * 128], Sst[:, p, :])
                self.cp("dve", W["sti"][:].rearrange("i p c -> i (p c)"), self.ps[0][0:64, :])
                self.dma(dst.rearrange("(p a) i j -> i p a j", a=2), W["sti"][:].rearrange("i p (a j) -> i p a j", a=2),
                         wk=[("nst", d, si, l)])
            S.barrier()
            S.emit_block()
        ar.release(m)

    def rwkv_super(self, ctx, d, sb, W):
        hT = ctx["hT"]
        ps, lfm = self.ps, self.lfm
        ts0 = sb * 256
        rT, kT, vT = W["rT"], W["kT"], W["vT"]
        for ci in range(14):
            pb = ps[ci % 2]
            for kc in range(8):
                self.mm(pb[:, 0:258], W["wc"][:, kc, ci * 128:(ci + 1) * 128], hT[:, kc, ts0:ts0 + 258],
                        start=(kc == 0), stop=(kc == 7))
            if ci < 12:
                dst = (rT, kT, vT)[ci // 4][:, ci % 4, :]
            else:
                dst = (W["wdT"], W["adT"])[ci - 12][:]
            t1 = W["t1a" if ci % 2 == 0 else "t1b"]
            self.act(t1[:], pb[:, 1:257], AF.Identity, scale=self.c0fm[:, ci:ci + 1])
            self.stt(t1[:], pb[:, 0:256], lfm[:, 12 + ci:13 + ci], t1[:], ALU.mult, ALU.add)
            self.stt(dst, pb[:, 2:258], lfm[:, 26 + ci:27 + ci], t1[:], ALU.mult, ALU.add)
        self.act(W["tw"][:], W["wdT"][:], AF.Tanh)
        self.cp("pool", W["adb"][:], W["adT"][:])
        for blk in ((1, 0) if d == 1 else (0, 1)):
            self.rwkv_block(ctx, d, sb * 2 + blk, blk, W)

    def rwkv_block(self, ctx, d, bi, blk, W):
        l, kind, T, hT = ctx["l"], ctx["kind"], ctx["T"], ctx["hT"]
        ps = self.ps
        t0 = bi * 128
        bs = slice(blk * 128, (blk + 1) * 128)
        lfm = self.lfm
        bc3 = lambda ap, n: ap.unsqueeze(2).broadcast_to([128, ap.shape[1], n])
        v3 = lambda t: t[:].rearrange("q (p c) -> q p c", p=4)
        rT, kT, vT = W["rT"][:, :, bs], W["kT"][:, :, bs], W["vT"][:, :, bs]
        tw, adb = W["tw"][:, bs], W["adb"][:, bs]
        if d == 0:
            self.dma(W["ybt"][:], self.yb_sc[t0:t0 + 128, :])
        hd = slice(d * 64, (d + 1) * 64)
        self.mm(ps[2][:, :], tw[hd, :], self.wup[hd, :], start=True, stop=False)
        self.mm(ps[2][:, :], self.onesb[0:1, 0:128], self.w0row[0:1, d, :], start=False, stop=True)
        self.act(W["sg"][:], ps[2][:, :], AF.Sigmoid)
        for p in range(4):
            self.mm(ps[3][:, p * 128:(p + 1) * 128], self.aup[hd, p * 128:(p + 1) * 128], adb[hd, :])
        for p in range(4):
            self.act(W["asg"][:, p, :], ps[3][:, p * 128:(p + 1) * 128], AF.Sigmoid, bias=lfm[:, 40 + d * 4 + p:41 + d * 4 + p])
        self.tt("dve", W["kq"][:], kT, bc3(lfm[:, 0:4], 128), ALU.mult)
        self.tt("dve", W["sq"][:], W["kq"][:], W["kq"][:], ALU.mult)
        self.mm(ps[4][:, :], self.bones[:], W["sq"][:].rearrange("q p c -> q (p c)"))
        self.act(W["tmpf"][:].rearrange("q p c -> q (p c)"), ps[4][:, :], AF.Ln, bias=1e-24)
        self.act(W["tmpf"][:], W["tmpf"][:], AF.Exp, scale=-0.5)
        self.tt("dve", W["kk"][:], W["kq"][:], W["tmpf"][:], ALU.mult)
        self.tt("dve", W["bT"][:], W["kk"][:], W["asg"][:], ALU.mult)
        self.tt("pool", W["u"][:], W["asg"][:], bc3(lfm[:, 4:8], 128), ALU.mult)
        self.tt("pool", W["u"][:], W["u"][:], bc3(self.omka[:, 0:4], 128), ALU.add)
        self.tt("pool", W["kdT"][:], kT, W["u"][:], ALU.mult)
        if CUT == 1:
            return
        tri = self.tri
        first, lastt = (0, 127) if d == 0 else (127, 0)
        for p in range(4):
            pc = ps[5 + p // 2]
            self.mm(pc[:, (p % 2) * 256:(p % 2) * 256 + 256], W["sg"][:, p * 128:(p + 1) * 128], tri[:, d, 0:256])
        self.mm(ps[7][:, :], tri[:, d, 256:384], W["sg"][:])
        emwl = W["emwl"]
        for hf in range(2):
            pc = ps[5 + hf]
            pv_ = pc[:, :].rearrange("q (p k t) -> q p k t", p=2, k=2)
            hs_ = slice(hf * 256, (hf + 1) * 256)
            self.act(v3(W["Ep"])[:, hf * 2:hf * 2 + 2, :], pv_[:, :, 0, :], AF.Exp)
            self.act(v3(W["Em"])[:, hf * 2:hf * 2 + 2, :], pv_[:, :, 0, :], AF.Exp, scale=-1.0)
            self.act(v3(W["E1"])[:, hf * 2:hf * 2 + 2, :], pv_[:, :, 1, :], AF.Exp)
            self.act(emwl[:, hf * 2:hf * 2 + 2, 0], pv_[:, :, 1, first], AF.Exp, scale=-1.0)
        self.act(W["ED"][:], ps[7][:, :], AF.Exp)
        self.tt("dve", emwl[:, :, 1], emwl[:, :, 0], v3(W["Ep"])[:, :, lastt], ALU.mult)
        OPS = W["OPS"]
        self.stt(OPS[:, :, 0, :], W["kk"][:], -1.0, v3(W["E1"]), ALU.mult, ALU.mult)
        self.tt("pool", OPS[:, :, 1, :], rT, v3(W["Ep"]), ALU.mult)
        self.tt("pool", OPS[:, :, 2, :], W["bT"][:], v3(W["Em"]), ALU.mult)
        self.tt("dve", OPS[:, :, 3, :], W["kdT"][:], v3(W["Em"]), ALU.mult)
        prod = W["kq"]
        self.tt("pool", prod[:], rT, W["kdT"][:], ALU.mult)
        self.tt("pool", prod[:], prod[:], bc3(lfm[:, 8:12], 128), ALU.mult)
        if CUT == 2:
            return
        for p in range(4):
            self.tr(ps[5][:, p * 128:(p + 1) * 128], W["bT"][:, p, :])
        self.tt("dve", W["Bt"][:], ps[5][:, :], W["ED"][:], ALU.mult)
        for p in range(4):
            self.tr(ps[6][:, p * 128:(p + 1) * 128], W["kdT"][:, p, :])
        self.tt("dve", W["Kt"][:], ps[6][:, :], W["ED"][:], ALU.mult)
        for p in range(4):
            self.tr(ps[7][:, p * 128:(p + 1) * 128], vT[:, p, :])
        self.cp("act", W["Vt"][:], ps[7][:, :])
        if d == 0:
            self.cp("act", W["vtok"][:], ps[7][:, :])
        AT = W["AT"]
        for h in range(8):
            p, h2 = h // 2, h % 2
            hp = slice(h2 * 64, h2 * 64 + 64)
            pa = ps[h % 4]
            self.mm(pa[:, 0:256], OPS[hp, p, 2, :], OPS[hp, p, 0:2, :])
            self.mm(pa[:, 256:512], OPS[hp, p, 3, :], OPS[hp, p, 0:2, :])
            if h % 2 == 0:
                self.tt("dve", AT[:, h, :], pa[:, :], self.bank[:, d, :], ALU.mult)
            else:
                self.cp("act", W["ATr"][:], pa[:, :])
                self.tt("pool", AT[:, h, :], W["ATr"][:], self.bank[:, d, :], ALU.mult)
        if CUT == 3:
            return
        Tm, Zm, Qm = W["Tm"], W["Zm"], W["Qm"]
        idb = self.identb[:].unsqueeze(1).broadcast_to([128, 8, 128])
        self.cp("pool", Tm[:], idb)
        self.cp("pool", Zm[:], idb)
        lvb = lambda lev: self.lvl[:, d * 7 + lev, :].unsqueeze(1).broadcast_to([128, 4, 128])
        v4 = lambda pb: pb[:, :].rearrange("q (c s) -> q c s", c=4)
        for lev in range(7):
            for g in range(2):
                pq = ps[2 + 3 * g]
                for c in range(4):
                    h = 4 * g + c
                    self.mm(pq[:, c * 128:(c + 1) * 128], AT[:, h, 0:128], Tm[:, h, :])
            for g in range(2):
                gs = slice(4 * g, 4 * g + 4)
                self.tt("dve", Qm[:, gs, :], v4(ps[2 + 3 * g]), lvb(lev), ALU.mult)
            for g in range(2):
                pt, pz = ps[3 + 3 * g], ps[4 + 3 * g]
                for c in range(4):
                    h = 4 * g + c
                    if 0 < lev < 6:
                        self.mm(pt[:, c * 128:(c + 1) * 128], Zm[:, h, :], Qm[:, h, :])
                    self.mm(pz[:, c * 128:(c + 1) * 128], Qm[:, h, :], Zm[:, h, :])
            for g in range(2):
                gs = slice(4 * g, 4 * g + 4)
                if lev == 0:
                    self.tt("dve", Tm[:, gs, :], Qm[:, gs, :], Tm[:, gs, :], ALU.add)
                elif lev < 6:
                    self.tt("dve", Tm[:, gs, :], v4(ps[3 + 3 * g]), Tm[:, gs, :], ALU.add)
                self.tt("dve", Zm[:, gs, :], v4(ps[4 + 3 * g]), Zm[:, gs, :], ALU.add)
        if CUT == 4:
            return
        Sst, S0s = W["S"], W["S0s"]
        Vt = W["Vt"]
        self.tt("dve", S0s[:], Sst[:], emwl[:, :, 0:1].broadcast_to([128, 4, 64]), ALU.mult)
        XB = (W["Xb0"], W["Xb1"])
        UB = (W["Ub0"], W["Ub1"])
        for g in range(2):
            px = ps[0 + 5 * g]
            for h in range(4 * g, 4 * g + 4):
                p, h2 = h // 2, h % 2
                hp = slice(h2 * 64, h2 * 64 + 64)
                hc = slice((h % 4) * 64, (h % 4 + 1) * 64)
                hcf = slice(h * 64, (h + 1) * 64)
                self.mm(px[:, hc], OPS[hp, p, 0, :], S0s[hp, p, :], start=True, stop=False)
                self.mm(px[:, hc], AT[:, h, 256:384], Vt[:, hcf], start=False, stop=True)
            self.cp("act", XB[g][:], px[:, 0:256])
        for g in range(2):
            pu = ps[1 + 5 * g]
            for h in range(4 * g, 4 * g + 4):
                hc = slice((h % 4) * 64, (h % 4 + 1) * 64)
                self.mm(pu[:, hc], Zm[:, h, :], XB[g][:, hc])
            self.cp("dve" if g else "act", UB[g][:], pu[:, 0:256])
        for h in range(8):
            p, h2 = h // 2, h % 2
            hp = slice(h2 * 64, h2 * 64 + 64)
            hc = slice(h * 64, (h + 1) * 64)
            hq = slice((h % 4) * 64, (h % 4 + 1) * 64)
            self.mm(ps[2][:, hc], OPS[hp, p, 1, :], S0s[hp, p, :], start=True, stop=False)
            self.mm(ps[2][:, hc], AT[:, h, 128:256], UB[h // 4][:, hq], start=False, stop=False)
            self.mm(ps[2][:, hc], AT[:, h, 384:512], Vt[:, hc], start=False, stop=True)
        for h in range(8):
            p, h2 = h // 2, h % 2
            hp = slice(h2 * 64, h2 * 64 + 64)
            hc = slice(h * 64, (h + 1) * 64)
            hq = slice((h % 4) * 64, (h % 4 + 1) * 64)
            self.mm(ps[3][hp, p * 64:(p + 1) * 64], W["Bt"][:, hc], UB[h // 4][:, hq], start=True, stop=False)
            self.mm(ps[3][hp, p * 64:(p + 1) * 64], W["Kt"][:, hc], Vt[:, hc], start=False, stop=True)
        self.tt("dve", Sst[:], Sst[:], emwl[:, :, 1:2].broadcast_to([128, 4, 64]), ALU.mult)
        self.tt("dve", Sst[:], Sst[:], ps[3][:, 0:256].rearrange("q (p i) -> q p i", p=4), ALU.add)
        if CUT == 5:
            return
        for p in range(4):
            self.mm(ps[4][:, p * 2:p * 2 + 2], prod[:, p, :], self.hsel[:, :])
        st8 = W["st8"]
        if d == 1:
            self.cp("dve", self.bonb[:, bi, :], ps[4][:, 0:8])
            self.cp("act", W["ysb"][:], ps[2][:, :])
            self.dma(self.yb_sc[t0:t0 + 128, :], W["ysb"][:])
            return
        ysb = W["ysb"]
        b8 = lambda ap: ap.unsqueeze(2).broadcast_to([128, 8, 64])
        y3 = lambda t: t[:].rearrange("q (h i) -> q h i", h=8)
        self.tt("dve", st8[:, 5, :], ps[4][:, 0:8], self.bonb[:, bi, :], ALU.add)
        self.tt("dve", ysb[:], ps[2][:, :], W["ybt"][:], ALU.add)
        for kc in range(8):
            self.mm(ps[5][:, :], hT[:, kc, 1 + t0:1 + t0 + 128], W["wcz"][:, kc, :], start=(kc == 0), stop=(kc == 7))
        self.act(W["sz"][:], ps[5][:, :], AF.Silu)
        sqy = W["sq"][:].rearrange("q p c -> q (p c)")
        yn = W["u"][:].rearrange("q p c -> q (p c)")
        yn3 = W["u"][:].rearrange("q p (a i) -> q (p a) i", a=2)
        self.red(st8[:, 0, :], y3(ysb))
        self.tt("pool", sqy, ysb[:], ysb[:], ALU.mult)
        self.red(st8[:, 1, :], W["sq"][:].rearrange("q p (a i) -> q (p a) i", a=2))
        self.ts("dve", st8[:, 0, :], st8[:, 0, :], 1.0 / 64, None, ALU.mult)
        self.tt("dve", st8[:, 2, :], st8[:, 0, :], st8[:, 0, :], ALU.mult)
        self.stt(st8[:, 3, :], st8[:, 1, :], 1.0 / 64, st8[:, 2, :], ALU.mult, ALU.subtract)
        self.act(st8[:, 4, :], st8[:, 3, :], AF.Sqrt, bias=float(GN_EPS))
        self.recip(st8[:, 4, :], st8[:, 4, :])
        self.tt("pool", yn3, y3(ysb), b8(st8[:, 0, :]), ALU.subtract)
        self.tt("pool", yn3, yn3, b8(st8[:, 4, :]), ALU.mult)
        self.tt("pool", yn, yn, W["lnx"][:, 0:512], ALU.mult)
        self.tt("pool", yn, yn, W["lnx"][:, 512:1024], ALU.add)
        self.tt("dve", y3(ysb), y3(W["vtok"]), b8(st8[:, 5, :]), ALU.mult)
        self.tt("pool", yn, yn, ysb[:], ALU.add)
        self.tt("dve", ysb[:], yn, W["sz"][:], ALU.mult)
        self.dma(self.yc_sc[t0:t0 + 128, :], ysb[:])

    def phase_Q(self, ctx, lo, hi):
        S = self.S
        l, kind, si, T, cond, last = ctx["l"], ctx["kind"], ctx["si"], ctx["T"], ctx["cond"], ctx["last"]
        KaT, Vaf, ckvT, KbT, Vbf, nk, nkt = (ctx[k] for k in ("KaT", "Vaf", "ckvT", "KbT", "Vbf", "nk", "nkt"))
        ps = self.ps
        TQ = min(T, 512)
        nst = TQ // 128
        ntile = T // TQ
        xin, xout = ctx["xin"], ctx["xout"]
        yout = (self.y_prompt[si] if kind == "p" else self.y_sample)

        def qa(name, shape, dt=F32):
            if lo.can(shape, dt):
                return lo.alloc(name, shape, dt)
            return hi.alloc(name, shape, dt)

        hTq = qa("hTq", [128, 8, TQ], BF16)
        yaT = qa("yaT", [64, 8, TQ], BF16)
        ybT = qa("ybT", [64, 8, TQ], BF16)
        ycT = qa("ycT", [128, 4, TQ], BF16)
        mlo_c, mhi_c = lo.mark(), hi.mark()
        for ti in range(ntile):
            tok0 = ti * TQ
            x_ap = xin[tok0:tok0 + TQ, :]
            qaT = qa("qaT", [128, 8, TQ], BF16)
            qbT = qa("qbT", [128, 8, TQ], BF16)
            self.memset("pool", qaT[:], 0.0)
            self.memset("pool", qbT[:], 0.0)
            mlo, mhi = lo.mark(), hi.mark()
            xt = [qa(f"qxt{i}", [128, D]) for i in range(2)]
            xn = [qa(f"qxn{i}", [128, D]) for i in range(2)]
            ss2 = qa("qss", [128, 2]); tm2 = qa("qtm", [128, 2]); rs2 = qa("qrs", [128, 2])
            waq = qa("waq", [128, 8, 512], BF16)
            wbq = qa("wbq", [128, 8, 768], BF16)
            qn = qa("qn", [128, 512]); qsq = qa("qsq", [128, 512]); qb = qa("qbf", [128, 768])
            yct = [qa(f"yct{i}", [128, 512]) for i in range(2)]
            rp = [qa(f"qrp{i}", [128, 192]) for i in range(2)]
            rtmp = qa("qrtmp", [128, 512])
            s8 = qa("qs8", [128, 3, 8])
            self.dma(waq[:], self.wsc[l][:, :, O_AQ:O_AQ + 512])
            self.dma(wbq[:], self.wsc[l][:, :, O_BQ:O_BQ + 768])
            self.compute_hT(ctx, hTq, 0, x_ap, nst, (xt, xn, ss2, tm2, rs2))
            for st in range(nst):
                sc = slice(st * 128, (st + 1) * 128)
                b = st % 2
                if kind == "s":
                    self.dma(rp[b][:], self.c_rope[tok0 + st * 128: tok0 + (st + 1) * 128, :])
                for kc in range(8):
                    self.mm(ps[0][:, :], hTq[:, kc, sc], waq[:, kc, :], start=(kc == 0), stop=(kc == 7))
                self.act(qsq[:], ps[0][:, :], AF.Square)
                self.red(s8[:, 0, :], qsq[:].rearrange("p (h d) -> p h d", h=8))
                self.rstd(s8[:, 1, :], s8[:, 0, :], 64, NORM_EPS, s8[:, 2, :])
                self.tt("dve", qn[:].rearrange("p (g kv d) -> p kv g d", g=4, kv=2),
                        ps[0][:, :].rearrange("p (kv g d) -> p kv g d", kv=2, g=4),
                        s8[:, 1, :].rearrange("p (kv g) -> p kv g", kv=2).unsqueeze(3).broadcast_to([128, 2, 4, 64]), ALU.mult)
                qn3 = qn[:].rearrange("p (h d) -> p h d", h=8)
                self.tt("pool", qn3, qn3, self.qkn_t[:, 0:64].unsqueeze(1).broadcast_to([128, 8, 64]), ALU.mult)
                if kind == "s":
                    self.rope(qn3, rp[b][:, 0:64], rp[b][:, 64:128], 8, 16, rtmp)
                for g in range(4):
                    self.tr(ps[1][:, g * 128:(g + 1) * 128], qn[:, g * 128:(g + 1) * 128])
                self.cp("act", qaT[0:64, 0:4, sc], ps[1][0:64, :].rearrange("q (g t) -> q g t", g=4))
                self.cp("dve", qaT[64:128, 4:8, sc], ps[1][64:128, :].rearrange("q (g t) -> q g t", g=4))
                for half in range(2):
                    for kc in range(8):
                        self.mm(ps[2 + half][:, 0:384], hTq[:, kc, sc], wbq[:, kc, half * 384:(half + 1) * 384],
                                start=(kc == 0), stop=(kc == 7))
                self.cp("act", qb[:, 0:384], ps[2][:, 0:384])
                self.cp("dve", qb[:, 384:768], ps[3][:, 0:384])
                qb3 = qb[:].rearrange("p (h c) -> p h c", h=8)
                if kind == "s":
                    self.rope(qb3[:, :, 64:96], rp[b][:, 128:160], rp[b][:, 160:192], 8, 8, rtmp)
                for h in range(8):
                    self.tr(ps[4 + h // 4][0:96, (h % 4) * 128:(h % 4 + 1) * 128], qb[:, h * 96:(h + 1) * 96])
                self.cp("act", qbT[0:96, 0:4, sc], ps[4][0:96, :].rearrange("q (g t) -> q g t", g=4))
                self.cp("dve", qbT[0:96, 4:8, sc], ps[5][0:96, :].rearrange("q (g t) -> q g t", g=4))
                self.dma(yct[b][:], self.yc_sc[tok0 + st * 128: tok0 + (st + 1) * 128, :])
                for p in range(4):
                    self.tr(ps[6][:, p * 128:(p + 1) * 128], yct[b][:, p * 128:(p + 1) * 128])
                self.cp("dve", ycT[:, :, sc], ps[6][:, :].rearrange("q (g t) -> q g t", g=4))
            S.barrier()
            lo.release(mlo); hi.release(mhi)
            PT = [qa(f"PT{i}", [128, TQ], BF16) for i in range(3)]
            den = qa("den", [65, TQ]); rden = qa("rden", [64, TQ]); szq = qa("szq", [64, TQ]); gq = qa("gq", [64, TQ])
            wz = [qa(f"wz{i}", [128, 8, 64], BF16) for i in range(2)]

            def finalize(gh, po, zoff, yT, h):
                self.cp("dve", den[64:65, :], po[64:65, 0:TQ])
                self.mm(ps[6][0:64, 0:TQ], self.ones[64:65, 0:64], den[64:65, :])
                for kc in range(8):
                    self.mm(ps[7][0:64, 0:TQ], wz[gh % 2][:, kc, :], hTq[:, kc, :], start=(kc == 0), stop=(kc == 7))
                self.act(szq[:, :], ps[7][0:64, 0:TQ], AF.Exp, scale=-1.0)
                self.stt(rden[:, :], szq[:, :], 1.0, ps[6][0:64, 0:TQ], ALU.add, ALU.mult)
                self.recip(rden[:, :], rden[:, :])
                self.tt("dve", gq[:, :], ps[7][0:64, 0:TQ], rden[:, :], ALU.mult)
                self.tt("dve", yT[:, h, :], po[0:64, 0:TQ], gq[:, :], ALU.mult)

            bscale = float(96 ** -0.5)
            steps = [("A", h, kt) for h in range(8) for kt in range(nkt)] + [("B", h, kt) for h in range(8) for kt in range(nkt)]
            N = len(steps)

            def build_kb(h):
                nkb = 0
                for kb in range(0, nk, 512):
                    w = min(512, nk - kb)
                    self.mm(ps[5][0:64, 0:w], self.uk[:, h * 64:(h + 1) * 64], ckvT[:, kb:kb + w])
                    self.cp("dve", KbT[0:64, h % 2, kb:kb + w], ps[5][0:64, 0:w])
                    nkb += 1

            def qk(i):
                mx, h, kt = steps[i]
                pS = ps[i % 3]
                gh = h if mx == "A" else 8 + h
                if kt == 0:
                    zo = O_AZ if mx == "A" else O_BZ
                    self.dma(wz[gh % 2][:], self.wsc[l][:, :, zo + h * 64:zo + (h + 1) * 64])
                    if mx == "A" and h == 7:
                        build_kb(0)
                    elif mx == "B" and h < 7:
                        build_kb(h + 1)
                if mx == "A":
                    self.mm(pS[:, 0:TQ], KaT[:, kt * 128:(kt + 1) * 128], qaT[:, h, :])
                else:
                    self.mm(pS[:, 0:TQ], KbT[:, h % 2, kt * 128:(kt + 1) * 128], qbT[:, h, :])

            def ex(i):
                mx = steps[i][0]
                self.act(PT[i % 3][:, :], ps[i % 3][:, 0:TQ], AF.Exp, scale=(0.125 if mx == "A" else bscale))

            def pv(i):
                mx, h, kt = steps[i]
                gh = h if mx == "A" else 8 + h
                po = ps[3 + gh % 2]
                if mx == "A":
                    o0 = kt * 130 + (h // 4) * 65
                    vv = Vaf[:, o0:o0 + 128]
                else:
                    o0 = kt * 520 + h * 65
                    vv = Vbf[:, o0:o0 + 128]
                self.mm(po[:, 0:TQ], vv, PT[i % 3][:, :], start=(kt == 0), stop=(kt == nkt - 1))
                if kt == nkt - 1:
                    finalize(gh, po, None, yaT if mx == "A" else ybT, h)

            qk(0)
            if N > 1:
                qk(1)
            for i in range(N):
                ex(i)
                if i + 2 < N:
                    qk(i + 2)
                pv(i)
            S.barrier()
            lo.release(mlo_c); hi.release(mhi_c)
            woa = [qa(f"woa{i}", [64, 8, 128], BF16) for i in range(2)]
            wob = [qa(f"wob{i}", [64, 8, 128], BF16) for i in range(2)]
            woc = [qa(f"woc{i}", [128, 4, 128], BF16) for i in range(2)]
            wg = [qa(f"wg{i}", [128, 8, 3, 128], BF16) for i in range(2)]
            wo_ = [qa(f"wout{i}", [128, 8, 128], BF16) for i in range(2)]
            sgm = [qa(f"sgm{i}", [128, TQ]) for i in range(2)]
            mix = qa("mix", [128, TQ]); mtmp = qa("mtmp", [128, TQ])
            mixT = qa("mixT", [128, 8, TQ], BF16)
            oT = [qa(f"oT{i}", [128, TQ]) for i in range(2)]
            xres = qa("xres", [128, nst, D])
            for st in range(nst):
                self.dma(xres[:, st, :], x_ap[st * 128:(st + 1) * 128, :])
            for dc in range(8):
                b = dc % 2
                dcs = slice(dc * 128, (dc + 1) * 128)
                self.dma(woa[b][:], self.wosc[l][0][:, :, dcs])
                self.dma(wob[b][:], self.wosc[l][1][:, :, dcs])
                self.dma(woc[b][:], self.wosc[l][2][:, :, dcs])
                for br in range(3):
                    self.dma(wg[b][:, :, br, :], self.wsc[l][:, :, O_G + br * 1024 + dc * 128:O_G + br * 1024 + (dc + 1) * 128])
                for br in range(3):
                    pp, pg = ps[br], ps[3 + br]
                    if br < 2:
                        wo, yT = (woa, wob)[br][b], (yaT, ybT)[br]
                        for h in range(8):
                            self.mm(pp[:, 0:TQ], wo[:, h, :], yT[:, h, :], start=(h == 0), stop=(h == 7))
                    else:
                        for p in range(4):
                            self.mm(pp[:, 0:TQ], woc[b][:, p, :], ycT[:, p, :], start=(p == 0), stop=(p == 3))
                    for kc in range(8):
                        self.mm(pg[:, 0:TQ], wg[b][:, kc, br, :], hTq[:, kc, :], start=(kc == 0), stop=(kc == 7))
                    sg_ = sgm[br % 2]
                    self.act(sg_[:, :], pg[:, 0:TQ], AF.Sigmoid)
                    if br == 0:
                        self.tt("dve", mix[:, :], pp[:, 0:TQ], sg_[:, :], ALU.mult)
                    else:
                        self.tt("dve", mtmp[:, :], pp[:, 0:TQ], sg_[:, :], ALU.mult)
                        dst = mix[:, :] if br == 1 else mixT[:, dc, :]
                        self.tt("pool", dst, mix[:, :], mtmp[:, :], ALU.add)
            for dc in range(8):
                b = dc % 2
                dcs = slice(dc * 128, (dc + 1) * 128)
                self.dma(wo_[b][:], self.woutsc[l][:, :, dcs])
                po = ps[6 + b]
                for kc in range(8):
                    self.mm(po[:, 0:TQ], wo_[b][:, kc, :], mixT[:, kc, :], start=(kc == 0), stop=(kc == 7))
                gsc = self.modG[:, l, cond, dc:dc + 1]
                if b == 0:
                    self.act(oT[b][:, :], po[:, 0:TQ], AF.Identity, scale=gsc)
                else:
                    self.ts("dve", oT[b][:, :], po[:, 0:TQ], gsc, None, ALU.mult)
                pt = ps[dc % 4]
                for st in range(nst):
                    self.tr(pt[:, st * 128:(st + 1) * 128], oT[b][:, st * 128:(st + 1) * 128])
                self.tt("dve", xres[:, :, dcs], xres[:, :, dcs],
                        pt[:, 0:nst * 128].rearrange("q (s f) -> q s f", s=nst), ALU.add)
            for st in range(nst):
                rows = slice(tok0 + st * 128, tok0 + (st + 1) * 128)
                if not last:
                    self.dma(xout[rows, :], xres[:, st, :])
                else:
                    fs = qa(f"fss{st}", [128, 4])
                    jv = mixT[:, 0:1024 // TQ, :].rearrange("q a t -> q (a t)")
                    self.act(jv, xres[:, st, :], AF.Square, accum=fs[:, 0:1])
                    self.rstd(fs[:, 2:3], fs[:, 0:1], D, NORM_EPS, fs[:, 1:2])
                    self.ts("dve", xres[:, st, :], xres[:, st, :], fs[:, 2:3], None, ALU.mult)
                    self.tt("pool", xres[:, st, :], xres[:, st, :], self.fnw_t[:], ALU.mult)
                    self.dma(yout[rows, :], xres[:, st, :], wk=[("yout", kind, si, tok0, st)])
            S.barrier()
            lo.release(mlo_c); hi.release(mhi_c)


def _in_maps(inputs):
    c = host_constants()
    f = lambda a: np.ascontiguousarray(np.asarray(a, dtype=np.float32))
    g = {k: f(v) for k, v in inputs.items()}
    nb = np.zeros((128, 128), np.float32)
    for l in range(DEPTH):
        nb[l * 32:l * 32 + 8] = g["norm_w"][l].reshape(8, 128)
        nb[l * 32 + 8:l * 32 + 32] = g["b_mod"][l].reshape(24, 128)
    lfm = np.zeros((DEPTH, 48, 128), np.float32)
    for l in range(DEPTH):
        lfm[l, 0:4] = g["c_k_k"][l].reshape(4, 128)
        lfm[l, 4:8] = g["c_k_a"][l].reshape(4, 128)
        lfm[l, 8:12] = g["c_r_k"][l].reshape(4, 128)
        lfm[l, 12:26] = g["c_mu_prev"][l].reshape(14, 128)
        lfm[l, 26:40] = g["c_mu_next"][l].reshape(14, 128)
        lfm[l, 40:48] = g["c_a0"][l].reshape(8, 128)
    qkn = np.concatenate([g["a_qnorm_w"], g["a_knorm_w"], g["b_kvnorm_w"]], axis=1)
    lnx = np.concatenate([g["c_lnx_w"], g["c_lnx_b"]], axis=1)
    shared = {
        "nb_rows": nb, "w_mod": g["w_mod"], "w_in": g["w_in"], "qkn": f(qkn),
        "b_w_uk": g["b_w_uk"], "b_w_uv": g["b_w_uv"], "lfm_rows": lfm, "c_w0": g["c_w0"],
        "c_w_up": f(g["c_w_up"].reshape(DEPTH, 128, 512)), "c_a_up": f(g["c_a_up"].reshape(DEPTH, 128, 512)),
        "lnx": f(lnx), "w_oa": g["w_oa"], "w_ob": g["w_ob"], "w_oc": g["w_oc"], "w_out": g["w_out"],
        "final_norm_w": g["final_norm_w"],
        "c_ident": c["ident"], "c_bank": c["bankmask"], "c_lvl": c["lvlmask"], "c_tri": c["tri"],
        "c_hsel": c["headsel"], "c_rope": c["rope"],
    }
    maps = []
    for i in range(NCORES):
        m = dict(shared)
        m["x_prompt"] = f(g["x_prompt"][i * NPS:(i + 1) * NPS])
        m["x_sample"] = f(g["x_sample"][i])
        m["cache_a_k"] = f(g["cache_a_k"][i].reshape(DEPTH, PAST, 128))
        m["cache_a_v"] = f(g["cache_a_v"][i].reshape(DEPTH, PAST, 128))
        m["cache_b_ckv"] = f(g["cache_b_ckv"][i])
        m["cache_b_krope"] = f(g["cache_b_krope"][i])
        m["state_c_fwd"] = f(g["state_c_fwd"][i])
        m["state_c_bwd"] = f(g["state_c_bwd"][i])
        m["cond"] = f(np.concatenate([g["c_ctx"].reshape(8, 128), g["c"][i].reshape(8, 128)], axis=0))
        maps.append(m)
    return maps


_NC_CACHE = {}


def kernel(**inputs):
    stage = int(os.environ.get("MK_STAGE", "99"))
    if stage not in _NC_CACHE:
        _NC_CACHE[stage] = Builder(stage).build()
    nc = _NC_CACHE[stage]
    maps = _in_maps(inputs)
    res = run_bass_kernel_spmd(nc, maps, core_ids=list(range(NCORES)))
    r = res.results
    cat = lambda name: np.concatenate([np.asarray(x[name]) for x in r], axis=0)
    y_prompt = cat("y_prompt")
    y_sample = np.stack([np.asarray(x["y_sample"]) for x in r], axis=0)
    new_a_k = cat("new_a_k").reshape(32, DEPTH, TP, 2, 64)
    new_a_v = cat("new_a_v").reshape(32, DEPTH, TP, 2, 64)
    new_ckv = cat("new_b_ckv")
    new_kr = cat("new_b_krope")
    new_sf = cat("new_c_state_fwd")
    new_sb = cat("new_c_state_bwd")
    return (y_prompt, y_sample, new_a_k, new_a_v, new_ckv, new_kr, new_sf, new_sb)
```
